# Optimizing a Trainium2 kernel written in Bass

```python
import math
import jax
import jax.numpy as jnp
from jax import lax
import numpy as np

D_MODEL = 1024
BATCH = 16
SEQ = 4096
DEPTH = 4

CTX_LEN = 256
GRID_W = 64
D_MIX = D_MODEL
GROUP_W = D_MIX // 4
H_A = 4
HKV_A = 2
G_A = H_A // HKV_A
DH_A = GROUP_W // H_A
WINDOW = 128
BLK = 128
HY_W = GROUP_W
HY_ORDER = 2
HY_SHORT = 3
HY_BANDS = 16
HY_EMB = 1 + 2 * HY_BANDS
HY_HID = 64
HY_TARGET = 1e-2
HY_FAST = 0.3
HY_SLOW = 1.5
H_C = 4
DH_C = GROUP_W // H_C
CHUNK_C = 64
H_D = 4
DH_D = GROUP_W // (2 * H_D)
N_EXPERTS = 16
EC_CAPACITY = 2
D_EXPERT = D_MODEL
ROPE_BASE = 10000.0
EPS = 1e-6
NEG = -1e30
IN_SPLITS = (H_A * DH_A, HKV_A * DH_A, HKV_A * DH_A,
             3 * HY_W,
             H_C * DH_C, H_C * DH_C, H_C * DH_C, H_C * DH_C,
             4 * H_C,
             2 * H_D * DH_D, 2 * H_D * DH_D, 2 * H_D * DH_D)
IN_WIDTH = sum(IN_SPLITS)

kernel_name = 'hybrid_diffusion_trunk'


def _rms(x, g):
    xf = x.astype(jnp.float32)
    y = xf * lax.rsqrt(jnp.mean(xf * xf, axis=-1, keepdims=True) + EPS)
    return y.astype(x.dtype) * g.astype(x.dtype)


def _split_cols(p):
    return jnp.split(p, np.cumsum(IN_SPLITS)[:-1].tolist(), axis=-1)


def _axial_tables(T, dh):
    rows = T // GRID_W
    r = jnp.broadcast_to(jnp.arange(rows, dtype=jnp.float32)[:, None], (rows, GRID_W)).reshape(T)
    col = jnp.broadcast_to(jnp.arange(GRID_W, dtype=jnp.float32)[None, :], (rows, GRID_W)).reshape(T)
    nf = dh // 4
    inv = ROPE_BASE ** (-jnp.arange(nf, dtype=jnp.float32) / nf)
    ang = jnp.stack([r[:, None] * inv, col[:, None] * inv], axis=1)
    return jnp.cos(ang), jnp.sin(ang)


def _rope2d(x, cos, sin):
    nf = x.shape[-1] // 4
    xr = x.reshape(*x.shape[:-1], 2, 2, nf)
    bshape = (x.shape[1],) + (1,) * (x.ndim - 3) + (2, nf)
    c = cos.reshape(bshape).astype(x.dtype)
    s = sin.reshape(bshape).astype(x.dtype)
    a, b = xr[..., 0, :], xr[..., 1, :]
    return jnp.stack([a * c - b * s, b * c + a * s], axis=-2).reshape(x.shape)


def _sink_softmax(s, sink):
    m = jnp.maximum(jnp.max(s, axis=-1, keepdims=True), sink)
    p = jnp.exp(s - m)
    return p / (jnp.sum(p, axis=-1, keepdims=True) + jnp.exp(sink - m))


def _window_gqa(qc, kc, vc, ql, kl, vl, q_gain, k_gain, sink, cos, sin, ctx_out):
    B, T = ql.shape[:2]
    S = kc.shape[1]
    nb = T // BLK
    scale = DH_A ** -0.5
    sink_hg = sink.astype(jnp.float32).reshape(HKV_A, G_A, 1, 1)
    kc = _rms(kc.reshape(B, S, HKV_A, DH_A), k_gain)
    vc = vc.reshape(B, S, HKV_A, DH_A)
    ql = _rope2d(_rms(ql.reshape(B, T, H_A, DH_A), q_gain), cos, sin)
    kl = _rope2d(_rms(kl.reshape(B, T, HKV_A, DH_A), k_gain), cos, sin)
    qb = ql.reshape(B, nb, BLK, HKV_A, G_A, DH_A)

    def windows(t):
        tp = jnp.pad(t, ((0, 0), (BLK, BLK), (0, 0), (0, 0))).reshape(B, nb + 2, BLK, HKV_A, DH_A)
        return jnp.concatenate([tp[:, :-2], tp[:, 1:-1], tp[:, 2:]], axis=2)

    kw = windows(kl)
    vw = windows(vl.reshape(B, T, HKV_A, DH_A))
    kpos = jnp.arange(-BLK, T + BLK).reshape(nb + 2, BLK)
    kpos = jnp.concatenate([kpos[:-2], kpos[1:-1], kpos[2:]], axis=1)
    qpos = jnp.arange(T).reshape(nb, BLK)
    valid = ((jnp.abs(qpos[:, :, None] - kpos[:, None, :]) <= WINDOW)
             & (kpos >= 0)[:, None, :] & (kpos < T)[:, None, :])
    s_loc = jnp.einsum('bnqhgd,bnkhd->bnhgqk', qb, kw).astype(jnp.float32) * scale
    s_loc = jnp.where(valid[None, :, None, None], s_loc, NEG)
    s_ctx = jnp.einsum('bnqhgd,bkhd->bnhgqk', qb, kc).astype(jnp.float32) * scale
    p = _sink_softmax(jnp.concatenate([s_loc, s_ctx], axis=-1), sink_hg).astype(vl.dtype)
    nk = 3 * BLK
    ol = (jnp.einsum('bnhgqk,bnkhd->bnqhgd', p[..., :nk], vw)
          + jnp.einsum('bnhgqk,bkhd->bnqhgd', p[..., nk:], vc)).reshape(B, T, H_A * DH_A)
    oc = None
    if ctx_out:
        qcg = _rms(qc.reshape(B, S, H_A, DH_A), q_gain).reshape(B, S, HKV_A, G_A, DH_A)
        sc = jnp.einsum('bqhgd,bkhd->bhgqk', qcg, kc).astype(jnp.float32) * scale
        pc = _sink_softmax(sc, sink_hg).astype(vc.dtype)
        oc = jnp.einsum('bhgqk,bkhd->bqhgd', pc, vc).reshape(B, S, H_A * DH_A)
    return oc, ol


def _hyena_spectra(L, fw1, fb1, freq, fw2, fb2, fw3):
    f32 = jnp.float32
    t = jnp.linspace(0.0, 1.0, L, dtype=f32)[:, None]
    w = 2.0 * math.pi * jnp.arange(L, dtype=f32)[:, None] / L
    bands = jnp.linspace(1e-4, HY_BANDS - 1, HY_BANDS, dtype=f32)
    z = jnp.concatenate([t, jnp.cos(bands * w), -jnp.sin(bands * w)], axis=-1)
    fr = freq.astype(f32)
    h = jnp.sin(fr * (z @ fw1.astype(f32) + fb1.astype(f32)))
    h = jnp.sin(fr * (h @ fw2.astype(f32) + fb2.astype(f32)))
    h = (h @ fw3.astype(f32)).reshape(L, 2, HY_ORDER, HY_W)
    deltas = jnp.abs(jnp.linspace(math.log(HY_TARGET) / HY_SLOW, math.log(HY_TARGET) / HY_FAST, HY_W, dtype=f32))
    h = h * jnp.exp(-t * deltas)[:, None, None, :]
    two_sided = jnp.concatenate([h[:, 0], jnp.zeros((1, HY_ORDER, HY_W), f32), jnp.flip(h[1:, 1], axis=0)], axis=0)
    two_sided = two_sided * lax.rsqrt(jnp.sum(two_sided * two_sided, axis=0, keepdims=True) + EPS)
    return jnp.fft.rfft(two_sided, axis=0)


def _fftconv(z, spec, bias):
    L = z.shape[1]
    zf = z.astype(jnp.float32)
    y = jnp.fft.irfft(jnp.fft.rfft(zf, n=2 * L, axis=1) * spec, n=2 * L, axis=1)[:, :L]
    return (y + zf * bias.astype(jnp.float32)).astype(z.dtype)


def _hyena_mixer(uc, ul, conv_w, fw1, fb1, freq, fw2, fb2, fw3, bias, ctx_out):
    def run(u):
        L = u.shape[1]
        up = jnp.pad(u, ((0, 0), (1, 1), (0, 0)))
        u = conv_w[0] * up[:, 0:L] + conv_w[1] * up[:, 1:L + 1] + conv_w[2] * up[:, 2:L + 2]
        v, x1, x2 = jnp.split(u, 3, axis=-1)
        spec = _hyena_spectra(L, fw1, fb1, freq, fw2, fb2, fw3)
        z = v
        for n, gate in enumerate((x1, x2)):
            z = gate * _fftconv(z, spec[:, n], bias[n])
        return z
    return (run(uc) if ctx_out else None), run(ul)


def _mlstm_scan(q, k, v, ig, lf, state, emit):
    B, H, T, d = q.shape
    nc = T // CHUNK_C

    def chunks(t):
        return jnp.moveaxis(t.reshape(B, H, nc, CHUNK_C, *t.shape[3:]), 2, 0)

    tril = jnp.tril(jnp.ones((CHUNK_C, CHUNK_C), dtype=bool))

    def step(carry, xs):
        C, n, m = carry
        qc, kc, vc, ic, fc = xs
        b = jnp.cumsum(fc, axis=-1)
        g_end = b[..., -1:] - b + ic
        m_new = jnp.maximum(b[..., -1] + m, jnp.max(g_end, axis=-1))
        w_prev = jnp.exp(b[..., -1] + m - m_new)
        w_tok = jnp.exp(g_end - m_new[..., None])
        C_new = w_prev[..., None, None] * C + jnp.einsum('bhs,bhse,bhsk->bhek', w_tok, vc, kc)
        n_new = w_prev[..., None] * n + jnp.einsum('bhs,bhsk->bhk', w_tok, kc)
        if not emit:
            return (C_new, n_new, m_new), None
        logw = jnp.where(tril, b[..., :, None] - b[..., None, :] + ic[..., None, :], NEG)
        inter = b + m[..., None]
        m_t = jnp.maximum(inter, jnp.max(logw, axis=-1))
        w_in = jnp.exp(inter - m_t)
        s = jnp.einsum('bhtk,bhsk->bhts', qc, kc) * jnp.exp(logw - m_t[..., None])
        num = w_in[..., None] * jnp.einsum('bhek,bhtk->bhte', C, qc) + jnp.einsum('bhts,bhse->bhte', s, vc)
        den = w_in * jnp.einsum('bhk,bhtk->bht', n, qc) + jnp.sum(s, axis=-1)
        h = num / jnp.maximum(jnp.abs(den), jnp.exp(-m_t))[..., None]
        return (C_new, n_new, m_new), h

    carry, hs = lax.scan(step, state, tuple(chunks(t) for t in (q, k, v, ig, lf)))
    if not emit:
        return None, carry
    return jnp.moveaxis(hs, 0, 2).reshape(B, H, T, d), carry


def _mlstm_mixer(pc, pl, b_gate, norm_gain, ctx_out):
    f32 = jnp.float32

    def prep(q, k, v, g):
        B, L = q.shape[:2]

        def heads(t):
            return jnp.swapaxes(t.reshape(B, L, H_C, DH_C), 1, 2).astype(f32)

        gates = jnp.moveaxis(g.reshape(B, L, 4, H_C).astype(f32) + b_gate.astype(f32), 1, -1)
        return heads(q), heads(k) * DH_C ** -0.5, heads(v), gates

    qc, kc, vc, oc_pre, gc = pc
    ql, kl, vl, ol_pre, gl = pl
    Qc, Kc, Vc, Gc = prep(qc, kc, vc, gc)
    Ql, Kl, Vl, Gl = prep(ql, kl, vl, gl)
    B = ql.shape[0]
    s0 = (jnp.zeros((B, H_C, DH_C, DH_C), f32), jnp.zeros((B, H_C, DH_C), f32), jnp.zeros((B, H_C), f32))
    lsig = jax.nn.log_sigmoid

    def flip(t):
        return jnp.flip(t, axis=2)

    hcf, sf = _mlstm_scan(Qc, Kc, Vc, Gc[:, 0], lsig(Gc[:, 2]), s0, ctx_out)
    hlf, _ = _mlstm_scan(Ql, Kl, Vl, Gl[:, 0], lsig(Gl[:, 2]), sf, True)
    hcb, sb = _mlstm_scan(flip(Qc), flip(Kc), flip(Vc), flip(Gc[:, 1]), flip(lsig(Gc[:, 3])), s0, ctx_out)
    hlb, _ = _mlstm_scan(flip(Ql), flip(Kl), flip(Vl), flip(Gl[:, 1]), flip(lsig(Gl[:, 3])), sb, True)

    def out(h, o_pre):
        B_, L = o_pre.shape[:2]
        h = jnp.swapaxes(h, 1, 2)
        o = jax.nn.sigmoid(o_pre.reshape(B_, L, H_C, DH_C).astype(f32))
        return (o * _rms(h, norm_gain)).astype(o_pre.dtype).reshape(B_, L, H_C * DH_C)

    yl = out(hlf + flip(hlb), ol_pre)
    yc = out(hcf + flip(hcb), oc_pre) if ctx_out else None
    return yc, yl


def _diff_attn(qc, kc, vc, ql, kl, vl, q_gain, k_gain, lq1, lk1, lq2, lk2, sub_gain, lam_init, cos, sin, ctx_out):
    B, T = ql.shape[:2]
    S = kc.shape[1]
    nb = T // BLK
    scale = DH_D ** -0.5
    f32 = jnp.float32
    lam = (jnp.exp(jnp.sum(lq1.astype(f32) * lk1.astype(f32)))
           - jnp.exp(jnp.sum(lq2.astype(f32) * lk2.astype(f32))) + lam_init)
    kc = _rms(kc.reshape(B, S, H_D, 2, DH_D), k_gain)
    vc = vc.reshape(B, S, H_D, 2 * DH_D)
    ql = _rope2d(_rms(ql.reshape(B, T, H_D, 2, DH_D), q_gain), cos, sin)
    kl = _rope2d(_rms(kl.reshape(B, T, H_D, 2, DH_D), k_gain), cos, sin)
    k_all = jnp.concatenate([kl, kc], axis=1)
    v_all = jnp.concatenate([vl.reshape(B, T, H_D, 2 * DH_D), vc], axis=1)

    def attend(q, k, v):
        s = jnp.einsum('bqhcd,bkhcd->bhcqk', q, k).astype(f32) * scale
        p = jax.nn.softmax(s, axis=-1)
        a = (p[:, :, 0] - lam * p[:, :, 1]).astype(v.dtype)
        return jnp.einsum('bhqk,bkhe->bqhe', a, v)

    def post(o):
        return (_rms(o, sub_gain) * (1.0 - lam_init)).reshape(*o.shape[:2], H_D * 2 * DH_D)

    qb = jnp.moveaxis(ql.reshape(B, nb, BLK, H_D, 2, DH_D), 1, 0)
    ol = lax.map(lambda q: attend(q, k_all, v_all), qb)
    ol = post(jnp.moveaxis(ol, 0, 1).reshape(B, T, H_D, 2 * DH_D))
    oc = post(attend(_rms(qc.reshape(B, S, H_D, 2, DH_D), q_gain), kc, vc)) if ctx_out else None
    return oc, ol


def _expert_choice_ffn(h, w_router, w_g, w_u, w_d):
    B, T, D = h.shape
    cap = (EC_CAPACITY * T) // N_EXPERTS
    aff = jax.nn.softmax((h @ w_router).astype(jnp.float32), axis=-1)
    gate, idx = lax.top_k(jnp.swapaxes(aff, 1, 2), cap)
    bidx = jnp.arange(B)[:, None, None]
    xe = h[bidx, idx]
    a = jnp.einsum('becd,edf->becf', xe, w_g)
    u = jnp.einsum('becd,edf->becf', xe, w_u)
    y = jnp.einsum('becf,efd->becd', jax.nn.silu(a) * u, w_d) * gate[..., None].astype(h.dtype)
    return jnp.zeros_like(h).at[bidx, idx].add(y)


def setup_inputs(seed: int = 0) -> dict:
    key = jax.random.key(seed)
    ks = iter(list(jax.random.split(key, 48)))
    f32 = jnp.float32

    def nrm(shape, s):
        return jax.random.normal(next(ks), shape, f32) * s

    def gain(shape):
        return 1.0 + nrm(shape, 0.05)

    L, D = DEPTH, D_MODEL
    b_gate = jnp.concatenate([nrm((L, 2, H_C), 0.1),
                              jnp.linspace(3.0, 6.0, H_C, dtype=f32) + nrm((L, 2, H_C), 0.1)], axis=1)
    return {
        'x': nrm((BATCH, SEQ, D), 1.0),
        'c': nrm((BATCH, D), 1.0),
        'ctx': nrm((BATCH, CTX_LEN, D), 1.0),
        'c_ctx': nrm((D,), 1.0),
        'w_ada': nrm((L, D, 6 * D), 0.5 * D ** -0.5),
        'b_ada': nrm((L, 6 * D), 0.02),
        'norm1_g': gain((L, D)),
        'norm2_g': gain((L, D)),
        'w_in': nrm((L, D, IN_WIDTH), D ** -0.5),
        'b_gate': b_gate,
        'a_qnorm': gain((L, DH_A)),
        'a_knorm': gain((L, DH_A)),
        'a_sink': nrm((L, H_A), 0.5),
        'hy_conv': nrm((L, HY_SHORT, 3 * HY_W), HY_SHORT ** -0.5),
        'hy_fw1': nrm((L, HY_EMB, HY_HID), HY_EMB ** -0.5),
        'hy_fb1': nrm((L, HY_HID), 0.02),
        'hy_freq': gain((L, HY_HID)),
        'hy_fw2': nrm((L, HY_HID, HY_HID), HY_HID ** -0.5),
        'hy_fb2': nrm((L, HY_HID), 0.02),
        'hy_fw3': nrm((L, HY_HID, 2 * HY_ORDER * HY_W), HY_HID ** -0.5),
        'hy_bias': nrm((L, HY_ORDER, HY_W), 0.5),
        'ml_norm': gain((L, DH_C)),
        'd_qnorm': gain((L, DH_D)),
        'd_knorm': gain((L, DH_D)),
        'd_lq1': nrm((L, DH_D), 0.1),
        'd_lk1': nrm((L, DH_D), 0.1),
        'd_lq2': nrm((L, DH_D), 0.1),
        'd_lk2': nrm((L, DH_D), 0.1),
        'd_subnorm': gain((L, 2 * DH_D)),
        'w_out': nrm((L, D_MIX, D), D_MIX ** -0.5),
        'w_router': nrm((L, D, N_EXPERTS), D ** -0.5),
        'w_e_gate': nrm((L, N_EXPERTS, D, D_EXPERT), D ** -0.5),
        'w_e_up': nrm((L, N_EXPERTS, D, D_EXPERT), D ** -0.5),
        'w_e_down': nrm((L, N_EXPERTS, D_EXPERT, D), D_EXPERT ** -0.5),
    }


def reference(x, c, ctx, c_ctx, w_ada, b_ada, norm1_g, norm2_g, w_in, b_gate,
              a_qnorm, a_knorm, a_sink,
              hy_conv, hy_fw1, hy_fb1, hy_freq, hy_fw2, hy_fb2, hy_fw3, hy_bias,
              ml_norm, d_qnorm, d_knorm, d_lq1, d_lk1, d_lq2, d_lk2, d_subnorm,
              w_out, w_router, w_e_gate, w_e_up, w_e_down):
    T = x.shape[1]
    cos_a, sin_a = _axial_tables(T, DH_A)
    cos_d, sin_d = _axial_tables(T, DH_D)
    silu_c = jax.nn.silu(c)
    silu_cc = jax.nn.silu(c_ctx)
    for l in range(DEPTH):
        last = l == DEPTH - 1
        ctx_out = not last
        lam_init = 0.8 - 0.6 * math.exp(-0.3 * l)
        mod_l = (silu_c @ w_ada[l] + b_ada[l])[:, None, :]
        mod_c = (silu_cc @ w_ada[l] + b_ada[l])[None, None, :]
        sh1l, sc1l, gt1l, sh2l, sc2l, gt2l = jnp.split(mod_l, 6, axis=-1)
        sh1c, sc1c, gt1c, sh2c, sc2c, gt2c = jnp.split(mod_c, 6, axis=-1)

        hl = _rms(x, norm1_g[l]) * (1.0 + sc1l) + sh1l
        hc = _rms(ctx, norm1_g[l]) * (1.0 + sc1c) + sh1c
        pl = _split_cols(hl @ w_in[l])
        pc = _split_cols(hc @ w_in[l])
        ya_c, ya_l = _window_gqa(pc[0], pc[1], pc[2], pl[0], pl[1], pl[2],
                                 a_qnorm[l], a_knorm[l], a_sink[l], cos_a, sin_a, ctx_out)
        yb_c, yb_l = _hyena_mixer(pc[3], pl[3], hy_conv[l], hy_fw1[l], hy_fb1[l], hy_freq[l],
                                  hy_fw2[l], hy_fb2[l], hy_fw3[l], hy_bias[l], ctx_out)
        yc_c, yc_l = _mlstm_mixer(pc[4:9], pl[4:9], b_gate[l], ml_norm[l], ctx_out)
        yd_c, yd_l = _diff_attn(pc[9], pc[10], pc[11], pl[9], pl[10], pl[11],
                                d_qnorm[l], d_knorm[l], d_lq1[l], d_lk1[l], d_lq2[l], d_lk2[l],
                                d_subnorm[l], lam_init, cos_d, sin_d, ctx_out)
        x = x + gt1l * (jnp.concatenate([ya_l, yb_l, yc_l, yd_l], axis=-1) @ w_out[l])

        x = x + gt2l * _expert_choice_ffn(_rms(x, norm2_g[l]) * (1.0 + sc2l) + sh2l,
                                          w_router[l], w_e_gate[l], w_e_up[l], w_e_down[l])
        if ctx_out:
            ctx = ctx + gt1c * (jnp.concatenate([ya_c, yb_c, yc_c, yd_c], axis=-1) @ w_out[l])
            ctx = ctx + gt2c * _expert_choice_ffn(_rms(ctx, norm2_g[l]) * (1.0 + sc2c) + sh2c,
                                                  w_router[l], w_e_gate[l], w_e_up[l], w_e_down[l])
    return x
```

```python
import contextlib
import math
import numpy as np
import ml_dtypes
import concourse.bass as bass
import concourse.mybir as mybir
from concourse.bass_utils import run_bass_kernel_spmd

F32 = mybir.dt.float32
BF16 = mybir.dt.bfloat16
I32 = mybir.dt.int32
U32 = mybir.dt.uint32
AF = mybir.ActivationFunctionType
ALU = mybir.AluOpType
AX = mybir.AxisListType

ENGS = ("pe", "dve", "act", "pool", "sp")

D = 1024
NB = 2
T = 4096
S = 256
DEPTH = 4
INW = 3088
EPS = 1e-6
NE = 16
A_Q, A_K, A_V = 0, 256, 384
B_U = 512
C_Q, C_K, C_V, C_O, C_G = 1280, 1536, 1792, 2048, 2304
D_Q, D_K, D_V = 2320, 2576, 2832


class Buf:
    __slots__ = ("w", "r")

    def __init__(self):
        self.w = None
        self.r = []


class TT:
    def __init__(self, t, nbuf=1):
        self.t = t
        self.b = Buf()

    def __getitem__(self, k):
        return self.t[k]


class Prog:
    NDMA = 8
    EPOCH = 30000

    def __init__(self, nc, stack):
        self.nc = nc
        self.stack = stack
        self.q = {e: [] for e in ENGS}
        self.cnt = {e: 0 for e in ENGS}
        self.sem = {e: stack.enter_context(nc.semaphore("c_" + e)) for e in ENGS}
        self.nsem = 0
        self.seen = {e: {} for e in ENGS}
        self.dsem = {}
        self.dcnt = {}
        self.dnext = {}
        for e in ("sp", "pool", "act"):
            self.dsem[e] = [stack.enter_context(nc.semaphore("d_%s%d" % (e, i))) for i in range(self.NDMA)]
            self.dcnt[e] = [0] * self.NDMA
            self.dnext[e] = 0
        self.n_inst = 0
        self.pe_sems = {id(self.sem["pe"])}
        self.pending = False

    def _need(self, eng, toks):
        seen = self.seen[eng]
        own = self.sem[eng]
        for tok in toks:
            if tok is None:
                continue
            sem, val = tok
            if eng == "pe" and id(sem) in self.pe_sems:
                continue
            k = id(sem)
            if seen.get(k, 0) >= val:
                continue
            seen[k] = val
            self.q[eng].append(("wait", sem, val))

    @staticmethod
    def _deps(reads, writes):
        toks = []
        for b in reads:
            toks.append(b.w)
        for b in writes:
            toks.append(b.w)
            toks.extend(b.r)
        return toks

    @staticmethod
    def _mark(tok, reads, writes):
        for b in reads:
            if len(b.r) > 24:
                b.r = b.r[-24:]
            b.r.append(tok)
        for b in writes:
            b.w = tok
            b.r = []

    def _bufs(self, xs):
        return [x.b if isinstance(x, TT) else x for x in xs]

    def op(self, eng, fn, reads=(), writes=(), sig=True):
        reads = self._bufs(reads)
        writes = self._bufs(writes)
        if not sig:
            assert eng == "pe"
            self._need(eng, self._deps(reads, writes))
            tok = (self.sem[eng], self.cnt[eng] + 1)
            self.q[eng].append(("opns", fn))
            self._mark(tok, reads, writes)
            self.n_inst += 1
            self.pending = True
            return tok
        if self.cnt[eng] >= self.EPOCH and not (eng == "pe" and self.pending):
            old = (self.sem[eng], self.cnt[eng])
            self.nsem += 1
            self.sem[eng] = self.stack.enter_context(self.nc.semaphore("c_%s_%d" % (eng, self.nsem)))
            self.cnt[eng] = 0
            if eng == "pe":
                self.pe_sems.add(id(self.sem[eng]))
        self._need(eng, self._deps(reads, writes))
        self.cnt[eng] += 1
        tok = (self.sem[eng], self.cnt[eng])
        self.q[eng].append(("op", fn, self.sem[eng]))
        self._mark(tok, reads, writes)
        self.n_inst += 1
        if eng == "pe":
            self.pending = False
        return tok

    def dma(self, eng, fn, reads=(), writes=()):
        reads = self._bufs(reads)
        writes = self._bufs(writes)
        self._need(eng, self._deps(reads, writes))
        i = self.dnext[eng]
        self.dnext[eng] = (i + 1) % self.NDMA
        sem = self.dsem[eng][i]
        if self.dcnt[eng][i] > 0:
            self._need(eng, [(sem, self.dcnt[eng][i])])
        self.dcnt[eng][i] += 16
        tok = (sem, self.dcnt[eng][i])
        self.q[eng].append(("dma", fn, sem))
        self._mark(tok, reads, writes)
        self.n_inst += 1
        return tok

    def barrier(self):
        toks = [(self.sem[e], self.cnt[e]) for e in ENGS if self.cnt[e] > 0]
        for e in self.dsem:
            for i in range(self.NDMA):
                if self.dcnt[e][i] > 0:
                    toks.append((self.dsem[e][i], self.dcnt[e][i]))
        for e in ENGS:
            self._need(e, toks)

    def replay(self, block):
        q = self.q

        def run(engobj, items):
            for it in items:
                if it[0] == "wait":
                    engobj.wait_ge(it[1], it[2])
                elif it[0] == "opns":
                    it[1](engobj)
                elif it[0] == "op":
                    it[1](engobj).then_inc(it[2], 1)
                else:
                    it[1](engobj).then_inc(it[2], 16)

        block.tensor(lambda e: run(e, q["pe"]))
        block.vector(lambda e: run(e, q["dve"]))
        block.scalar(lambda e: run(e, q["act"]))
        block.gpsimd(lambda e: run(e, q["pool"]))
        block.sync(lambda e: run(e, q["sp"]))


def MM(out, lhsT, rhs, start=True, stop=True):
    return lambda e: e.matmul(out, lhsT=lhsT, rhs=rhs, start=start, stop=stop, skip_group_check=True)


def TR(out, in_, ident):
    return lambda e: e.transpose(out, in_, ident)


def ACT(out, in_, func, bias=None, scale=None, accum_out=None):
    kw = {}
    if bias is not None:
        kw["bias"] = bias
    if scale is not None:
        kw["scale"] = scale
    if accum_out is not None:
        kw["accum_out"] = accum_out
    return lambda e: e.activation(out=out, in_=in_, func=func, **kw)


def TS(out, in0, s1, s2, op0, op1=None, accum_out=None):
    if op1 is None:
        return lambda e: e.tensor_scalar(out=out, in0=in0, scalar1=s1, scalar2=None, op0=op0)
    if accum_out is not None:
        return lambda e: e.tensor_scalar(out=out, in0=in0, scalar1=s1, scalar2=s2, op0=op0, op1=op1, accum_out=accum_out)
    return lambda e: e.tensor_scalar(out=out, in0=in0, scalar1=s1, scalar2=s2, op0=op0, op1=op1)


def TTo(out, in0, in1, op):
    return lambda e: e.tensor_tensor(out=out, in0=in0, in1=in1, op=op)


def STT(out, in0, scalar, in1, op0, op1):
    return lambda e: e.scalar_tensor_tensor(out=out, in0=in0, scalar=scalar, in1=in1, op0=op0, op1=op1)


def CP(out, in_):
    return lambda e: e.tensor_copy(out=out, in_=in_)


def RED(out, in_, op=None, axis=None):
    op = op or ALU.add
    axis = axis or AX.X
    return lambda e: e.tensor_reduce(out=out, in_=in_, axis=axis, op=op)


def RCP(out, in_):
    return lambda e: e.reciprocal(out=out, in_=in_)


def MSET(ap, c):
    return lambda e: e.memset(ap, c)


def DMA(out, in_, **kw):
    return lambda e: e.dma_start(out=out, in_=in_, **kw)


WEIGHT_SPECS = [
    ("w_ada", [DEPTH, D, 6 * D]), ("b_ada", [DEPTH, 6 * D]), ("norm1_g", [DEPTH, D]), ("norm2_g", [DEPTH, D]),
    ("w_in", [DEPTH, D, INW]), ("b_gate", [DEPTH, 16]), ("a_qnorm", [DEPTH, 64]), ("a_knorm", [DEPTH, 64]),
    ("a_sink", [DEPTH, 4]), ("hy_conv", [DEPTH, 3, 768]), ("hy_fw1", [DEPTH, 33, 64]), ("hy_fb1", [DEPTH, 64]),
    ("hy_freq", [DEPTH, 64]), ("hy_fw2", [DEPTH, 64, 64]), ("hy_fb2", [DEPTH, 64]), ("hy_fw3", [DEPTH, 64, 1024]),
    ("hy_bias", [DEPTH, 2, 256]), ("ml_norm", [DEPTH, 64]), ("d_qnorm", [DEPTH, 32]), ("d_knorm", [DEPTH, 32]),
    ("d_lq1", [DEPTH, 32]), ("d_lk1", [DEPTH, 32]), ("d_lq2", [DEPTH, 32]), ("d_lk2", [DEPTH, 32]),
    ("d_subnorm", [DEPTH, 64]), ("w_out", [DEPTH, D, D]), ("w_router", [DEPTH, D, NE]),
    ("w_e_gate", [DEPTH, NE, D, D]), ("w_e_up", [DEPTH, NE, D, D]), ("w_e_down", [DEPTH, NE, D, D]),
]


class K:
    def __init__(self, ext_in=(), ext_out=()):
        self.nc = bass.Bass("TRN2", target_bir_lowering=False)
        self.ext_in = set(ext_in)
        self.ext_out = set(ext_out)
        self.dram = {}
        self.stack = contextlib.ExitStack()
        self.P = Prog(self.nc, self.stack)
        self.uid = 0

    def dt(self, name, shape, dtype, kind=None):
        if name in self.dram:
            return self.dram[name]
        if kind is None:
            kind = "ExternalInput" if name in self.ext_in else ("ExternalOutput" if name in self.ext_out else "Internal")
        t = self.nc.dram_tensor(name, list(shape), dtype, kind=kind).ap()
        self.dram[name] = t
        return t

    def sb(self, st, shape, dtype, name=None):
        self.uid += 1
        return TT(st.enter_context(self.nc.sbuf_tensor("%s_%d" % (name or "sb", self.uid), list(shape), dtype)))

    def dbg(self, name, tile_ap, shape, dtype, reads):
        if name not in self.ext_out or name in self.dram:
            return
        t = self.dt(name, shape, dtype)
        self.P.dma("sp", DMA(t, tile_ap), reads=reads, writes=[Buf()])

    def ps(self, st, shape, dtype, name=None):
        self.uid += 1
        return TT(st.enter_context(self.nc.psum_tensor("%s_%d" % (name or "ps", self.uid), list(shape), dtype)))


def declare_io(k):
    k.dt("x", [NB, T, D], F32, "ExternalInput")
    k.dt("ctx", [NB, S, D], F32, "ExternalInput")
    k.dt("ccT", [128, 8, 3], F32, "ExternalInput")
    for n, shp in WEIGHT_SPECS:
        k.dt(n, shp, F32, "ExternalInput")
    k.dt("ident_bf", [128, 128], BF16, "ExternalInput")
    k.dt("ident_f", [128, 128], F32, "ExternalInput")
    k.dt("mask_low", [128, 128], BF16, "ExternalInput")
    k.dt("mask_up", [128, 128], BF16, "ExternalInput")
    k.dt("tri_f", [128, 128], F32, "ExternalInput")
    k.dt("tri_b", [128, 128], F32, "ExternalInput")
    for tag, L in (("L", T), ("C", S)):
        nt = L // 128
        k.dt("zT" + tag, [33, L], F32, "ExternalInput")
        k.dt("dec" + tag, [L, 256], F32, "ExternalInput")
        k.dt("alt" + tag, [128, nt], BF16, "ExternalInput")
        for nm in ("Cm", "Sm", "Si"):
            k.dt(nm + tag, [nt, 128, nt, 128], BF16, "ExternalInput")
    k.dt("ropeA", [T, 64], F32, "ExternalInput")
    k.dt("ropeD", [T, 32], F32, "ExternalInput")
    k.dt("y", [NB, T, D], F32, "ExternalOutput")
    k.dt("mod", [DEPTH, 3, 6 * D], F32)
    k.dt("ctxs", [NB, S + 128, D], F32)
    for b in range(NB):
        k.dt("pL%d" % b, [T, INW], BF16)
        k.dt("pC%d" % b, [S, INW], BF16)
        k.dt("gL%d" % b, [T, 16], F32)
        k.dt("gC%d" % b, [S, 16], F32)
        k.dt("ycL%d" % b, [T, D], BF16)
        k.dt("ycC%d" % b, [S, D], BF16)
        k.dt("h2L%d" % b, [T, D], BF16)
        k.dt("h2C%d" % b, [S + 128, D], BF16)
        k.dt("affL%d" % b, [16, T], F32)
        k.dt("affC%d" % b, [16, S], F32)
    k.dt("idxL", [128, 4, 64], I32)
    k.dt("gateL", [128, 4, 64], F32)
    k.dt("idxC", [128, 1, 64], I32)
    k.dt("gateC", [128, 1, 64], F32)
    k.dt("zeros", [128, D], F32, "ExternalInput")


def streams(k):
    out = []
    for b in range(NB):
        out.append(("L%d" % b, T, k.dram["y"][b], b))
    for b in range(NB):
        out.append(("C%d" % b, S, k.dram["ctxs"][b, 0:S], 2))
    return out


def phase_init(k):
    P = k.P
    d = k.dram
    by = Buf()
    for b in range(NB):
        for j in range(4):
            P.dma("sp", DMA(d["y"][b, j * 1024:(j + 1) * 1024, :], d["x"][b, j * 1024:(j + 1) * 1024, :]), writes=[by])
        P.dma("sp", DMA(d["ctxs"][b, 0:S], d["ctx"][b]), writes=[by])
        P.dma("sp", DMA(d["ctxs"][b, S:S + 128], d["zeros"]), writes=[by])
        P.dma("sp", DMA(d["h2C%d" % b][S:S + 128], d["zeros"][:, 0:D // 2].bitcast(BF16)), writes=[by])
    P.barrier()


def phase_ada(k, layers):
    P = k.P
    d = k.dram
    with contextlib.ExitStack() as st:
        cc = k.sb(st, [128, 24], F32)
        sT = k.sb(st, [128, 24], BF16)
        P.dma("sp", DMA(cc[:], d["ccT"].rearrange("p k r -> p (k r)")), writes=[cc])
        P.op("act", ACT(sT[:], cc[:], AF.Silu), reads=[cc], writes=[sT])
        wb = [k.sb(st, [128, 8, 512], BF16, "wada") for _ in range(3)]
        pss = [k.ps(st, [128, 512], F32, "adaps") for _ in range(2)]
        bada = k.sb(st, [3, 6 * D], F32)
        modsb = k.sb(st, [3, 6 * D], F32)
        it = 0
        for l in layers:
            P.dma("sp", DMA(bada[:], d["b_ada"][l:l + 1, :].broadcast_to([3, 6 * D])), writes=[bada])
            for j in range(12):
                w = wb[it % 3]
                ps = pss[it % 2]
                it += 1
                src = d["w_ada"][l, :, j * 512:(j + 1) * 512].rearrange("(k p) n -> p k n", p=128)
                P.dma("pool", DMA(w[:], src), writes=[w])
                for kk in range(8):
                    P.op("pe", MM(ps[0:3, :], sT[:, kk * 3:(kk + 1) * 3], w[:, kk, :], start=(kk == 0), stop=(kk == 7)),
                         reads=[sT, w], writes=[ps], sig=(kk == 7))
                P.op("dve", TTo(modsb[0:3, j * 512:(j + 1) * 512], ps[0:3, :], bada[0:3, j * 512:(j + 1) * 512], ALU.add),
                     reads=[ps, bada], writes=[modsb])
            P.dma("sp", DMA(d["mod"][l], modsb[:]), reads=[modsb], writes=[Buf()])
        P.barrier()


def load_bcast(k, P, dst, src_row, eng="sp"):
    n = src_row.shape[-1]
    P.dma(eng, DMA(dst[:], src_row.broadcast_to([128, n])), writes=[dst])


def norm_mod_T(k, P, st_tiles, xsrc, A, sh, ident, hT, j, tp, xt, tmp, hb, ss, junk):
    P.dma("sp", DMA(xt[:], xsrc), writes=[xt])
    P.op("act", ACT(junk[:], xt[:], AF.Square, accum_out=ss[:, 0:1]), reads=[xt], writes=[junk, ss])
    P.op("dve", TS(ss[:, 1:2], ss[:, 0:1], 1.0 / D, EPS, ALU.mult, ALU.add), reads=[ss], writes=[ss])
    P.op("act", ACT(ss[:, 2:3], ss[:, 1:2], AF.Sqrt), reads=[ss], writes=[ss])
    P.op("dve", RCP(ss[:, 3:4], ss[:, 2:3]), reads=[ss], writes=[ss])
    P.op("dve", STT(tmp[:], xt[:], ss[:, 3:4], A[:], ALU.mult, ALU.mult), reads=[xt, ss, A], writes=[tmp])
    P.op("dve", TTo(hb[:], tmp[:], sh[:], ALU.add), reads=[tmp, sh], writes=[hb])
    for kk in range(8):
        P.op("pe", TR(tp[:, kk, :], hb[:, kk * 128:(kk + 1) * 128], ident[:]), reads=[hb, ident], writes=[tp], sig=(kk == 7))
    P.op("act", ACT(hT[:, :, j * 128:(j + 1) * 128], tp[:], AF.Copy), reads=[tp], writes=[hT])


def phase_proj(k, l):
    P = k.P
    d = k.dram
    with contextlib.ExitStack() as st:
        ident = k.sb(st, [128, 128], BF16)
        P.dma("sp", DMA(ident[:], d["ident_bf"]), writes=[ident])
        win = k.sb(st, [128, 8, INW], BF16, "win")
        wbufs = [Buf() for _ in range(7)]
        wbufs[6] = wbufs[5]
        for c in range(6):
            c0 = c * 512
            c1 = INW if c == 5 else c0 + 512
            src = d["w_in"][l, :, c0:c1].rearrange("(k p) n -> p k n", p=128)
            P.dma("pool", DMA(win[:, :, c0:c1], src), writes=[wbufs[c]])
        g1 = k.sb(st, [128, D], F32)
        load_bcast(k, P, g1, d["norm1_g"][l:l + 1, :])
        A = k.sb(st, [128, D], F32)
        sh = k.sb(st, [128, D], F32)
        sc = k.sb(st, [128, D], F32)
        xts = [k.sb(st, [128, D], F32, "xt") for _ in range(3)]
        tmps = [k.sb(st, [128, D], F32, "tmp") for _ in range(2)]
        hbs = [k.sb(st, [128, D], BF16, "hb") for _ in range(2)]
        sss = [k.sb(st, [128, 4], F32, "ss") for _ in range(3)]
        junk = k.sb(st, [128, D], BF16, "junk")
        hTs = [k.sb(st, [128, 8, 512], BF16, "hT") for _ in range(2)]
        pbs = [k.sb(st, [128, INW], BF16, "pb") for _ in range(2)]
        gbs = [k.sb(st, [128, 16], F32, "gb") for _ in range(2)]
        tps = [k.ps(st, [128, 8, 128], BF16, "tp") for _ in range(2)]
        accs = [k.ps(st, [128, 512], F32, "acc") for _ in range(4)]
        it = 0
        ia = 0
        ig = 0
        for (tag, ntok, xs, mrow) in streams(k):
            load_bcast(k, P, sc, d["mod"][l, mrow:mrow + 1, D:2 * D])
            load_bcast(k, P, sh, d["mod"][l, mrow:mrow + 1, 0:D])
            P.op("dve", STT(A[:], sc[:], 1.0, g1[:], ALU.add, ALU.mult), reads=[sc, g1], writes=[A])
            gs = min(512, ntok)
            nt = gs // 128
            pd = d["p" + tag]
            gd = d["g" + tag]
            for g in range(ntok // gs):
                hT = hTs[ig % 2]
                ig += 1
                for j in range(nt):
                    r0 = g * gs + j * 128
                    norm_mod_T(k, P, None, xs[r0:r0 + 128, :], A, sh, ident, hT, j, tps[it % 2], xts[it % 3],
                               tmps[it % 2], hbs[it % 2], sss[it % 3], junk)
                    it += 1
                for j in range(nt):
                    r0 = g * gs + j * 128
                    pb = pbs[j % 2]
                    gb = gbs[j % 2]
                    for c in range(7):
                        c0 = c * 448
                        c1 = min(INW, c0 + 448)
                        acc = accs[ia % 4]
                        ia += 1
                        for kk in range(8):
                            P.op("pe", MM(acc[:, 0:c1 - c0], hT[:, kk, j * 128:(j + 1) * 128], win[:, kk, c0:c1],
                                          start=(kk == 0), stop=(kk == 7)), reads=[hT] + wbufs[0:6], writes=[acc], sig=(kk == 7))
                        if ia % 2 == 0:
                            P.op("dve", CP(pb[:, c0:c1], acc[:, 0:c1 - c0]), reads=[acc], writes=[pb])
                        else:
                            P.op("act", ACT(pb[:, c0:c1], acc[:, 0:c1 - c0], AF.Copy), reads=[acc], writes=[pb])
                        if c == 5:
                            P.op("dve", CP(gb[:], acc[:, C_G - c0:C_G - c0 + 16]), reads=[acc], writes=[gb])
                    P.dma("sp", DMA(pd[r0:r0 + 128, :], pb[:]), reads=[pb], writes=[Buf()])
                    P.dma("sp", DMA(gd[r0:r0 + 128, :], gb[:]), reads=[gb], writes=[Buf()])
        if not getattr(k, "no_proj_barrier", False):
            P.barrier()


_CONSTS = None


def consts():
    global _CONSTS
    if _CONSTS is None:
        c = {}
        c["ident_bf"] = np.eye(128, dtype=np.float32).astype(ml_dtypes.bfloat16)
        c["ident_f"] = np.eye(128, dtype=np.float32)
        c["zeros"] = np.zeros((128, D), np.float32)
        kk = np.arange(128)[:, None]
        qq = np.arange(128)[None, :]
        c["mask_low"] = (qq <= kk).astype(np.float32).astype(ml_dtypes.bfloat16)
        c["mask_up"] = (kk <= qq).astype(np.float32).astype(ml_dtypes.bfloat16)
        c["tri_f"] = (kk <= qq).astype(np.float32) - np.float32(0.5)
        c["tri_b"] = (kk >= qq).astype(np.float32) - np.float32(0.5)
        for tag, L in (("L", T), ("C", S)):
            nt = L // 128
            tt = np.linspace(0.0, 1.0, L, dtype=np.float32)[:, None]
            w = (np.float32(2.0 * math.pi) * np.arange(L, dtype=np.float32)[:, None] / np.float32(L)).astype(np.float32)
            bands = np.linspace(1e-4, 15, 16, dtype=np.float32)
            zz = np.concatenate([tt, np.cos(bands * w), -np.sin(bands * w)], axis=-1).astype(np.float32)
            c["zT" + tag] = np.ascontiguousarray(zz.T)
            deltas = np.abs(np.linspace(math.log(1e-2) / 1.5, math.log(1e-2) / 0.3, 256, dtype=np.float32))
            c["dec" + tag] = np.exp(-tt * deltas).astype(np.float32)
            idx = np.arange(L, dtype=np.int64)
            ang = (2.0 * np.pi / (2 * L)) * ((idx[:, None] * idx[None, :]) % (2 * L)).astype(np.float64)
            Cm = np.cos(ang)
            Sn = np.sin(ang)
            altv = np.where(idx % 2 == 0, 1.0, -1.0)
            SmP = -Sn.copy()
            SmP[:, 0] = altv
            Si = -Sn.copy()
            Si[0, :] = altv
            def tile(M):
                return np.ascontiguousarray(M.reshape(nt, 128, nt, 128).transpose(2, 1, 0, 3)).astype(ml_dtypes.bfloat16)
            c["Cm" + tag] = tile(Cm)
            c["Sm" + tag] = tile(SmP)
            c["Si" + tag] = tile(Si)
            c["alt" + tag] = np.ascontiguousarray(altv.reshape(nt, 128).T).astype(ml_dtypes.bfloat16)
        for nm, dh in (("ropeA", 64), ("ropeD", 32)):
            nf = dh // 4
            t = np.arange(T)
            r = (t // 64).astype(np.float32)
            col = (t % 64).astype(np.float32)
            inv = (np.float32(10000.0) ** (-np.arange(nf, dtype=np.float32) / np.float32(nf))).astype(np.float32)
            ang = np.stack([r[:, None] * inv, col[:, None] * inv], axis=1).astype(np.float32)
            c[nm] = np.concatenate([np.cos(ang).reshape(T, 2 * nf), np.sin(ang).reshape(T, 2 * nf)], axis=1).astype(np.float32)
        _CONSTS = c
    return _CONSTS


def core_inputs(inp, core):
    m = {}
    b0 = core * NB
    m["x"] = np.ascontiguousarray(inp["x"][b0:b0 + NB])
    m["ctx"] = np.ascontiguousarray(inp["ctx"][b0:b0 + NB])
    cc = np.concatenate([inp["c"][b0:b0 + NB], inp["c_ctx"][None, :]], axis=0)
    m["ccT"] = np.ascontiguousarray(cc.T.reshape(8, 128, 3).transpose(1, 0, 2))
    for n, shp in WEIGHT_SPECS:
        m[n] = np.ascontiguousarray(np.asarray(inp[n], dtype=np.float32).reshape(shp))
    m.update(consts())
    return m


def qk_prep(P, eng, src, U, dh, gains, tab, out, w):
    sq, qn, ss = w["sq"], w["qn"], w["ss"]
    n = U * dh
    P.op(eng, TTo(sq[:, 0:n], src[:, 0:n], src[:, 0:n], ALU.mult), reads=[src], writes=[sq])
    P.op("dve", RED(ss[:, 0:U], sq[:, 0:n].rearrange("p (u d) -> p u d", d=dh)), reads=[sq], writes=[ss])
    P.op("dve", TS(ss[:, U:2 * U], ss[:, 0:U], 1.0 / dh, EPS, ALU.mult, ALU.add), reads=[ss], writes=[ss])
    P.op("act", ACT(ss[:, 0:U], ss[:, U:2 * U], AF.Sqrt), reads=[ss], writes=[ss])
    P.op("dve", RCP(ss[:, 2 * U:3 * U], ss[:, 0:U]), reads=[ss], writes=[ss])
    rb = ss[:, 2 * U:3 * U].unsqueeze(2).broadcast_to([128, U, dh])
    qv = qn[:, 0:n].rearrange("p (u d) -> p u d", d=dh)
    P.op("dve", TTo(qv, src[:, 0:n].rearrange("p (u d) -> p u d", d=dh), rb, ALU.mult), reads=[src, ss], writes=[qn])
    if tab is None:
        P.op(eng, TTo(out[:, 0:n], qn[:, 0:n], gains[:].rearrange("p u d -> p (u d)"), ALU.mult), reads=[qn, gains], writes=[out])
        return
    P.op(eng, TTo(qn[:, 0:n], qn[:, 0:n], gains[:].rearrange("p u d -> p (u d)"), ALU.mult), reads=[qn, gains], writes=[qn])
    nf = dh // 4
    q5 = qn[:, 0:n].rearrange("p (u a h f) -> p u a h f", a=2, h=2, f=nf)
    o5 = out[:, 0:n].rearrange("p (u a h f) -> p u a h f", a=2, h=2, f=nf)
    a_, b_ = q5[:, :, :, 0, :], q5[:, :, :, 1, :]
    c_ = tab[:, 0].unsqueeze(1).broadcast_to([128, U, 2, nf])
    s_ = tab[:, 1].unsqueeze(1).broadcast_to([128, U, 2, nf])
    hn = n // 2
    t1 = w["t1"][:, 0:hn].rearrange("p (u a f) -> p u a f", a=2, f=nf)
    t2 = w["t2"][:, 0:hn].rearrange("p (u a f) -> p u a f", a=2, f=nf)
    P.op("dve", TTo(t1, a_, c_, ALU.mult), reads=[qn, tab], writes=[w["t1"]])
    P.op(eng, TTo(t2, b_, s_, ALU.mult), reads=[qn, tab], writes=[w["t2"]])
    P.op("dve", TTo(o5[:, :, :, 0, :], t1, t2, ALU.subtract), reads=[w["t1"], w["t2"]], writes=[out])
    P.op(eng, TTo(t1, b_, c_, ALU.mult), reads=[qn, tab], writes=[w["t1"]])
    P.op("dve", TTo(t2, a_, s_, ALU.mult), reads=[qn, tab], writes=[w["t2"]])
    P.op(eng, TTo(o5[:, :, :, 1, :], t1, t2, ALU.add), reads=[w["t1"], w["t2"]], writes=[out])


def prep_work(k, st, n, U):
    return {"sq": k.sb(st, [128, n], F32, "sq"), "qn": k.sb(st, [128, n], F32, "qn"), "ss": k.sb(st, [128, 3 * U], F32, "ss"),
            "t1": k.sb(st, [128, n // 2], F32, "t1"), "t2": k.sb(st, [128, n // 2], F32, "t2")}


def phase_mixA(k, l, ctx_out):
    P = k.P
    d = k.dram
    scale = 64 ** -0.5
    with contextlib.ExitStack() as st:
        ident = k.sb(st, [128, 128], BF16)
        P.dma("sp", DMA(ident[:], d["ident_bf"]), writes=[ident])
        lowm = k.sb(st, [128, 128], BF16)
        upm = k.sb(st, [128, 128], BF16)
        P.dma("sp", DMA(lowm[:], d["mask_low"]), writes=[lowm])
        P.dma("sp", DMA(upm[:], d["mask_up"]), writes=[upm])
        gains = k.sb(st, [128, 6, 64], F32)
        gq = k.sb(st, [128, 64], F32)
        gk = k.sb(st, [128, 64], F32)
        load_bcast(k, P, gq, d["a_qnorm"][l:l + 1, :])
        load_bcast(k, P, gk, d["a_knorm"][l:l + 1, :])
        P.op("dve", CP(gains[:, 0:4, :], gq[:].unsqueeze(1).broadcast_to([128, 4, 64])), reads=[gq], writes=[gains])
        P.op("dve", CP(gains[:, 4:6, :], gk[:].unsqueeze(1).broadcast_to([128, 2, 64])), reads=[gk], writes=[gains])
        esink = k.sb(st, [128, 4], F32)
        load_bcast(k, P, esink, d["a_sink"][l:l + 1, :])
        P.op("act", ACT(esink[:], esink[:], AF.Exp), reads=[esink], writes=[esink])
        qT = k.sb(st, [64, 4, T], BF16, "qT")
        kT = k.sb(st, [64, 2, T + S], BF16, "kT")
        qcT = k.sb(st, [64, 4, S], BF16, "qcT")
        va = k.sb(st, [128, 34, 2, 65], BF16, "va")
        P.op("pool", MSET(va[:], 1.0), writes=[va])
        srcs = [k.sb(st, [128, 384], BF16, "src") for _ in range(3)]
        outs = [k.sb(st, [128, 384], BF16, "out") for _ in range(2)]
        tabs = [k.sb(st, [128, 2, 2, 16], F32, "tab") for _ in range(3)]
        works = [prep_work(k, st, 384, 6) for _ in range(2)]
        tps = [k.ps(st, [64, 6, 128], BF16, "tpA") for _ in range(2)]
        sps = [k.ps(st, [128, 256], F32, "spA") for _ in range(3)]
        ops = [k.ps(st, [128, 4, 65], F32, "opA") for _ in range(2)]
        pts = [k.sb(st, [128, 256], BF16, "ptA") for _ in range(3)]
        dens = [k.sb(st, [128, 8], F32, "denA") for _ in range(2)]
        yas = [k.sb(st, [128, 4, 64], BF16, "yaA") for _ in range(2)]
        for b in range(NB):
            pL, pC = d["pL%d" % b], d["pC%d" % b]
            for i in range(34):
                src, out, tab, w, tp = srcs[i % 3], outs[i % 2], tabs[i % 3], works[i % 2], tps[i % 2]
                lat = i < 32
                pd = pL if lat else pC
                r0 = (i if lat else i - 32) * 128
                P.dma("sp", DMA(src[:], pd[r0:r0 + 128, A_Q:A_Q + 384]), writes=[src])
                P.dma("sp", DMA(va[:, i, :, 0:64], pd[r0:r0 + 128, A_V:A_V + 128].rearrange("p (h e) -> p h e", e=64)),
                      writes=[va])
                if lat:
                    P.dma("sp", DMA(tab[:].rearrange("p c a f -> p (c a f)"), d["ropeA"][r0:r0 + 128, :]), writes=[tab])
                qk_prep(P, "dve", src, 6, 64, gains, tab if lat else None, out, w)
                k.dbg("dbgA_src", src[:], [128, 384], BF16, [src])
                k.dbg("dbgA_ss", w["ss"][:], [128, 18], F32, [w["ss"]])
                k.dbg("dbgA_qn", w["qn"][:], [128, 384], F32, [w["qn"]])
                k.dbg("dbgA_out", out[:], [128, 384], BF16, [out])
                for u in range(6):
                    P.op("pe", TR(tp[:, u, :], out[:, u * 64:(u + 1) * 64], ident[:]), reads=[out, ident], writes=[tp], sig=(u == 5))
                c0 = i * 128
                if lat:
                    P.op("act", ACT(qT[:, :, c0:c0 + 128], tp[:, 0:4, :], AF.Copy), reads=[tp], writes=[qT])
                else:
                    P.op("act", ACT(qcT[:, :, r0:r0 + 128], tp[:, 0:4, :], AF.Copy), reads=[tp], writes=[qcT])
                P.op("act", ACT(kT[:, :, c0:c0 + 128], tp[:, 4:6, :], AF.Copy), reads=[tp], writes=[kT])
            blocks = [(n, True) for n in range(32)] + ([(n, False) for n in range(2)] if ctx_out else [])
            isp = 0
            for bi, (n, lat) in enumerate(blocks):
                op_ = ops[bi % 2]
                den = dens[bi % 2]
                ya = yas[bi % 2]
                if lat:
                    keys = []
                    if n > 0:
                        keys.append((n - 1, lowm))
                    keys.append((n, None))
                    if n < 31:
                        keys.append((n + 1, upm))
                    keys += [(32, None), (33, None)]
                else:
                    keys = [(32, None), (33, None)]
                first = True
                for h in range(2):
                    for (kt, msk) in keys:
                        sp_ = sps[isp % 3]
                        pt = pts[isp % 3]
                        isp += 1
                        if lat:
                            rhs = qT[:, 2 * h:2 * h + 2, n * 128:(n + 1) * 128]
                            qdep = qT
                        else:
                            rhs = qcT[:, 2 * h:2 * h + 2, n * 128:(n + 1) * 128]
                            qdep = qcT
                        P.op("pe", MM(sp_[:].rearrange("p (g q) -> p g q", g=2), kT[:, h, kt * 128:(kt + 1) * 128], rhs),
                             reads=[kT, qdep], writes=[sp_])
                        P.op("act", ACT(pt[:], sp_[:], AF.Exp, scale=scale), reads=[sp_], writes=[pt])
                        if msk is not None:
                            P.op("dve", TTo(pt[:].rearrange("p (g q) -> p g q", g=2), pt[:].rearrange("p (g q) -> p g q", g=2),
                                            msk[:].unsqueeze(1).broadcast_to([128, 2, 128]), ALU.mult), reads=[pt, msk], writes=[pt])
                        for g in range(2):
                            P.op("pe", MM(op_[:, 2 * h + g, :], pt[:, g * 128:(g + 1) * 128], va[:, kt, h, :], start=first, stop=True),
                                 reads=[pt, va], writes=[op_], sig=(g == 1))
                            first = False
                k.dbg("dbgA_pt", pt[:], [128, 256], BF16, [pt])
                k.dbg("dbgA_qT", qT[:, 0, 0:512], [64, 512], BF16, [qT])
                P.op("dve", TTo(den[:, 0:4], op_[:, :, 64], esink[:], ALU.add), reads=[op_, esink], writes=[den])
                P.op("dve", RCP(den[:, 4:8], den[:, 0:4]), reads=[den], writes=[den])
                P.op("dve", TTo(ya[:], op_[:, :, 0:64], den[:, 4:8].unsqueeze(2).broadcast_to([128, 4, 64]), ALU.mult),
                     reads=[op_, den], writes=[ya])
                yd = d["ycL%d" % b] if lat else d["ycC%d" % b]
                P.dma("sp", DMA(yd[n * 128:(n + 1) * 128, 0:256], ya[:].rearrange("p h e -> p (h e)")), reads=[ya], writes=[Buf()])
        P.barrier()


def phase_mixD(k, l, ctx_out):
    P = k.P
    d = k.dram
    scale = 32 ** -0.5
    lam_init = 0.8 - 0.6 * math.exp(-0.3 * l)
    with contextlib.ExitStack() as st:
        ident = k.sb(st, [128, 128], BF16)
        P.dma("sp", DMA(ident[:], d["ident_bf"]), writes=[ident])
        gains = k.sb(st, [128, 16, 32], F32)
        gq = k.sb(st, [128, 32], F32)
        gk = k.sb(st, [128, 32], F32)
        load_bcast(k, P, gq, d["d_qnorm"][l:l + 1, :])
        load_bcast(k, P, gk, d["d_knorm"][l:l + 1, :])
        P.op("dve", CP(gains[:, 0:8, :], gq[:].unsqueeze(1).broadcast_to([128, 8, 32])), reads=[gq], writes=[gains])
        P.op("dve", CP(gains[:, 8:16, :], gk[:].unsqueeze(1).broadcast_to([128, 8, 32])), reads=[gk], writes=[gains])
        lq = k.sb(st, [128, 4, 32], F32)
        for i, nm in enumerate(("d_lq1", "d_lk1", "d_lq2", "d_lk2")):
            P.dma("sp", DMA(lq[:, i, :], d[nm][l:l + 1, :].broadcast_to([128, 32])), writes=[lq])
        lam = k.sb(st, [128, 8], F32)
        lp = k.sb(st, [128, 2, 32], F32)
        P.op("dve", TTo(lp[:, 0, :], lq[:, 0, :], lq[:, 1, :], ALU.mult), reads=[lq], writes=[lp])
        P.op("dve", TTo(lp[:, 1, :], lq[:, 2, :], lq[:, 3, :], ALU.mult), reads=[lq], writes=[lp])
        P.op("dve", RED(lam[:, 0:2], lp[:]), reads=[lp], writes=[lam])
        P.op("act", ACT(lam[:, 2:4], lam[:, 0:2], AF.Exp), reads=[lam], writes=[lam])
        P.op("dve", TTo(lam[:, 4:5], lam[:, 3:4], lam[:, 2:3], ALU.subtract), reads=[lam], writes=[lam])
        P.op("dve", TS(lam[:, 5:6], lam[:, 4:5], -lam_init, None, ALU.add), reads=[lam], writes=[lam])
        neglam = lam[:, 5:6]
        subg = k.sb(st, [128, 64], F32)
        load_bcast(k, P, subg, d["d_subnorm"][l:l + 1, :])
        P.op("dve", TS(subg[:], subg[:], 1.0 - lam_init, None, ALU.mult), reads=[subg], writes=[subg])
        qT = k.sb(st, [128, 3, T], BF16, "qTd")
        kT = k.sb(st, [128, 3, T + S], BF16, "kTd")
        qcT = k.sb(st, [128, 3, S], BF16, "qcTd")
        va = k.sb(st, [128, 34, 4, 65], BF16, "vad")
        P.op("pool", MSET(va[:], 1.0), writes=[va])
        srcs = [k.sb(st, [128, 512], BF16, "src") for _ in range(3)]
        outs = [k.sb(st, [128, 512], BF16, "out") for _ in range(2)]
        tabs = [k.sb(st, [128, 2, 2, 8], F32, "tab") for _ in range(3)]
        works = [prep_work(k, st, 512, 16) for _ in range(2)]
        tps = [k.ps(st, [128, 6, 128], BF16, "tpD") for _ in range(2)]
        sps = [k.ps(st, [128, 512], F32, "spD") for _ in range(2)]
        accs = [k.ps(st, [128, 4, 65], F32, "accD") for _ in range(4)]
        pts = [k.sb(st, [128, 512], BF16, "ptD") for _ in range(3)]
        rr = [k.sb(st, [128, 16], F32, "rD") for _ in range(2)]
        o1s = [k.sb(st, [128, 4, 64], F32, "o1D") for _ in range(2)]
        o2s = [k.sb(st, [128, 4, 64], F32, "o2D") for _ in range(2)]
        yds = [k.sb(st, [128, 4, 256], BF16, "ydD") for _ in range(2)]
        for b in range(NB):
            pL, pC = d["pL%d" % b], d["pC%d" % b]
            for i in range(34):
                src, out, tab, w, tp = srcs[i % 3], outs[i % 2], tabs[i % 3], works[i % 2], tps[i % 2]
                lat = i < 32
                pd = pL if lat else pC
                r0 = (i if lat else i - 32) * 128
                P.dma("sp", DMA(src[:], pd[r0:r0 + 128, D_Q:D_Q + 512]), writes=[src])
                P.dma("sp", DMA(va[:, i, :, 0:64], pd[r0:r0 + 128, D_V:D_V + 256].rearrange("p (h e) -> p h e", e=64)),
                      writes=[va])
                if lat:
                    P.dma("sp", DMA(tab[:].rearrange("p c a f -> p (c a f)"), d["ropeD"][r0:r0 + 128, :]), writes=[tab])
                qk_prep(P, "dve", src, 16, 32, gains, tab if lat else None, out, w)
                for u in range(6):
                    qk0 = (u // 3) * 256
                    cw = 96 if u % 3 < 2 else 64
                    cb = qk0 + (u % 3) * 96
                    P.op("pe", TR(tp[0:cw, u, :], out[:, cb:cb + cw], ident[:]), reads=[out, ident], writes=[tp], sig=(u == 5))
                c0 = i * 128
                if lat:
                    P.op("act", ACT(qT[0:96, :, c0:c0 + 128], tp[0:96, 0:3, :], AF.Copy), reads=[tp], writes=[qT])
                else:
                    P.op("act", ACT(qcT[0:96, :, r0:r0 + 128], tp[0:96, 0:3, :], AF.Copy), reads=[tp], writes=[qcT])
                P.op("act", ACT(kT[0:96, :, c0:c0 + 128], tp[0:96, 3:6, :], AF.Copy), reads=[tp], writes=[kT])
            chunks = [(q, True) for q in range(8)] + ([(0, False)] if ctx_out else [])
            isp = 0
            iacc = 0
            for ci, (Q, lat) in enumerate(chunks):
                nq = 512 if lat else 256
                nqt = nq // 128
                keys = list(range(34)) if lat else [32, 33]
                yd = yds[ci % 2]
                for hd in range(4):
                    acc2 = (accs[(iacc % 2) * 2], accs[(iacc % 2) * 2 + 1])
                    r_, o1, o2 = rr[iacc % 2], o1s[iacc % 2], o2s[iacc % 2]
                    iacc += 1
                    for c in range(2):
                        u = hd * 2 + c
                        grp, j = u // 3, u % 3
                        acc = acc2[c]
                        first = True
                        for kt in keys:
                            sp_ = sps[isp % 2]
                            pt = pts[isp % 3]
                            isp += 1
                            if lat:
                                rhs = qT[32 * j:32 * j + 32, grp, Q * 512:Q * 512 + nq]
                                qdep = qT
                            else:
                                rhs = qcT[32 * j:32 * j + 32, grp, 0:nq]
                                qdep = qcT
                            P.op("pe", MM(sp_[:, 0:nq], kT[32 * j:32 * j + 32, grp, kt * 128:(kt + 1) * 128], rhs),
                                 reads=[kT, qdep], writes=[sp_])
                            P.op("act", ACT(pt[:, 0:nq], sp_[:, 0:nq], AF.Exp, scale=scale), reads=[sp_], writes=[pt])
                            for qt in range(nqt):
                                P.op("pe", MM(acc[:, qt, :], pt[:, qt * 128:(qt + 1) * 128], va[:, kt, hd, :], start=first, stop=True),
                                     reads=[pt, va], writes=[acc], sig=(qt == nqt - 1))
                                first = False
                    a0, a1 = acc2
                    P.op("dve", RCP(r_[:, 0:nqt], a0[:, 0:nqt, 64]), reads=[a0], writes=[r_])
                    P.op("dve", RCP(r_[:, 4:4 + nqt], a1[:, 0:nqt, 64]), reads=[a1], writes=[r_])
                    P.op("dve", TS(r_[:, 4:4 + nqt], r_[:, 4:4 + nqt], neglam, None, ALU.mult), reads=[r_, lam], writes=[r_])
                    P.op("dve", TTo(o1[:, 0:nqt, :], a0[:, 0:nqt, 0:64], r_[:, 0:nqt].unsqueeze(2).broadcast_to([128, nqt, 64]), ALU.mult),
                         reads=[a0, r_], writes=[o1])
                    P.op("dve", TTo(o2[:, 0:nqt, :], a1[:, 0:nqt, 0:64], r_[:, 4:4 + nqt].unsqueeze(2).broadcast_to([128, nqt, 64]), ALU.mult),
                         reads=[a1, r_], writes=[o2])
                    P.op("dve", TTo(o1[:, 0:nqt, :], o1[:, 0:nqt, :], o2[:, 0:nqt, :], ALU.add), reads=[o1, o2], writes=[o1])
                    P.op("dve", TTo(o2[:, 0:nqt, :], o1[:, 0:nqt, :], o1[:, 0:nqt, :], ALU.mult), reads=[o1], writes=[o2])
                    P.op("dve", RED(r_[:, 8:8 + nqt], o2[:, 0:nqt, :]), reads=[o2], writes=[r_])
                    P.op("dve", TS(r_[:, 8:8 + nqt], r_[:, 8:8 + nqt], 1.0 / 64, EPS, ALU.mult, ALU.add), reads=[r_], writes=[r_])
                    P.op("act", ACT(r_[:, 12:12 + nqt], r_[:, 8:8 + nqt], AF.Sqrt), reads=[r_], writes=[r_])
                    P.op("dve", RCP(r_[:, 8:8 + nqt], r_[:, 12:12 + nqt]), reads=[r_], writes=[r_])
                    P.op("dve", TTo(o1[:, 0:nqt, :], o1[:, 0:nqt, :], r_[:, 8:8 + nqt].unsqueeze(2).broadcast_to([128, nqt, 64]), ALU.mult),
                         reads=[o1, r_], writes=[o1])
                    P.op("dve", TTo(yd[:, 0:nqt, hd * 64:(hd + 1) * 64], o1[:, 0:nqt, :], subg[:].unsqueeze(1).broadcast_to([128, nqt, 64]), ALU.mult),
                         reads=[o1, subg], writes=[yd])
                ydst = d["ycL%d" % b] if lat else d["ycC%d" % b]
                rows = ydst[Q * 512:Q * 512 + nq, 768:1024].rearrange("(q p) c -> p q c", p=128)
                P.dma("sp", DMA(rows, yd[:, 0:nqt, :]), reads=[yd], writes=[Buf()])
        P.barrier()


def phase_mixC(k, l, ctx_out):
    P = k.P
    d = k.dram
    with contextlib.ExitStack() as st:
        ident = k.sb(st, [128, 128], BF16)
        P.dma("sp", DMA(ident[:], d["ident_bf"]), writes=[ident])
        masks = [k.sb(st, [128, 128], BF16, "mk") for _ in range(2)]
        P.dma("sp", DMA(masks[0][:], d["mask_up"]), writes=[masks[0]])
        P.dma("sp", DMA(masks[1][:], d["mask_low"]), writes=[masks[1]])
        tris = [k.sb(st, [128, 128], F32, "tri") for _ in range(2)]
        P.dma("sp", DMA(tris[0][:], d["tri_f"]), writes=[tris[0]])
        P.dma("sp", DMA(tris[1][:], d["tri_b"]), writes=[tris[1]])
        halfo = k.sb(st, [128, 128], F32)
        P.op("pool", MSET(halfo[:], 0.5), writes=[halfo])
        bg = k.sb(st, [128, 16], F32)
        load_bcast(k, P, bg, d["b_gate"][l:l + 1, :])
        mlg = k.sb(st, [128, 64], F32)
        load_bcast(k, P, mlg, d["ml_norm"][l:l + 1, :])
        hf = k.sb(st, [128, 34, 256], F32, "hf")
        caug = k.sb(st, [128, 2, 65], F32, "caug")
        srcs = [k.sb(st, [128, 768], BF16, "srcC") for _ in range(3)]
        opres = [k.sb(st, [128, 256], BF16, "opre") for _ in range(2)]
        gts = [k.sb(st, [128, 16], F32, "gt") for _ in range(3)]
        vaugs = [k.sb(st, [128, 4, 65], BF16, "vaug") for _ in range(2)]
        for v in vaugs:
            P.op("pool", MSET(v[:], 1.0), writes=[v])
        gws = [k.sb(st, [128, 40], F32, "gw") for _ in range(2)]
        qks = [k.sb(st, [128, 512], BF16, "qk") for _ in range(2)]
        qkTs = [k.sb(st, [128, 4, 128], BF16, "qkT") for _ in range(2)]
        sts = [k.sb(st, [128, 128], BF16, "ST") for _ in range(3)]
        css = [k.sb(st, [128, 2, 65], BF16, "Cs") for _ in range(2)]
        ctmp = k.sb(st, [128, 65], F32, "ctmp")
        hws = [k.sb(st, [128, 16], F32, "hw") for _ in range(2)]
        hts = [k.sb(st, [128, 4, 64], F32, "ht") for _ in range(2)]
        h2s = [k.sb(st, [128, 4, 64], F32, "h2") for _ in range(2)]
        sgs = [k.sb(st, [128, 256], F32, "sg") for _ in range(2)]
        ycs = [k.sb(st, [128, 256], BF16, "yc") for _ in range(2)]
        psg = [k.ps(st, [128, 8], F32, "psg") for _ in range(1)]
        tpc = [k.ps(st, [128, 4, 128], BF16, "tpc") for _ in range(2)]
        sps = [k.ps(st, [128, 128], F32, "spC") for _ in range(2)]
        ups = [k.ps(st, [128, 2, 65], F32, "upC") for _ in range(1)]
        ops = [k.ps(st, [128, 4, 65], F32, "opC") for _ in range(2)]
        it = 0
        ist = 0
        for b in range(NB):
            pL, pC, gL, gC = d["pL%d" % b], d["pC%d" % b], d["gL%d" % b], d["gC%d" % b]
            for dr in range(2):
                order = [32, 33] + list(range(32)) if dr == 0 else [33, 32] + list(range(31, -1, -1))
                icol, fcol = (0, 8) if dr == 0 else (4, 12)
                P.op("dve", MSET(caug[:], 0.0), writes=[caug])
                for i in order:
                    lat = i < 32
                    pd, gd = (pL, gL) if lat else (pC, gC)
                    r0 = (i if lat else i - 32) * 128
                    src, gt, vaug, gw, qk, qkT = srcs[it % 3], gts[it % 3], vaugs[it % 2], gws[it % 2], qks[it % 2], qkTs[it % 2]
                    tp, op_, cs, hw, ht, h2 = tpc[it % 2], ops[it % 2], css[it % 2], hws[it % 2], hts[it % 2], h2s[it % 2]
                    it += 1
                    P.dma("sp", DMA(src[:], pd[r0:r0 + 128, C_Q:C_Q + 768]), writes=[src])
                    P.dma("sp", DMA(gt[:], gd[r0:r0 + 128, :]), writes=[gt])
                    P.op("pool", CP(vaug[:, :, 0:64], src[:, 512:768].rearrange("p (h e) -> p h e", e=64)), reads=[src], writes=[vaug])
                    P.op("dve", TTo(gw[:, 0:4], gt[:, icol:icol + 4], bg[:, icol:icol + 4], ALU.add), reads=[gt, bg], writes=[gw])
                    P.op("dve", TTo(gw[:, 4:8], gt[:, fcol:fcol + 4], bg[:, fcol:fcol + 4], ALU.add), reads=[gt, bg], writes=[gw])
                    P.op("act", ACT(gw[:, 8:12], gw[:, 4:8], AF.Exp, scale=-1.0), reads=[gw], writes=[gw])
                    P.op("act", ACT(gw[:, 12:16], gw[:, 8:12], AF.Ln, bias=1.0), reads=[gw], writes=[gw])
                    pg = psg[0]
                    P.op("pe", MM(pg[:, 0:4], tris[dr][:], gw[:, 12:16], start=True, stop=True), reads=[tris[dr], gw], writes=[pg], sig=False)
                    P.op("pe", MM(pg[:, 4:8], halfo[:], gw[:, 12:16], start=False, stop=True), reads=[halfo, gw], writes=[pg])
                    P.op("act", ACT(gw[:, 16:20], pg[:, 0:4], AF.Exp, scale=-1.0), reads=[pg], writes=[gw])
                    P.op("dve", TTo(gw[:, 20:24], pg[:, 0:4], gw[:, 0:4], ALU.add), reads=[pg, gw], writes=[gw])
                    P.op("act", ACT(gw[:, 24:28], gw[:, 20:24], AF.Exp, bias=math.log(0.125)), reads=[gw], writes=[gw])
                    P.op("act", ACT(gw[:, 28:32], pg[:, 4:8], AF.Exp, scale=-1.0), reads=[pg], writes=[gw])
                    P.op("act", ACT(gw[:, 32:36], pg[:, 4:8], AF.Exp, scale=-2.0), reads=[pg], writes=[gw])
                    P.op("dve", TTo(qk[:, 0:256].rearrange("p (h e) -> p h e", e=64), src[:, 0:256].rearrange("p (h e) -> p h e", e=64),
                                    gw[:, 16:20].unsqueeze(2).broadcast_to([128, 4, 64]), ALU.mult), reads=[src, gw], writes=[qk])
                    P.op("dve", TTo(qk[:, 256:512].rearrange("p (h e) -> p h e", e=64), src[:, 256:512].rearrange("p (h e) -> p h e", e=64),
                                    gw[:, 24:28].unsqueeze(2).broadcast_to([128, 4, 64]), ALU.mult), reads=[src, gw], writes=[qk])
                    for u in range(4):
                        P.op("pe", TR(tp[:, u, :], qk[:, u * 128:(u + 1) * 128], ident[:]), reads=[qk, ident], writes=[tp], sig=(u == 3))
                    P.op("act", ACT(qkT[:], tp[:], AF.Copy), reads=[tp], writes=[qkT])
                    for hc in range(4):
                        pp, off = hc // 2, (hc % 2) * 64
                        P.op("dve", TS(cs[off:off + 64, pp, :], caug[off:off + 64, pp, :], gw[off:off + 64, 28 + hc:29 + hc], None, ALU.mult),
                             reads=[caug, gw], writes=[cs])
                    first = True
                    for hc in range(4):
                        pp, off = hc // 2, (hc % 2) * 64
                        sp_ = sps[ist % 2]
                        ST = sts[ist % 3]
                        ist += 1
                        P.op("pe", MM(sp_[:], qkT[off:off + 64, 2 + pp, :], qkT[off:off + 64, pp, :]), reads=[qkT], writes=[sp_])
                        P.op("dve", TTo(ST[:], sp_[:], masks[dr][:], ALU.mult), reads=[sp_, masks[dr]], writes=[ST])
                        P.op("pe", MM(op_[:, hc, :], ST[:], vaug[:, hc, :], start=first, stop=True), reads=[ST, vaug], writes=[op_], sig=False)
                        first = False
                        P.op("pe", MM(op_[:, hc, :], qkT[off:off + 64, pp, :], cs[off:off + 64, pp, :], start=False, stop=True),
                             reads=[qkT, cs], writes=[op_])
                    up = ups[0]
                    for pp in range(2):
                        P.op("pe", MM(up[:].rearrange("p a e -> p (a e)"), qk[:, 256 + pp * 128:256 + (pp + 1) * 128],
                                      vaug[:, 2 * pp:2 * pp + 2, :].rearrange("p a e -> p (a e)"), start=True, stop=True),
                             reads=[qk, vaug], writes=[up])
                        for hh in range(2):
                            hc, off = 2 * pp + hh, hh * 64
                            P.op("dve", TS(ctmp[off:off + 64, :], caug[off:off + 64, pp, :], gw[off:off + 64, 32 + hc:33 + hc], None, ALU.mult),
                                 reads=[caug, gw], writes=[ctmp])
                            P.op("dve", STT(caug[off:off + 64, pp, :], up[off:off + 64, hh, :], gw[off:off + 64, 28 + hc:29 + hc],
                                            ctmp[off:off + 64, :], ALU.mult, ALU.add), reads=[up, gw, ctmp], writes=[caug])
                    P.op("act", ACT(hw[:, 0:4], op_[:, :, 64], AF.Abs), reads=[op_], writes=[hw])
                    P.op("dve", TS(hw[:, 0:4], hw[:, 0:4], 1.0, None, ALU.max), reads=[hw], writes=[hw])
                    P.op("dve", RCP(hw[:, 4:8], hw[:, 0:4]), reads=[hw], writes=[hw])
                    if dr == 0:
                        P.op("dve", TTo(hf[:, i, :].rearrange("p (h e) -> p h e", e=64), op_[:, :, 0:64],
                                        hw[:, 4:8].unsqueeze(2).broadcast_to([128, 4, 64]), ALU.mult), reads=[op_, hw], writes=[hf])
                        continue
                    P.op("dve", TTo(ht[:], op_[:, :, 0:64], hw[:, 4:8].unsqueeze(2).broadcast_to([128, 4, 64]), ALU.mult),
                         reads=[op_, hw], writes=[ht])
                    if (not lat) and (not ctx_out):
                        continue
                    opre, sg, yc = opres[it % 2], sgs[it % 2], ycs[it % 2]
                    P.dma("sp", DMA(opre[:], pd[r0:r0 + 128, C_O:C_O + 256]), writes=[opre])
                    P.op("act", ACT(sg[:], opre[:], AF.Sigmoid), reads=[opre], writes=[sg])
                    P.op("dve", TTo(ht[:], ht[:], hf[:, i, :].rearrange("p (h e) -> p h e", e=64), ALU.add), reads=[ht, hf], writes=[ht])
                    P.op("pool", TTo(h2[:], ht[:], ht[:], ALU.mult), reads=[ht], writes=[h2])
                    P.op("dve", RED(hw[:, 8:12], h2[:]), reads=[h2], writes=[hw])
                    P.op("dve", TS(hw[:, 8:12], hw[:, 8:12], 1.0 / 64, EPS, ALU.mult, ALU.add), reads=[hw], writes=[hw])
                    P.op("act", ACT(hw[:, 12:16], hw[:, 8:12], AF.Sqrt), reads=[hw], writes=[hw])
                    P.op("dve", RCP(hw[:, 8:12], hw[:, 12:16]), reads=[hw], writes=[hw])
                    P.op("dve", TTo(ht[:], ht[:], hw[:, 8:12].unsqueeze(2).broadcast_to([128, 4, 64]), ALU.mult), reads=[ht, hw], writes=[ht])
                    P.op("pool", TTo(ht[:], ht[:], mlg[:].unsqueeze(1).broadcast_to([128, 4, 64]), ALU.mult), reads=[ht, mlg], writes=[ht])
                    P.op("dve", TTo(yc[:], ht[:].rearrange("p h e -> p (h e)"), sg[:], ALU.mult), reads=[ht, sg], writes=[yc])
                    ydst = d["ycL%d" % b] if lat else d["ycC%d" % b]
                    P.dma("sp", DMA(ydst[r0:r0 + 128, 512:768], yc[:]), reads=[yc], writes=[Buf()])
        P.barrier()


HY_LS = (("L", T), ("C", S))
TWO_PI = 2.0 * math.pi


def phase_hy_filter(k, l, tag, L):
    P = k.P
    d = k.dram
    nt = L // 128
    cw = min(512, L)
    ks = k.dt("KS" + tag, [L, 2, 512], F32)
    with contextlib.ExitStack() as st:
        fw1 = k.sb(st, [33, 64], F32)
        fw2 = k.sb(st, [64, 64], F32)
        fw3 = k.sb(st, [64, 1024], F32)
        P.dma("sp", DMA(fw1[:], d["hy_fw1"][l]), writes=[fw1])
        P.dma("sp", DMA(fw2[:], d["hy_fw2"][l]), writes=[fw2])
        P.dma("sp", DMA(fw3[:], d["hy_fw3"][l]), writes=[fw3])
        col = k.sb(st, [64, 8], F32)
        with k.nc.allow_non_contiguous_dma(reason="tiny per-partition vectors"):
            for i, nm in enumerate(("hy_freq", "hy_fb1", "hy_fb2")):
                P.dma("sp", DMA(col[:, i:i + 1], d[nm][l].rearrange("(p o) -> p o", o=1)), writes=[col])
        P.op("dve", TS(col[:, 0:1], col[:, 0:1], 1.0 / TWO_PI, None, ALU.mult), reads=[col], writes=[col])
        P.op("dve", TTo(col[:, 3:4], col[:, 0:1], col[:, 1:2], ALU.mult), reads=[col], writes=[col])
        P.op("dve", TTo(col[:, 4:5], col[:, 0:1], col[:, 2:3], ALU.mult), reads=[col], writes=[col])
        ones = k.sb(st, [128, 128], BF16)
        P.op("pool", MSET(ones[:], 1.0), writes=[ones])
        alt = k.sb(st, [128, nt], BF16)
        P.dma("sp", DMA(alt[:], d["alt" + tag]), writes=[alt])
        h2T = k.sb(st, [64, L], F32, "h2T")
        sumT = k.sb(st, [128, nt, 512], BF16, "sumT")
        difT = k.sb(st, [128, nt, 512], BF16, "difT")
        zts = [k.sb(st, [33, cw], F32, "zt") for _ in range(2)]
        a1s = [k.sb(st, [64, cw], F32, "a1") for _ in range(2)]
        h1s = [k.sb(st, [64, cw], F32, "h1") for _ in range(2)]
        psm = [k.ps(st, [128, 512], F32, "psm") for _ in range(4)]
        nrm = k.ps(st, [128, 512], F32, "nrm")
        nyq = k.ps(st, [128, 512], F32, "nyq")
        ip = 0
        for c in range(L // cw):
            zt, a1, h1 = zts[c % 2], a1s[c % 2], h1s[c % 2]
            P.dma("sp", DMA(zt[:], d["zT" + tag][:, c * cw:(c + 1) * cw]), writes=[zt])
            for (w_, src_, dst_, bcol) in ((fw1, zt, h1, 3), (fw2, h1, None, 4)):
                ps = psm[ip % 4]
                ip += 1
                kk = 33 if w_ is fw1 else 64
                P.op("pe", MM(ps[0:64, 0:cw], w_[0:kk, :], src_[0:kk, :]), reads=[w_, src_], writes=[ps])
                P.op("dve", TS(a1[:], ps[0:64, 0:cw], col[:, 0:1], col[:, bcol:bcol + 1], ALU.mult, ALU.add), reads=[ps, col], writes=[a1])
                for _ in range(2):
                    P.op("dve", STT(a1[:], a1[:], -0.5, a1[:], ALU.is_lt, ALU.add), reads=[a1], writes=[a1])
                    P.op("dve", STT(a1[:], a1[:], 0.5, a1[:], ALU.is_gt, ALU.subtract), reads=[a1], writes=[a1])
                sc_ = TWO_PI * (1.0 - 1e-6)
                if dst_ is not None:
                    P.op("act", ACT(dst_[:], a1[:], AF.Sin, scale=sc_), reads=[a1], writes=[dst_])
                else:
                    P.op("act", ACT(h2T[:, c * cw:(c + 1) * cw], a1[:], AF.Sin, scale=sc_), reads=[a1], writes=[h2T])
        decs = [k.sb(st, [128, 256], F32, "dec") for _ in range(2)]
        hds = [k.sb(st, [128, 2, 512], F32, "hd") for _ in range(2)]
        sqs = [k.sb(st, [128, 2, 512], BF16, "sqh") for _ in range(2)]
        for lt in range(nt):
            dec, hd, sq = decs[lt % 2], hds[lt % 2], sqs[lt % 2]
            P.dma("sp", DMA(dec[:], d["dec" + tag][lt * 128:(lt + 1) * 128, :]), writes=[dec])
            for dr in range(2):
                ps = psm[ip % 4]
                ip += 1
                P.op("pe", MM(ps[:], h2T[:, lt * 128:(lt + 1) * 128], fw3[:, dr * 512:(dr + 1) * 512]), reads=[h2T, fw3], writes=[ps])
                P.op("dve", TTo(hd[:, dr, :].rearrange("p (o c) -> p o c", o=2), ps[:].rearrange("p (o c) -> p o c", o=2),
                                dec[:].unsqueeze(1).broadcast_to([128, 2, 256]), ALU.mult), reads=[ps, dec], writes=[hd])
            if lt == 0:
                P.op("dve", MSET(hd[0:1, 1, :], 0.0), writes=[hd])
            P.op("dve", TTo(sumT[:, lt, :], hd[:, 0, :], hd[:, 1, :], ALU.add), reads=[hd], writes=[sumT])
            P.op("pool", TTo(difT[:, lt, :], hd[:, 0, :], hd[:, 1, :], ALU.subtract), reads=[hd], writes=[difT])
            P.op("pool", TTo(sq[:], hd[:], hd[:], ALU.mult), reads=[hd], writes=[sq])
            for dr in range(2):
                P.op("pe", MM(nrm[:], ones[:], sq[:, dr, :], start=(lt == 0 and dr == 0), stop=True), reads=[ones, sq], writes=[nrm], sig=(dr == 1))
            P.op("pe", MM(nyq[0:1, :], alt[:, lt:lt + 1], sumT[:, lt, :], start=(lt == 0), stop=True), reads=[alt, sumT], writes=[nyq])
        rn = k.sb(st, [128, 512], F32, "rn")
        rn2 = k.sb(st, [128, 512], F32, "rn2")
        nq = k.sb(st, [1, 512], F32, "nq")
        P.op("dve", TS(rn[:], nrm[:], EPS, None, ALU.add), reads=[nrm], writes=[rn])
        P.op("act", ACT(rn2[:], rn[:], AF.Sqrt), reads=[rn], writes=[rn2])
        P.op("dve", RCP(rn[:], rn2[:]), reads=[rn2], writes=[rn])
        P.op("dve", TS(rn[:], rn[:], 1.0 / L, None, ALU.mult), reads=[rn], writes=[rn])
        P.op("dve", CP(nq[:], nyq[0:1, :]), reads=[nyq], writes=[nq])
        cbs = [k.sb(st, [128, nt, 128], BF16, "cb") for _ in range(2)]
        sbs = [k.sb(st, [128, nt, 128], BF16, "sb") for _ in range(2)]
        kos = [k.sb(st, [128, 2, 512], F32, "ko") for _ in range(2)]
        for ft in range(nt):
            cb, sb_, ko = cbs[ft % 2], sbs[ft % 2], kos[ft % 2]
            P.dma("sp", DMA(cb[:], d["Cm" + tag][ft]), writes=[cb])
            P.dma("sp", DMA(sb_[:], d["Sm" + tag][ft]), writes=[sb_])
            pr, pi = psm[(2 * ft) % 4], psm[(2 * ft + 1) % 4]
            for s_ in range(nt):
                P.op("pe", MM(pr[:], cb[:, s_, :], sumT[:, s_, :], start=(s_ == 0), stop=(s_ == nt - 1)), reads=[cb, sumT], writes=[pr], sig=(s_ == nt - 1))
            for s_ in range(nt):
                P.op("pe", MM(pi[:], sb_[:, s_, :], difT[:, s_, :], start=(s_ == 0), stop=(s_ == nt - 1)), reads=[sb_, difT], writes=[pi], sig=(s_ == nt - 1))
            P.op("dve", TTo(ko[:, 0, :], pr[:], rn[:], ALU.mult), reads=[pr, rn], writes=[ko])
            P.op("dve", TTo(ko[:, 1, :], pi[:], rn[:], ALU.mult), reads=[pi, rn], writes=[ko])
            if ft == 0:
                P.op("dve", TS(ko[0:1, 0, :], ko[0:1, 0, :], 0.5, None, ALU.mult), reads=[ko], writes=[ko])
                P.op("dve", STT(ko[0:1, 1, :], nq[0:1, :], 0.5, rn[0:1, :], ALU.mult, ALU.mult), reads=[nq, rn], writes=[ko])
            P.dma("sp", DMA(ks[ft * 128:(ft + 1) * 128], ko[:]), reads=[ko], writes=[Buf()])
        P.barrier()


def phase_hy_signal(k, l, tag, L):
    P = k.P
    d = k.dram
    nt = L // 128
    ks = k.dt("KS" + tag, [L, 2, 512], F32)
    hg = k.dt("HG" + tag, [L, 2, 512], F32)
    with contextlib.ExitStack() as st:
        wc = k.sb(st, [128, 3, 768], F32, "wc")
        for j in range(3):
            P.dma("sp", DMA(wc[:, j, :], d["hy_conv"][l, j:j + 1, :].broadcast_to([128, 768])), writes=[wc])
        hb = k.sb(st, [128, 2, 256], F32, "hbias")
        for n in range(2):
            P.dma("sp", DMA(hb[:, n, :], d["hy_bias"][l, n:n + 1, :].broadcast_to([128, 256])), writes=[hb])
        z = k.sb(st, [128, nt, 512], BF16, "z")
        zbufs = [Buf() for _ in range(nt)]
        yr = k.sb(st, [128, nt, 512], BF16, "yr")
        yi = k.sb(st, [128, nt, 512], BF16, "yi")
        uas = [k.sb(st, [128, 3, 768], BF16, "ua") for _ in range(2)]
        ucs = [k.sb(st, [128, 768], F32, "uc") for _ in range(2)]
        uts = [k.sb(st, [128, 768], F32, "ut") for _ in range(2)]
        ggs = [k.sb(st, [128, 2, 512], F32, "gg") for _ in range(2)]
        it = 0
        for i in range(nt):
            gg = ggs[i % 2]
            for b in range(NB):
                pd = d["p%s%d" % (tag, b)]
                ua, uc, ut = uas[it % 2], ucs[it % 2], uts[it % 2]
                it += 1
                r0 = i * 128
                if i == 0 or i == nt - 1:
                    P.op("pool", MSET(ua[:], 0.0), writes=[ua])
                if i == 0:
                    P.dma("sp", DMA(ua[1:128, 0, :], pd[0:127, B_U:B_U + 768]), writes=[ua])
                else:
                    P.dma("sp", DMA(ua[:, 0, :], pd[r0 - 1:r0 + 127, B_U:B_U + 768]), writes=[ua])
                P.dma("sp", DMA(ua[:, 1, :], pd[r0:r0 + 128, B_U:B_U + 768]), writes=[ua])
                if i == nt - 1:
                    P.dma("sp", DMA(ua[0:127, 2, :], pd[r0 + 1:r0 + 128, B_U:B_U + 768]), writes=[ua])
                else:
                    P.dma("sp", DMA(ua[:, 2, :], pd[r0 + 1:r0 + 129, B_U:B_U + 768]), writes=[ua])
                P.op("dve", TTo(uc[:], ua[:, 0, :], wc[:, 0, :], ALU.mult), reads=[ua, wc], writes=[uc])
                P.op("pool", TTo(ut[:], ua[:, 1, :], wc[:, 1, :], ALU.mult), reads=[ua, wc], writes=[ut])
                P.op("dve", TTo(uc[:], uc[:], ut[:], ALU.add), reads=[uc, ut], writes=[uc])
                P.op("pool", TTo(ut[:], ua[:, 2, :], wc[:, 2, :], ALU.mult), reads=[ua, wc], writes=[ut])
                P.op("dve", TTo(z[:, i, b * 256:(b + 1) * 256], uc[:, 0:256], ut[:, 0:256], ALU.add), reads=[uc, ut], writes=[zbufs[i]])
                P.op("dve", TTo(gg[:, :, b * 256:(b + 1) * 256], uc[:, 256:768].rearrange("p (n c) -> p n c", n=2),
                                ut[:, 256:768].rearrange("p (n c) -> p n c", n=2), ALU.add), reads=[uc, ut], writes=[gg])
            P.dma("sp", DMA(hg[i * 128:(i + 1) * 128], gg[:]), reads=[gg], writes=[Buf()])
        P.barrier()
        cbs = [k.sb(st, [128, nt, 128], BF16, "cb") for _ in range(2)]
        sbs = [k.sb(st, [128, nt, 128], BF16, "sb") for _ in range(2)]
        kts = [k.sb(st, [128, 2, 256], F32, "kt") for _ in range(2)]
        t1s = [k.sb(st, [128, 512], F32, "t1") for _ in range(2)]
        t2s = [k.sb(st, [128, 512], F32, "t2") for _ in range(2)]
        gts = [k.sb(st, [128, 512], F32, "gt") for _ in range(2)]
        ybs = [k.sb(st, [128, 512], BF16, "yb") for _ in range(2)]
        psm = [k.ps(st, [128, 512], F32, "psh") for _ in range(4)]
        yrb, yib = Buf(), Buf()
        for n in range(2):
            for ft in range(nt):
                cb, sb_, kt, t1, t2 = cbs[ft % 2], sbs[ft % 2], kts[ft % 2], t1s[ft % 2], t2s[ft % 2]
                P.dma("sp", DMA(cb[:], d["Cm" + tag][ft]), writes=[cb])
                P.dma("sp", DMA(sb_[:], d["Sm" + tag][ft]), writes=[sb_])
                P.dma("sp", DMA(kt[:], ks[ft * 128:(ft + 1) * 128, :, n * 256:(n + 1) * 256]), writes=[kt])
                pr, pi = psm[(2 * ft) % 4], psm[(2 * ft + 1) % 4]
                for s_ in range(nt):
                    P.op("pe", MM(pr[:], cb[:, s_, :], z[:, s_, :], start=(s_ == 0), stop=(s_ == nt - 1)), reads=[cb, zbufs[s_]], writes=[pr], sig=(s_ == nt - 1))
                for s_ in range(nt):
                    P.op("pe", MM(pi[:], sb_[:, s_, :], z[:, s_, :], start=(s_ == 0), stop=(s_ == nt - 1)), reads=[sb_, zbufs[s_]], writes=[pi], sig=(s_ == nt - 1))
                kr = kt[:, 0, :].unsqueeze(1).broadcast_to([128, 2, 256])
                ki = kt[:, 1, :].unsqueeze(1).broadcast_to([128, 2, 256])
                v3 = lambda a: a.rearrange("p (b c) -> p b c", b=2)
                P.op("dve", TTo(v3(t1[:]), v3(pr[:]), kr, ALU.mult), reads=[pr, kt], writes=[t1])
                P.op("dve", TTo(v3(t2[:]), v3(pi[:]), ki, ALU.mult), reads=[pi, kt], writes=[t2])
                P.op("pool", TTo(yr[:, ft, :], t1[:], t2[:], ALU.subtract), reads=[t1, t2], writes=[yrb])
                t3, t4 = t1s[(ft + 1) % 2], t2s[(ft + 1) % 2]
                P.op("dve", TTo(v3(t3[:]), v3(pr[:]), ki, ALU.mult), reads=[pr, kt], writes=[t3])
                P.op("dve", TTo(v3(t4[:]), v3(pi[:]), kr, ALU.mult), reads=[pi, kt], writes=[t4])
                P.op("pool", TTo(yi[:, ft, :], t3[:], t4[:], ALU.add), reads=[t3, t4], writes=[yib])
                if ft == 0:
                    P.op("pool", CP(yr[0:1, 0, :], t1[0:1, :]), reads=[t1], writes=[yrb])
                    P.op("pool", CP(yi[0:1, 0, :], t2[0:1, :]), reads=[t2], writes=[yib])
            for tt in range(nt):
                cb, sb_, gt, t1, yb = cbs[tt % 2], sbs[tt % 2], gts[tt % 2], t1s[tt % 2], ybs[tt % 2]
                P.dma("sp", DMA(cb[:], d["Cm" + tag][tt]), writes=[cb])
                P.dma("sp", DMA(sb_[:], d["Si" + tag][tt]), writes=[sb_])
                P.dma("sp", DMA(gt[:], hg[tt * 128:(tt + 1) * 128, n, :]), writes=[gt])
                py = psm[tt % 4]
                for f_ in range(nt):
                    P.op("pe", MM(py[:], cb[:, f_, :], yr[:, f_, :], start=(f_ == 0), stop=False), reads=[cb, yrb], writes=[py], sig=False)
                for f_ in range(nt):
                    P.op("pe", MM(py[:], sb_[:, f_, :], yi[:, f_, :], start=False, stop=(f_ == nt - 1)), reads=[sb_, yib], writes=[py], sig=(f_ == nt - 1))
                v3 = lambda a: a.rearrange("p (b c) -> p b c", b=2)
                P.op("dve", TTo(v3(t1[:]), v3(z[:, tt, :]), hb[:, n, :].unsqueeze(1).broadcast_to([128, 2, 256]), ALU.mult),
                     reads=[zbufs[tt], hb], writes=[t1])
                P.op("dve", TTo(t1[:], t1[:], py[:], ALU.add), reads=[t1, py], writes=[t1])
                if n == 0:
                    P.op("dve", TTo(z[:, tt, :], t1[:], gt[:], ALU.mult), reads=[t1, gt], writes=[zbufs[tt]])
                else:
                    P.op("dve", TTo(yb[:], t1[:], gt[:], ALU.mult), reads=[t1, gt], writes=[yb])
                    for b in range(NB):
                        P.dma("sp", DMA(d["yc%s%d" % (tag, b)][tt * 128:(tt + 1) * 128, 256:512], yb[:, b * 256:(b + 1) * 256]),
                              reads=[yb], writes=[Buf()])
        P.barrier()


SP = S + 128


def phase_out_router(k, l, ctx_out):
    P = k.P
    d = k.dram
    with contextlib.ExitStack() as st:
        ident = k.sb(st, [128, 128], BF16)
        P.dma("sp", DMA(ident[:], d["ident_bf"]), writes=[ident])
        identf = k.sb(st, [128, 128], F32)
        P.dma("sp", DMA(identf[:], d["ident_f"]), writes=[identf])
        wout = k.sb(st, [128, 8, D], BF16, "wout")
        wob = [Buf(), Buf()]
        for c in range(2):
            src = d["w_out"][l, :, c * 512:(c + 1) * 512].rearrange("(k p) n -> p k n", p=128)
            P.dma("pool", DMA(wout[:, :, c * 512:(c + 1) * 512], src), writes=[wob[c]])
        wr = k.sb(st, [128, 8, NE], BF16, "wr")
        P.dma("pool", DMA(wr[:], d["w_router"][l].rearrange("(k p) n -> p k n", p=128)), writes=[wr])
        g2 = k.sb(st, [128, D], F32)
        load_bcast(k, P, g2, d["norm2_g"][l:l + 1, :])
        gt1 = k.sb(st, [128, D], F32)
        A2 = k.sb(st, [128, D], F32)
        sh2 = k.sb(st, [128, D], F32)
        sc2 = k.sb(st, [128, D], F32)
        ycs = [k.sb(st, [128, D], BF16, "ycin") for _ in range(3)]
        yTs = [k.sb(st, [128, 8, 128], BF16, "yT") for _ in range(2)]
        xts = [k.sb(st, [128, D], F32, "xo") for _ in range(3)]
        tms = [k.sb(st, [128, D], F32, "tm") for _ in range(2)]
        hbs = [k.sb(st, [128, D], BF16, "h2b") for _ in range(2)]
        hTs = [k.sb(st, [128, 8, 128], BF16, "h2T") for _ in range(2)]
        sss = [k.sb(st, [128, 4], F32, "ss2") for _ in range(3)]
        junk = k.sb(st, [128, D], BF16, "junk2")
        lgs = [k.sb(st, [128, 40], F32, "lg") for _ in range(2)]
        affs = [k.sb(st, [16, 512], F32, "affs") for _ in range(2)]
        tps = [k.ps(st, [128, 8, 128], BF16, "tpo") for _ in range(2)]
        accs = [k.ps(st, [128, 512], F32, "acco") for _ in range(3)]
        lps = [k.ps(st, [128, 16], F32, "lpo") for _ in range(1)]
        aps_ = [k.ps(st, [16, 512], F32, "apo") for _ in range(2)]
        it = 0
        ia = 0
        for (tag, ntok, xs, mrow) in streams(k):
            if tag[0] == "C" and not ctx_out:
                continue
            load_bcast(k, P, gt1, d["mod"][l, mrow:mrow + 1, 2 * D:3 * D])
            load_bcast(k, P, sc2, d["mod"][l, mrow:mrow + 1, 4 * D:5 * D])
            load_bcast(k, P, sh2, d["mod"][l, mrow:mrow + 1, 3 * D:4 * D])
            P.op("dve", STT(A2[:], sc2[:], 1.0, g2[:], ALU.add, ALU.mult), reads=[sc2, g2], writes=[A2])
            yc_d = d["yc" + tag]
            h2_d = d["h2" + tag]
            aff_d = d["aff" + tag]
            gsz = min(4, ntok // 128)
            for i in range(ntok // 128):
                yc, yT, xt, tm, hb, hT, ss, tp, lg = (ycs[it % 3], yTs[it % 2], xts[it % 3], tms[it % 2], hbs[it % 2],
                                                      hTs[it % 2], sss[it % 3], tps[it % 2], lgs[it % 2])
                it += 1
                r0 = i * 128
                P.dma("sp", DMA(yc[:], yc_d[r0:r0 + 128, :]), writes=[yc])
                P.dma("sp", DMA(xt[:], xs[r0:r0 + 128, :]), writes=[xt])
                for kk in range(8):
                    P.op("pe", TR(tp[:, kk, :], yc[:, kk * 128:(kk + 1) * 128], ident[:]), reads=[yc, ident], writes=[tp], sig=(kk == 7))
                P.op("act", ACT(yT[:], tp[:], AF.Copy), reads=[tp], writes=[yT])
                for c in range(2):
                    acc = accs[ia % 3]
                    ia += 1
                    for kk in range(8):
                        P.op("pe", MM(acc[:], yT[:, kk, :], wout[:, kk, c * 512:(c + 1) * 512], start=(kk == 0), stop=(kk == 7)),
                             reads=[yT, wob[c]], writes=[acc], sig=(kk == 7))
                    P.op("dve", TTo(tm[:, c * 512:(c + 1) * 512], acc[:], gt1[:, c * 512:(c + 1) * 512], ALU.mult), reads=[acc, gt1], writes=[tm])
                P.op("pool", TTo(xt[:], xt[:], tm[:], ALU.add), reads=[xt, tm], writes=[xt])
                P.dma("sp", DMA(xs[r0:r0 + 128, :], xt[:]), reads=[xt], writes=[Buf()])
                P.op("act", ACT(junk[:], xt[:], AF.Square, accum_out=ss[:, 0:1]), reads=[xt], writes=[junk, ss])
                P.op("dve", TS(ss[:, 1:2], ss[:, 0:1], 1.0 / D, EPS, ALU.mult, ALU.add), reads=[ss], writes=[ss])
                P.op("act", ACT(ss[:, 2:3], ss[:, 1:2], AF.Sqrt), reads=[ss], writes=[ss])
                P.op("dve", RCP(ss[:, 3:4], ss[:, 2:3]), reads=[ss], writes=[ss])
                P.op("dve", STT(tm[:], xt[:], ss[:, 3:4], A2[:], ALU.mult, ALU.mult), reads=[xt, ss, A2], writes=[tm])
                P.op("pool", TTo(hb[:], tm[:], sh2[:], ALU.add), reads=[tm, sh2], writes=[hb])
                P.dma("sp", DMA(h2_d[r0:r0 + 128, :], hb[:]), reads=[hb], writes=[Buf()])
                for kk in range(8):
                    P.op("pe", TR(tp[:, kk, :], hb[:, kk * 128:(kk + 1) * 128], ident[:]), reads=[hb, ident], writes=[tp], sig=(kk == 7))
                P.op("act", ACT(hT[:], tp[:], AF.Copy), reads=[tp], writes=[hT])
                lp = lps[0]
                for kk in range(8):
                    P.op("pe", MM(lp[:], hT[:, kk, :], wr[:, kk, :], start=(kk == 0), stop=(kk == 7)), reads=[hT, wr], writes=[lp], sig=(kk == 7))
                P.op("dve", RED(lg[:, 32:33], lp[:], op=ALU.max), reads=[lp], writes=[lg])
                P.op("dve", TS(lg[:, 33:34], lg[:, 32:33], -1.0, None, ALU.mult), reads=[lg], writes=[lg])
                P.op("act", ACT(lg[:, 0:16], lp[:], AF.Exp, bias=lg[:, 33:34], accum_out=lg[:, 34:35]), reads=[lp, lg], writes=[lg])
                P.op("dve", RCP(lg[:, 35:36], lg[:, 34:35]), reads=[lg], writes=[lg])
                P.op("dve", TS(lg[:, 16:32], lg[:, 0:16], lg[:, 35:36], None, ALU.mult), reads=[lg], writes=[lg])
                gi = i % gsz
                ap_ = aps_[(i // gsz) % 2]
                af = affs[(i // gsz) % 2]
                P.op("pe", TR(ap_[:, gi * 128:(gi + 1) * 128], lg[:, 16:32], identf[:]), reads=[lg, identf], writes=[ap_])
                if gi == gsz - 1:
                    n_ = gsz * 128
                    c0 = (i // gsz) * n_
                    P.op("act", ACT(af[:, 0:n_], ap_[:, 0:n_], AF.Copy), reads=[ap_], writes=[af])
                    P.dma("sp", DMA(aff_d[:, c0:c0 + n_], af[:, 0:n_]), reads=[af], writes=[Buf()])
        P.barrier()


def phase_topk(k, l, tag, ntok, cap):
    P = k.P
    d = k.dram
    nti = max(1, cap // 128)
    rows = min(cap, 128)
    with contextlib.ExitStack() as st:
        identf = k.sb(st, [128, 128], F32)
        P.dma("sp", DMA(identf[:], d["ident_f"]), writes=[identf])
        work = k.sb(st, [64, ntok], F32, "tkw")
        P.op("pool", MSET(work[:], 0.0), writes=[work])
        for b in range(NB):
            P.dma("sp", DMA(work[32 * b:32 * b + 16, :], d["aff%s%d" % (tag, b)]), writes=[work])
        vals = k.sb(st, [64, cap], F32, "tkv")
        idx = k.sb(st, [64, cap], U32, "tki")
        idf = k.sb(st, [64, cap], F32, "tkif")
        for r in range(cap // 8):
            sl = slice(r * 8, (r + 1) * 8)
            P.op("dve", (lambda o, i_: (lambda e: e.max(out=o, in_=i_)))(vals[:, sl], work[:]), reads=[work], writes=[vals])
            P.op("dve", (lambda o, m, i_: (lambda e: e.max_index(out=o, in_max=m, in_values=i_)))(idx[:, sl], vals[:, sl], work[:]),
                 reads=[work, vals], writes=[idx])
            P.op("dve", (lambda o, m, i_: (lambda e: e.match_replace(out=o, in_to_replace=m, in_values=i_, imm_value=-1.0)))(work[:], vals[:, sl], work[:]),
                 reads=[vals, work], writes=[work])
        P.op("dve", CP(idf[:], idx[:]), reads=[idx], writes=[idf])
        idT = k.sb(st, [128, nti, 64], I32, "idT")
        gT = k.sb(st, [128, nti, 64], F32, "gT")
        if rows < 128:
            P.op("pool", (lambda o: (lambda e: e.iota(o, pattern=[[0, 64]], base=S, channel_multiplier=1)))(idT[:, 0, :]), writes=[idT])
            P.op("pool", MSET(gT[:], 0.0), writes=[gT])
        tp = k.ps(st, [128, 2, 64], F32, "tkp")
        for j in range(nti):
            P.op("pe", TR(tp[0:rows, 0, :], idf[:, j * rows:(j + 1) * rows], identf[0:64, 0:64]), reads=[idf, identf], writes=[tp], sig=False)
            P.op("pe", TR(tp[0:rows, 1, :], vals[:, j * rows:(j + 1) * rows], identf[0:64, 0:64]), reads=[vals, identf], writes=[tp])
            P.op("dve", CP(idT[0:rows, j, :], tp[0:rows, 0, :]), reads=[tp], writes=[idT])
            P.op("dve", CP(gT[0:rows, j, :], tp[0:rows, 1, :]), reads=[tp], writes=[gT])
        P.dma("sp", DMA(d["idx" + tag], idT[:]), reads=[idT], writes=[Buf()])
        P.dma("sp", DMA(d["gate" + tag], gT[:]), reads=[gT], writes=[Buf()])
        P.barrier()


def phase_moe(k, l, ctx_out):
    P = k.P
    d = k.dram
    with contextlib.ExitStack() as st:
        ident = k.sb(st, [128, 128], BF16)
        P.dma("sp", DMA(ident[:], d["ident_bf"]), writes=[ident])
        idxL = k.sb(st, [128, 4, 64], I32, "idxL")
        gateL = k.sb(st, [128, 4, 64], F32, "gateL")
        idxC = k.sb(st, [128, 1, 64], I32, "idxC")
        gateC = k.sb(st, [128, 1, 64], F32, "gateC")
        P.dma("sp", DMA(idxL[:], d["idxL"]), writes=[idxL])
        P.dma("sp", DMA(gateL[:], d["gateL"]), writes=[gateL])
        if ctx_out:
            P.dma("sp", DMA(idxC[:], d["idxC"]), writes=[idxC])
            P.dma("sp", DMA(gateC[:], d["gateC"]), writes=[gateC])
        gt2 = [k.sb(st, [128, D], F32, "gt2") for _ in range(3)]
        for r in range(3):
            load_bcast(k, P, gt2[r], d["mod"][l, r:r + 1, 5 * D:6 * D])
        wgs = [k.sb(st, [128, 8, D], BF16, "wg") for _ in range(2)]
        wus = [k.sb(st, [128, 8, D], BF16, "wu") for _ in range(2)]
        wds = [k.sb(st, [128, 8, D], BF16, "wd") for _ in range(2)]
        xes = [k.sb(st, [128, D], BF16, "xe") for _ in range(4)]
        xeTs = [k.sb(st, [128, 8, 512], BF16, "xeT") for _ in range(2)]
        actT = k.sb(st, [128, 8, 512], BF16, "actT")
        sils = [k.sb(st, [128, 512], F32, "sil") for _ in range(2)]
        yvs = [k.sb(st, [128, D], F32, "yv") for _ in range(2)]
        tps = [k.ps(st, [128, 8, 128], BF16, "tpm") for _ in range(2)]
        pas = [k.ps(st, [128, 512], F32, "pa") for _ in range(2)]
        pus = [k.ps(st, [128, 512], F32, "pu") for _ in range(2)]
        pys = [k.ps(st, [128, 512], F32, "py") for _ in range(2)]
        G = {}
        iu = 0
        ixe = 0
        iy = 0
        ifc = 0
        for e in range(NE):
            wg, wu, wd = wgs[e % 2], wus[e % 2], wds[e % 2]
            for (w_, nm) in ((wg, "w_e_gate"), (wu, "w_e_up"), (wd, "w_e_down")):
                P.dma("pool", DMA(w_[:], d[nm][l, e].rearrange("(k p) n -> p k n", p=128)), writes=[w_])
            units = [("L", b) for b in range(NB)] + ([("C", b) for b in range(NB)] if ctx_out else [])
            for (tg, b) in units:
                lat = tg == "L"
                nti = 4 if lat else 1
                ntk = nti * 128
                col = 32 * b + e
                idx, gate = (idxL, gateL) if lat else (idxC, gateC)
                h2d = d["h2%s%d" % (tg, b)]
                xd = d["y"].rearrange("b t c -> (b t) c") if lat else d["ctxs"].rearrange("b t c -> (b t) c")
                xoff = b * (T if lat else SP) * D
                mrow = b if lat else 2
                xeT = xeTs[iu % 2]
                iu += 1
                for j in range(nti):
                    xe = xes[ixe % 4]
                    tp = tps[ixe % 2]
                    ixe += 1
                    P.dma("pool", (lambda o, i_, off: (lambda en: en.indirect_dma_start(
                        out=o, out_offset=None, in_=i_, in_offset=bass.IndirectOffsetOnAxis(ap=off, axis=0))))(
                        xe[:], h2d[:, :], idx[:, j, col:col + 1]), reads=[idx], writes=[xe])
                    for kk in range(8):
                        P.op("pe", TR(tp[:, kk, :], xe[:, kk * 128:(kk + 1) * 128], ident[:]), reads=[xe, ident], writes=[tp], sig=(kk == 7))
                    P.op("act", ACT(xeT[:, :, j * 128:(j + 1) * 128], tp[:], AF.Copy), reads=[tp], writes=[xeT])
                for fc in range(8):
                    pa, pu, sil = pas[ifc % 2], pus[ifc % 2], sils[ifc % 2]
                    ifc += 1
                    for kk in range(8):
                        P.op("pe", MM(pa[:, 0:ntk], wg[:, kk, fc * 128:(fc + 1) * 128], xeT[:, kk, 0:ntk], start=(kk == 0), stop=(kk == 7)),
                             reads=[wg, xeT], writes=[pa], sig=(kk == 7))
                    for kk in range(8):
                        P.op("pe", MM(pu[:, 0:ntk], wu[:, kk, fc * 128:(fc + 1) * 128], xeT[:, kk, 0:ntk], start=(kk == 0), stop=(kk == 7)),
                             reads=[wu, xeT], writes=[pu], sig=(kk == 7))
                    P.op("act", ACT(sil[:, 0:ntk], pa[:, 0:ntk], AF.Silu), reads=[pa], writes=[sil])
                    P.op("dve", TTo(actT[:, fc, 0:ntk], sil[:, 0:ntk], pu[:, 0:ntk], ALU.mult), reads=[sil, pu], writes=[actT])
                gkey = (tg, b)
                if gkey not in G:
                    G[gkey] = Buf()
                P.op("pool", lambda en: en.nop(), writes=[G[gkey]])
                for j in range(nti):
                    yv = yvs[iy % 2]
                    for dc in range(2):
                        py = pys[(2 * iy + dc) % 2]
                        for fc in range(8):
                            P.op("pe", MM(py[:], actT[:, fc, j * 128:(j + 1) * 128], wd[:, fc, dc * 512:(dc + 1) * 512], start=(fc == 0), stop=(fc == 7)),
                                 reads=[actT, wd], writes=[py], sig=(fc == 7))
                        P.op("act", ACT(yv[:, dc * 512:(dc + 1) * 512], py[:], AF.Copy, scale=gate[:, j, col:col + 1]), reads=[py, gate], writes=[yv])
                    iy += 1
                    P.op("dve", TTo(yv[:], yv[:], gt2[mrow][:], ALU.mult), reads=[yv, gt2[mrow]], writes=[yv])
                    P.dma("pool", (lambda o, i_, off, eo=xoff: (lambda en: en.indirect_dma_start(
                        out=o, out_offset=bass.IndirectOffsetOnAxis(ap=off, axis=0), in_=i_, in_offset=None, compute_op=ALU.add,
                        element_offset=eo)))(
                        xd, yv[:], idx[:, j, col:col + 1]), reads=[yv, idx, G[gkey]], writes=[])
        P.barrier()


def emit_layers(k, layers):
    phase_init(k)
    phase_ada(k, layers)
    for l in layers:
        ctx_out = l != DEPTH - 1
        phase_proj(k, l)
        phase_mixA(k, l, ctx_out)
        for tag, L in HY_LS:
            if tag == "C" and not ctx_out:
                continue
            phase_hy_filter(k, l, tag, L)
            phase_hy_signal(k, l, tag, L)
        phase_mixC(k, l, ctx_out)
        phase_mixD(k, l, ctx_out)
        phase_out_router(k, l, ctx_out)
        phase_topk(k, l, "L", T, 512)
        if ctx_out:
            phase_topk(k, l, "C", S, 32)
        phase_moe(k, l, ctx_out)


def build_program(layers, ext_out=()):
    k = K(ext_out=ext_out)
    declare_io(k)
    emit_layers(k, layers)
    with k.nc.Block() as block:
        k.P.replay(block)
    k.stack.close()
    return k


FUSED = False
N_CORES = 8


def kernel(**inputs):
    inp = {n: np.asarray(v) for n, v in inputs.items()}
    maps = [core_inputs(inp, c) for c in range(N_CORES)]
    if FUSED:
        k = build_program(list(range(DEPTH)))
        res = run_bass_kernel_spmd(k.nc, maps, core_ids=list(range(N_CORES)))
        ys = [r["y"] for r in res.results]
    else:
        for l in range(DEPTH):
            k = build_program([l], ext_out=("ctxs",))
            res = run_bass_kernel_spmd(k.nc, maps, core_ids=list(range(N_CORES)))
            for c in range(N_CORES):
                maps[c]["x"] = np.ascontiguousarray(res.results[c]["y"])
                maps[c]["ctx"] = np.ascontiguousarray(res.results[c]["ctxs"][:, 0:S])
        ys = [m["x"] for m in maps]
    return np.concatenate(ys, axis=0).astype(np.float32)
```

```python
import contextlib
import math
import numpy as np
import ml_dtypes
import concourse.bass as bass
import concourse.mybir as mybir
from concourse.bass_utils import run_bass_kernel_spmd

F32 = mybir.dt.float32
BF16 = mybir.dt.bfloat16
I32 = mybir.dt.int32
U32 = mybir.dt.uint32
AF = mybir.ActivationFunctionType
ALU = mybir.AluOpType
AX = mybir.AxisListType

ENGS = ("pe", "dve", "act", "pool", "sp")

D = 1024
NB = 2
T = 4096
S = 256
DEPTH = 4
INW = 3088
EPS = 1e-6
NE = 16
A_Q, A_K, A_V = 0, 256, 384
B_U = 512
C_Q, C_K, C_V, C_O, C_G = 1280, 1536, 1792, 2048, 2304
D_Q, D_K, D_V = 2320, 2576, 2832


class Buf:
    __slots__ = ("w", "r")

    def __init__(self):
        self.w = None
        self.r = []


class TT:
    def __init__(self, t, nbuf=1):
        self.t = t
        self.b = Buf()

    def __getitem__(self, k):
        return self.t[k]


class Prog:
    NDMA = 8
    EPOCH = 30000

    def __init__(self, nc, stack):
        self.nc = nc
        self.stack = stack
        self.q = {e: [] for e in ENGS}
        self.cnt = {e: 0 for e in ENGS}
        self.sem = {e: stack.enter_context(nc.semaphore("c_" + e)) for e in ENGS}
        self.nsem = 0
        self.seen = {e: {} for e in ENGS}
        self.dsem = {}
        self.dcnt = {}
        self.dnext = {}
        for e in ("sp", "pool", "act"):
            self.dsem[e] = [stack.enter_context(nc.semaphore("d_%s%d" % (e, i))) for i in range(self.NDMA)]
            self.dcnt[e] = [0] * self.NDMA
            self.dnext[e] = 0
        self.n_inst = 0
        self.pe_sems = {id(self.sem["pe"])}
        self.pending = False

    def _need(self, eng, toks):
        seen = self.seen[eng]
        own = self.sem[eng]
        for tok in toks:
            if tok is None:
                continue
            sem, val = tok
            if eng == "pe" and id(sem) in self.pe_sems:
                continue
            k = id(sem)
            if seen.get(k, 0) >= val:
                continue
            seen[k] = val
            self.q[eng].append(("wait", sem, val))

    @staticmethod
    def _deps(reads, writes):
        toks = []
        for b in reads:
            toks.append(b.w)
        for b in writes:
            toks.append(b.w)
            toks.extend(b.r)
        return toks

    @staticmethod
    def _mark(tok, reads, writes):
        for b in reads:
            if len(b.r) > 24:
                b.r = b.r[-24:]
            b.r.append(tok)
        for b in writes:
            b.w = tok
            b.r = []

    def _bufs(self, xs):
        return [x.b if isinstance(x, TT) else x for x in xs]

    def op(self, eng, fn, reads=(), writes=(), sig=True):
        reads = self._bufs(reads)
        writes = self._bufs(writes)
        if not sig:
            assert eng == "pe"
            self._need(eng, self._deps(reads, writes))
            tok = (self.sem[eng], self.cnt[eng] + 1)
            self.q[eng].append(("opns", fn))
            self._mark(tok, reads, writes)
            self.n_inst += 1
            self.pending = True
            return tok
        if self.cnt[eng] >= self.EPOCH and not (eng == "pe" and self.pending):
            old = (self.sem[eng], self.cnt[eng])
            self.nsem += 1
            self.sem[eng] = self.stack.enter_context(self.nc.semaphore("c_%s_%d" % (eng, self.nsem)))
            self.cnt[eng] = 0
            if eng == "pe":
                self.pe_sems.add(id(self.sem[eng]))
        self._need(eng, self._deps(reads, writes))
        self.cnt[eng] += 1
        tok = (self.sem[eng], self.cnt[eng])
        self.q[eng].append(("op", fn, self.sem[eng]))
        self._mark(tok, reads, writes)
        self.n_inst += 1
        if eng == "pe":
            self.pending = False
        return tok

    def dma(self, eng, fn, reads=(), writes=()):
        reads = self._bufs(reads)
        writes = self._bufs(writes)
        self._need(eng, self._deps(reads, writes))
        i = self.dnext[eng]
        self.dnext[eng] = (i + 1) % self.NDMA
        sem = self.dsem[eng][i]
        if self.dcnt[eng][i] > 0:
            self._need(eng, [(sem, self.dcnt[eng][i])])
        self.dcnt[eng][i] += 16
        tok = (sem, self.dcnt[eng][i])
        self.q[eng].append(("dma", fn, sem))
        self._mark(tok, reads, writes)
        self.n_inst += 1
        return tok

    def barrier(self):
        toks = [(self.sem[e], self.cnt[e]) for e in ENGS if self.cnt[e] > 0]
        for e in self.dsem:
            for i in range(self.NDMA):
                if self.dcnt[e][i] > 0:
                    toks.append((self.dsem[e][i], self.dcnt[e][i]))
        for e in ENGS:
            self._need(e, toks)

    def mark(self):
        for e in ENGS:
            self.q[e].append(("mark",))

    def replay_all(self):
        segs = {e: [[]] for e in ENGS}
        for e in ENGS:
            for it in self.q[e]:
                if it[0] == "mark":
                    segs[e].append([])
                else:
                    segs[e][-1].append(it)
        nseg = len(segs["pe"])
        full = self.q
        for i in range(nseg):
            self.q = {e: segs[e][i] for e in ENGS}
            with self.nc.Block() as block:
                self.replay(block)
        self.q = full

    def replay(self, block):
        q = self.q

        def run(engobj, items):
            for it in items:
                if it[0] == "wait":
                    engobj.wait_ge(it[1], it[2])
                elif it[0] == "opns":
                    it[1](engobj)
                elif it[0] == "op":
                    it[1](engobj).then_inc(it[2], 1)
                else:
                    it[1](engobj).then_inc(it[2], 16)

        block.tensor(lambda e: run(e, q["pe"]))
        block.vector(lambda e: run(e, q["dve"]))
        block.scalar(lambda e: run(e, q["act"]))
        block.gpsimd(lambda e: run(e, q["pool"]))
        block.sync(lambda e: run(e, q["sp"]))


def MM(out, lhsT, rhs, start=True, stop=True):
    return lambda e: e.matmul(out, lhsT=lhsT, rhs=rhs, start=start, stop=stop, skip_group_check=True)


def TR(out, in_, ident):
    return lambda e: e.transpose(out, in_, ident)


def ACT(out, in_, func, bias=None, scale=None, accum_out=None):
    kw = {}
    if bias is not None:
        kw["bias"] = bias
    if scale is not None:
        kw["scale"] = scale
    if accum_out is not None:
        kw["accum_out"] = accum_out
    return lambda e: e.activation(out=out, in_=in_, func=func, **kw)


def TS(out, in0, s1, s2, op0, op1=None, accum_out=None):
    if op1 is None:
        return lambda e: e.tensor_scalar(out=out, in0=in0, scalar1=s1, scalar2=None, op0=op0)
    if accum_out is not None:
        return lambda e: e.tensor_scalar(out=out, in0=in0, scalar1=s1, scalar2=s2, op0=op0, op1=op1, accum_out=accum_out)
    return lambda e: e.tensor_scalar(out=out, in0=in0, scalar1=s1, scalar2=s2, op0=op0, op1=op1)


def TTo(out, in0, in1, op):
    return lambda e: e.tensor_tensor(out=out, in0=in0, in1=in1, op=op)


def STT(out, in0, scalar, in1, op0, op1):
    return lambda e: e.scalar_tensor_tensor(out=out, in0=in0, scalar=scalar, in1=in1, op0=op0, op1=op1)


def CP(out, in_):
    return lambda e: e.tensor_copy(out=out, in_=in_)


def RED(out, in_, op=None, axis=None):
    op = op or ALU.add
    axis = axis or AX.X
    return lambda e: e.tensor_reduce(out=out, in_=in_, axis=axis, op=op)


def RCP(out, in_):
    return lambda e: e.reciprocal(out=out, in_=in_)


def MSET(ap, c):
    return lambda e: e.memset(ap, c)


def DMA(out, in_, **kw):
    return lambda e: e.dma_start(out=out, in_=in_, **kw)


WEIGHT_SPECS = [
    ("w_ada", [DEPTH, D, 6 * D]), ("b_ada", [DEPTH, 6 * D]), ("norm1_g", [DEPTH, D]), ("norm2_g", [DEPTH, D]),
    ("w_in", [DEPTH, D, INW]), ("b_gate", [DEPTH, 16]), ("a_qnorm", [DEPTH, 64]), ("a_knorm", [DEPTH, 64]),
    ("a_sink", [DEPTH, 4]), ("hy_conv", [DEPTH, 3, 768]), ("hy_fw1", [DEPTH, 33, 64]), ("hy_fb1", [DEPTH, 64]),
    ("hy_freq", [DEPTH, 64]), ("hy_fw2", [DEPTH, 64, 64]), ("hy_fb2", [DEPTH, 64]), ("hy_fw3", [DEPTH, 64, 1024]),
    ("hy_bias", [DEPTH, 2, 256]), ("ml_norm", [DEPTH, 64]), ("d_qnorm", [DEPTH, 32]), ("d_knorm", [DEPTH, 32]),
    ("d_lq1", [DEPTH, 32]), ("d_lk1", [DEPTH, 32]), ("d_lq2", [DEPTH, 32]), ("d_lk2", [DEPTH, 32]),
    ("d_subnorm", [DEPTH, 64]), ("w_out", [DEPTH, D, D]), ("w_router", [DEPTH, D, NE]),
    ("w_e_gate", [DEPTH, NE, D, D]), ("w_e_up", [DEPTH, NE, D, D]), ("w_e_down", [DEPTH, NE, D, D]),
]


class K:
    def __init__(self, ext_in=(), ext_out=()):
        self.nc = bass.Bass("TRN2", target_bir_lowering=False)
        self.ext_in = set(ext_in)
        self.ext_out = set(ext_out)
        self.dram = {}
        self.stack = contextlib.ExitStack()
        self.P = Prog(self.nc, self.stack)
        self.uid = 0

    def dt(self, name, shape, dtype, kind=None):
        if name in self.dram:
            return self.dram[name]
        if kind is None:
            kind = "ExternalInput" if name in self.ext_in else ("ExternalOutput" if name in self.ext_out else "Internal")
        t = self.nc.dram_tensor(name, list(shape), dtype, kind=kind).ap()
        self.dram[name] = t
        return t

    def sb(self, st, shape, dtype, name=None):
        self.uid += 1
        return TT(st.enter_context(self.nc.sbuf_tensor("%s_%d" % (name or "sb", self.uid), list(shape), dtype)))

    def dbg(self, name, tile_ap, shape, dtype, reads):
        if name not in self.ext_out or name in self.dram:
            return
        t = self.dt(name, shape, dtype)
        self.P.dma("sp", DMA(t, tile_ap), reads=reads, writes=[Buf()])

    def ps(self, st, shape, dtype, name=None):
        self.uid += 1
        return TT(st.enter_context(self.nc.psum_tensor("%s_%d" % (name or "ps", self.uid), list(shape), dtype)))


def declare_io(k):
    k.dt("x", [NB, T, D], F32, "ExternalInput")
    k.dt("ctx", [NB, S, D], F32, "ExternalInput")
    k.dt("ccT", [128, 8, 3], F32, "ExternalInput")
    for n, shp in WEIGHT_SPECS:
        k.dt(n, shp, F32, "ExternalInput")
    k.dt("ident_bf", [128, 128], BF16, "ExternalInput")
    k.dt("ident_f", [128, 128], F32, "ExternalInput")
    k.dt("mask_low", [128, 128], BF16, "ExternalInput")
    k.dt("mask_up", [128, 128], BF16, "ExternalInput")
    k.dt("tri_f", [128, 128], F32, "ExternalInput")
    k.dt("tri_b", [128, 128], F32, "ExternalInput")
    for tag, L in (("L", T), ("C", S)):
        nt = L // 128
        k.dt("zT" + tag, [33, L], F32, "ExternalInput")
        k.dt("dec" + tag, [L, 256], F32, "ExternalInput")
        k.dt("alt" + tag, [128, nt], BF16, "ExternalInput")
        for nm in ("Cm", "Sm", "Si"):
            k.dt(nm + tag, [nt, 128, nt, 128], BF16, "ExternalInput")
    k.dt("ropeA", [T, 64], F32, "ExternalInput")
    k.dt("ropeD", [T, 32], F32, "ExternalInput")
    k.dt("y", [NB, T, D], F32, "ExternalOutput")
    k.dt("mod", [DEPTH, 3, 6 * D], F32)
    k.dt("ctxs", [NB, S + 128, D], F32)
    for b in range(NB):
        k.dt("pL%d" % b, [T, INW], BF16)
        k.dt("pC%d" % b, [S, INW], BF16)
        k.dt("gL%d" % b, [T, 16], F32)
        k.dt("gC%d" % b, [S, 16], F32)
        k.dt("ycL%d" % b, [T, D], BF16)
        k.dt("ycC%d" % b, [S, D], BF16)
        k.dt("h2L%d" % b, [T, D], BF16)
        k.dt("h2C%d" % b, [S + 128, D], BF16)
        k.dt("affL%d" % b, [16, T], F32)
        k.dt("affC%d" % b, [16, S], F32)
    k.dt("idxL", [128, 4, 64], I32)
    k.dt("gateL", [128, 4, 64], F32)
    k.dt("idxC", [128, 1, 64], I32)
    k.dt("gateC", [128, 1, 64], F32)
    k.dt("zeros", [128, D], F32, "ExternalInput")


def streams(k):
    out = []
    for b in range(NB):
        out.append(("L%d" % b, T, k.dram["y"][b], b))
    for b in range(NB):
        out.append(("C%d" % b, S, k.dram["ctxs"][b, 0:S], 2))
    return out


def phase_init(k):
    P = k.P
    d = k.dram
    by = Buf()
    for b in range(NB):
        for j in range(4):
            P.dma("sp", DMA(d["y"][b, j * 1024:(j + 1) * 1024, :], d["x"][b, j * 1024:(j + 1) * 1024, :]), writes=[by])
        P.dma("sp", DMA(d["ctxs"][b, 0:S], d["ctx"][b]), writes=[by])
        P.dma("sp", DMA(d["ctxs"][b, S:S + 128], d["zeros"]), writes=[by])
        P.dma("sp", DMA(d["h2C%d" % b][S:S + 128], d["zeros"][:, 0:D // 2].bitcast(BF16)), writes=[by])
    P.barrier()


def phase_ada(k, layers):
    P = k.P
    d = k.dram
    with contextlib.ExitStack() as st:
        cc = k.sb(st, [128, 24], F32)
        sT = k.sb(st, [128, 24], BF16)
        P.dma("sp", DMA(cc[:], d["ccT"].rearrange("p k r -> p (k r)")), writes=[cc])
        P.op("act", ACT(sT[:], cc[:], AF.Silu), reads=[cc], writes=[sT])
        wb = [k.sb(st, [128, 8, 512], BF16, "wada") for _ in range(3)]
        pss = [k.ps(st, [128, 512], F32, "adaps") for _ in range(2)]
        bada = k.sb(st, [3, 6 * D], F32)
        modsb = k.sb(st, [3, 6 * D], F32)
        it = 0
        for l in layers:
            P.dma("sp", DMA(bada[:], d["b_ada"][l:l + 1, :].broadcast_to([3, 6 * D])), writes=[bada])
            for j in range(12):
                w = wb[it % 3]
                ps = pss[it % 2]
                it += 1
                src = d["w_ada"][l, :, j * 512:(j + 1) * 512].rearrange("(k p) n -> p k n", p=128)
                P.dma("pool", DMA(w[:], src), writes=[w])
                for kk in range(8):
                    P.op("pe", MM(ps[0:3, :], sT[:, kk * 3:(kk + 1) * 3], w[:, kk, :], start=(kk == 0), stop=(kk == 7)),
                         reads=[sT, w], writes=[ps], sig=(kk == 7))
                P.op("dve", TTo(modsb[0:3, j * 512:(j + 1) * 512], ps[0:3, :], bada[0:3, j * 512:(j + 1) * 512], ALU.add),
                     reads=[ps, bada], writes=[modsb])
            P.dma("sp", DMA(d["mod"][l], modsb[:]), reads=[modsb], writes=[Buf()])
        P.barrier()


def load_bcast(k, P, dst, src_row, eng="sp"):
    n = src_row.shape[-1]
    P.dma(eng, DMA(dst[:], src_row.broadcast_to([128, n])), writes=[dst])


def norm_mod_T(k, P, st_tiles, xsrc, A, sh, ident, hT, j, tp, xt, tmp, hb, ss, junk):
    P.dma("sp", DMA(xt[:], xsrc), writes=[xt])
    P.op("act", ACT(junk[:], xt[:], AF.Square, accum_out=ss[:, 0:1]), reads=[xt], writes=[junk, ss])
    P.op("dve", TS(ss[:, 1:2], ss[:, 0:1], 1.0 / D, EPS, ALU.mult, ALU.add), reads=[ss], writes=[ss])
    P.op("act", ACT(ss[:, 2:3], ss[:, 1:2], AF.Sqrt), reads=[ss], writes=[ss])
    P.op("dve", RCP(ss[:, 3:4], ss[:, 2:3]), reads=[ss], writes=[ss])
    P.op("dve", STT(tmp[:], xt[:], ss[:, 3:4], A[:], ALU.mult, ALU.mult), reads=[xt, ss, A], writes=[tmp])
    P.op("dve", TTo(hb[:], tmp[:], sh[:], ALU.add), reads=[tmp, sh], writes=[hb])
    for kk in range(8):
        P.op("pe", TR(tp[:, kk, :], hb[:, kk * 128:(kk + 1) * 128], ident[:]), reads=[hb, ident], writes=[tp], sig=(kk == 7))
    P.op("act", ACT(hT[:, :, j * 128:(j + 1) * 128], tp[:], AF.Copy), reads=[tp], writes=[hT])


def phase_proj(k, l):
    P = k.P
    d = k.dram
    with contextlib.ExitStack() as st:
        ident = k.sb(st, [128, 128], BF16)
        P.dma("sp", DMA(ident[:], d["ident_bf"]), writes=[ident])
        win = k.sb(st, [128, 8, INW], BF16, "win")
        wbufs = [Buf() for _ in range(7)]
        wbufs[6] = wbufs[5]
        for c in range(6):
            c0 = c * 512
            c1 = INW if c == 5 else c0 + 512
            src = d["w_in"][l, :, c0:c1].rearrange("(k p) n -> p k n", p=128)
            P.dma("pool", DMA(win[:, :, c0:c1], src), writes=[wbufs[c]])
        g1 = k.sb(st, [128, D], F32)
        load_bcast(k, P, g1, d["norm1_g"][l:l + 1, :])
        A = k.sb(st, [128, D], F32)
        sh = k.sb(st, [128, D], F32)
        sc = k.sb(st, [128, D], F32)
        xts = [k.sb(st, [128, D], F32, "xt") for _ in range(3)]
        tmps = [k.sb(st, [128, D], F32, "tmp") for _ in range(2)]
        hbs = [k.sb(st, [128, D], BF16, "hb") for _ in range(2)]
        sss = [k.sb(st, [128, 4], F32, "ss") for _ in range(3)]
        junk = k.sb(st, [128, D], BF16, "junk")
        hTs = [k.sb(st, [128, 8, 512], BF16, "hT") for _ in range(2)]
        pbs = [k.sb(st, [128, INW], BF16, "pb") for _ in range(2)]
        gbs = [k.sb(st, [128, 16], F32, "gb") for _ in range(2)]
        tps = [k.ps(st, [128, 8, 128], BF16, "tp") for _ in range(2)]
        accs = [k.ps(st, [128, 512], F32, "acc") for _ in range(4)]
        it = 0
        ia = 0
        ig = 0
        for (tag, ntok, xs, mrow) in streams(k):
            load_bcast(k, P, sc, d["mod"][l, mrow:mrow + 1, D:2 * D])
            load_bcast(k, P, sh, d["mod"][l, mrow:mrow + 1, 0:D])
            P.op("dve", STT(A[:], sc[:], 1.0, g1[:], ALU.add, ALU.mult), reads=[sc, g1], writes=[A])
            gs = min(512, ntok)
            nt = gs // 128
            pd = d["p" + tag]
            gd = d["g" + tag]
            for g in range(ntok // gs):
                hT = hTs[ig % 2]
                ig += 1
                for j in range(nt):
                    r0 = g * gs + j * 128
                    norm_mod_T(k, P, None, xs[r0:r0 + 128, :], A, sh, ident, hT, j, tps[it % 2], xts[it % 3],
                               tmps[it % 2], hbs[it % 2], sss[it % 3], junk)
                    it += 1
                for j in range(nt):
                    r0 = g * gs + j * 128
                    pb = pbs[j % 2]
                    gb = gbs[j % 2]
                    for c in range(7):
                        c0 = c * 448
                        c1 = min(INW, c0 + 448)
                        acc = accs[ia % 4]
                        ia += 1
                        for kk in range(8):
                            P.op("pe", MM(acc[:, 0:c1 - c0], hT[:, kk, j * 128:(j + 1) * 128], win[:, kk, c0:c1],
                                          start=(kk == 0), stop=(kk == 7)), reads=[hT] + wbufs[0:6], writes=[acc], sig=(kk == 7))
                        if ia % 2 == 0:
                            P.op("dve", CP(pb[:, c0:c1], acc[:, 0:c1 - c0]), reads=[acc], writes=[pb])
                        else:
                            P.op("act", ACT(pb[:, c0:c1], acc[:, 0:c1 - c0], AF.Copy), reads=[acc], writes=[pb])
                        if c == 5:
                            P.op("dve", CP(gb[:], acc[:, C_G - c0:C_G - c0 + 16]), reads=[acc], writes=[gb])
                    P.dma("sp", DMA(pd[r0:r0 + 128, :], pb[:]), reads=[pb], writes=[Buf()])
                    P.dma("sp", DMA(gd[r0:r0 + 128, :], gb[:]), reads=[gb], writes=[Buf()])
        if not getattr(k, "no_proj_barrier", False):
            P.barrier()


_CONSTS = None


def consts():
    global _CONSTS
    if _CONSTS is None:
        c = {}
        c["ident_bf"] = np.eye(128, dtype=np.float32).astype(ml_dtypes.bfloat16)
        c["ident_f"] = np.eye(128, dtype=np.float32)
        c["zeros"] = np.zeros((128, D), np.float32)
        kk = np.arange(128)[:, None]
        qq = np.arange(128)[None, :]
        c["mask_low"] = (qq <= kk).astype(np.float32).astype(ml_dtypes.bfloat16)
        c["mask_up"] = (kk <= qq).astype(np.float32).astype(ml_dtypes.bfloat16)
        c["tri_f"] = (kk <= qq).astype(np.float32) - np.float32(0.5)
        c["tri_b"] = (kk >= qq).astype(np.float32) - np.float32(0.5)
        for tag, L in (("L", T), ("C", S)):
            nt = L // 128
            tt = np.linspace(0.0, 1.0, L, dtype=np.float32)[:, None]
            w = (np.float32(2.0 * math.pi) * np.arange(L, dtype=np.float32)[:, None] / np.float32(L)).astype(np.float32)
            bands = np.linspace(1e-4, 15, 16, dtype=np.float32)
            zz = np.concatenate([tt, np.cos(bands * w), -np.sin(bands * w)], axis=-1).astype(np.float32)
            c["zT" + tag] = np.ascontiguousarray(zz.T)
            deltas = np.abs(np.linspace(math.log(1e-2) / 1.5, math.log(1e-2) / 0.3, 256, dtype=np.float32))
            c["dec" + tag] = np.exp(-tt * deltas).astype(np.float32)
            idx = np.arange(L, dtype=np.int64)
            ang = (2.0 * np.pi / (2 * L)) * ((idx[:, None] * idx[None, :]) % (2 * L)).astype(np.float64)
            Cm = np.cos(ang)
            Sn = np.sin(ang)
            altv = np.where(idx % 2 == 0, 1.0, -1.0)
            SmP = -Sn.copy()
            SmP[:, 0] = altv
            Si = -Sn.copy()
            Si[0, :] = altv
            def tile(M):
                return np.ascontiguousarray(M.reshape(nt, 128, nt, 128).transpose(2, 1, 0, 3)).astype(ml_dtypes.bfloat16)
            c["Cm" + tag] = tile(Cm)
            c["Sm" + tag] = tile(SmP)
            c["Si" + tag] = tile(Si)
            c["alt" + tag] = np.ascontiguousarray(altv.reshape(nt, 128).T).astype(ml_dtypes.bfloat16)
        for nm, dh in (("ropeA", 64), ("ropeD", 32)):
            nf = dh // 4
            t = np.arange(T)
            r = (t // 64).astype(np.float32)
            col = (t % 64).astype(np.float32)
            inv = (np.float32(10000.0) ** (-np.arange(nf, dtype=np.float32) / np.float32(nf))).astype(np.float32)
            ang = np.stack([r[:, None] * inv, col[:, None] * inv], axis=1).astype(np.float32)
            c[nm] = np.concatenate([np.cos(ang).reshape(T, 2 * nf), np.sin(ang).reshape(T, 2 * nf)], axis=1).astype(np.float32)
        _CONSTS = c
    return _CONSTS


def core_inputs(inp, core):
    m = {}
    b0 = core * NB
    m["x"] = np.ascontiguousarray(inp["x"][b0:b0 + NB])
    m["ctx"] = np.ascontiguousarray(inp["ctx"][b0:b0 + NB])
    cc = np.concatenate([inp["c"][b0:b0 + NB], inp["c_ctx"][None, :]], axis=0)
    m["ccT"] = np.ascontiguousarray(cc.T.reshape(8, 128, 3).transpose(1, 0, 2))
    for n, shp in WEIGHT_SPECS:
        m[n] = np.ascontiguousarray(np.asarray(inp[n], dtype=np.float32).reshape(shp))
    m.update(consts())
    return m


def qk_prep(P, eng, src, U, dh, gains, tab, out, w):
    sq, qn, ss = w["sq"], w["qn"], w["ss"]
    n = U * dh
    P.op(eng, TTo(sq[:, 0:n], src[:, 0:n], src[:, 0:n], ALU.mult), reads=[src], writes=[sq])
    P.op("dve", RED(ss[:, 0:U], sq[:, 0:n].rearrange("p (u d) -> p u d", d=dh)), reads=[sq], writes=[ss])
    P.op("dve", TS(ss[:, U:2 * U], ss[:, 0:U], 1.0 / dh, EPS, ALU.mult, ALU.add), reads=[ss], writes=[ss])
    P.op("act", ACT(ss[:, 0:U], ss[:, U:2 * U], AF.Sqrt), reads=[ss], writes=[ss])
    P.op("dve", RCP(ss[:, 2 * U:3 * U], ss[:, 0:U]), reads=[ss], writes=[ss])
    rb = ss[:, 2 * U:3 * U].unsqueeze(2).broadcast_to([128, U, dh])
    qv = qn[:, 0:n].rearrange("p (u d) -> p u d", d=dh)
    P.op("dve", TTo(qv, src[:, 0:n].rearrange("p (u d) -> p u d", d=dh), rb, ALU.mult), reads=[src, ss], writes=[qn])
    if tab is None:
        P.op(eng, TTo(out[:, 0:n], qn[:, 0:n], gains[:].rearrange("p u d -> p (u d)"), ALU.mult), reads=[qn, gains], writes=[out])
        return
    P.op(eng, TTo(qn[:, 0:n], qn[:, 0:n], gains[:].rearrange("p u d -> p (u d)"), ALU.mult), reads=[qn, gains], writes=[qn])
    nf = dh // 4
    q5 = qn[:, 0:n].rearrange("p (u a h f) -> p u a h f", a=2, h=2, f=nf)
    o5 = out[:, 0:n].rearrange("p (u a h f) -> p u a h f", a=2, h=2, f=nf)
    a_, b_ = q5[:, :, :, 0, :], q5[:, :, :, 1, :]
    c_ = tab[:, 0].unsqueeze(1).broadcast_to([128, U, 2, nf])
    s_ = tab[:, 1].unsqueeze(1).broadcast_to([128, U, 2, nf])
    hn = n // 2
    t1 = w["t1"][:, 0:hn].rearrange("p (u a f) -> p u a f", a=2, f=nf)
    t2 = w["t2"][:, 0:hn].rearrange("p (u a f) -> p u a f", a=2, f=nf)
    P.op("dve", TTo(t1, a_, c_, ALU.mult), reads=[qn, tab], writes=[w["t1"]])
    P.op(eng, TTo(t2, b_, s_, ALU.mult), reads=[qn, tab], writes=[w["t2"]])
    P.op("dve", TTo(o5[:, :, :, 0, :], t1, t2, ALU.subtract), reads=[w["t1"], w["t2"]], writes=[out])
    P.op(eng, TTo(t1, b_, c_, ALU.mult), reads=[qn, tab], writes=[w["t1"]])
    P.op("dve", TTo(t2, a_, s_, ALU.mult), reads=[qn, tab], writes=[w["t2"]])
    P.op(eng, TTo(o5[:, :, :, 1, :], t1, t2, ALU.add), reads=[w["t1"], w["t2"]], writes=[out])


def prep_work(k, st, n, U):
    return {"sq": k.sb(st, [128, n], F32, "sq"), "qn": k.sb(st, [128, n], F32, "qn"), "ss": k.sb(st, [128, 3 * U], F32, "ss"),
            "t1": k.sb(st, [128, n // 2], F32, "t1"), "t2": k.sb(st, [128, n // 2], F32, "t2")}


def phase_mixA(k, l, ctx_out):
    P = k.P
    d = k.dram
    scale = 64 ** -0.5
    with contextlib.ExitStack() as st:
        ident = k.sb(st, [128, 128], BF16)
        P.dma("sp", DMA(ident[:], d["ident_bf"]), writes=[ident])
        lowm = k.sb(st, [128, 128], BF16)
        upm = k.sb(st, [128, 128], BF16)
        P.dma("sp", DMA(lowm[:], d["mask_low"]), writes=[lowm])
        P.dma("sp", DMA(upm[:], d["mask_up"]), writes=[upm])
        gains = k.sb(st, [128, 6, 64], F32)
        gq = k.sb(st, [128, 64], F32)
        gk = k.sb(st, [128, 64], F32)
        load_bcast(k, P, gq, d["a_qnorm"][l:l + 1, :])
        load_bcast(k, P, gk, d["a_knorm"][l:l + 1, :])
        P.op("dve", CP(gains[:, 0:4, :], gq[:].unsqueeze(1).broadcast_to([128, 4, 64])), reads=[gq], writes=[gains])
        P.op("dve", CP(gains[:, 4:6, :], gk[:].unsqueeze(1).broadcast_to([128, 2, 64])), reads=[gk], writes=[gains])
        esink = k.sb(st, [128, 4], F32)
        load_bcast(k, P, esink, d["a_sink"][l:l + 1, :])
        P.op("act", ACT(esink[:], esink[:], AF.Exp), reads=[esink], writes=[esink])
        qT = k.sb(st, [64, 4, T], BF16, "qT")
        kT = k.sb(st, [64, 2, T + S], BF16, "kT")
        qcT = k.sb(st, [64, 4, S], BF16, "qcT")
        va = k.sb(st, [128, 34, 2, 65], BF16, "va")
        P.op("pool", MSET(va[:], 1.0), writes=[va])
        srcs = [k.sb(st, [128, 384], BF16, "src") for _ in range(3)]
        outs = [k.sb(st, [128, 384], BF16, "out") for _ in range(2)]
        tabs = [k.sb(st, [128, 2, 2, 16], F32, "tab") for _ in range(3)]
        works = [prep_work(k, st, 384, 6) for _ in range(2)]
        tps = [k.ps(st, [64, 6, 128], BF16, "tpA") for _ in range(2)]
        sps = [k.ps(st, [128, 256], F32, "spA") for _ in range(3)]
        ops = [k.ps(st, [128, 4, 65], F32, "opA") for _ in range(2)]
        pts = [k.sb(st, [128, 256], BF16, "ptA") for _ in range(3)]
        dens = [k.sb(st, [128, 8], F32, "denA") for _ in range(2)]
        yas = [k.sb(st, [128, 4, 64], BF16, "yaA") for _ in range(2)]
        for b in range(NB):
            pL, pC = d["pL%d" % b], d["pC%d" % b]
            for i in range(34):
                src, out, tab, w, tp = srcs[i % 3], outs[i % 2], tabs[i % 3], works[i % 2], tps[i % 2]
                lat = i < 32
                pd = pL if lat else pC
                r0 = (i if lat else i - 32) * 128
                P.dma("sp", DMA(src[:], pd[r0:r0 + 128, A_Q:A_Q + 384]), writes=[src])
                P.dma("sp", DMA(va[:, i, :, 0:64], pd[r0:r0 + 128, A_V:A_V + 128].rearrange("p (h e) -> p h e", e=64)),
                      writes=[va])
                if lat:
                    P.dma("sp", DMA(tab[:].rearrange("p c a f -> p (c a f)"), d["ropeA"][r0:r0 + 128, :]), writes=[tab])
                qk_prep(P, "dve", src, 6, 64, gains, tab if lat else None, out, w)
                k.dbg("dbgA_src", src[:], [128, 384], BF16, [src])
                k.dbg("dbgA_ss", w["ss"][:], [128, 18], F32, [w["ss"]])
                k.dbg("dbgA_qn", w["qn"][:], [128, 384], F32, [w["qn"]])
                k.dbg("dbgA_out", out[:], [128, 384], BF16, [out])
                for u in range(6):
                    P.op("pe", TR(tp[:, u, :], out[:, u * 64:(u + 1) * 64], ident[:]), reads=[out, ident], writes=[tp], sig=(u == 5))
                c0 = i * 128
                if lat:
                    P.op("act", ACT(qT[:, :, c0:c0 + 128], tp[:, 0:4, :], AF.Copy), reads=[tp], writes=[qT])
                else:
                    P.op("act", ACT(qcT[:, :, r0:r0 + 128], tp[:, 0:4, :], AF.Copy), reads=[tp], writes=[qcT])
                P.op("act", ACT(kT[:, :, c0:c0 + 128], tp[:, 4:6, :], AF.Copy), reads=[tp], writes=[kT])
            blocks = [(n, True) for n in range(32)] + ([(n, False) for n in range(2)] if ctx_out else [])
            isp = 0
            for bi, (n, lat) in enumerate(blocks):
                op_ = ops[bi % 2]
                den = dens[bi % 2]
                ya = yas[bi % 2]
                if lat:
                    keys = []
                    if n > 0:
                        keys.append((n - 1, lowm))
                    keys.append((n, None))
                    if n < 31:
                        keys.append((n + 1, upm))
                    keys += [(32, None), (33, None)]
                else:
                    keys = [(32, None), (33, None)]
                first = True
                for h in range(2):
                    for (kt, msk) in keys:
                        sp_ = sps[isp % 3]
                        pt = pts[isp % 3]
                        isp += 1
                        if lat:
                            rhs = qT[:, 2 * h:2 * h + 2, n * 128:(n + 1) * 128]
                            qdep = qT
                        else:
                            rhs = qcT[:, 2 * h:2 * h + 2, n * 128:(n + 1) * 128]
                            qdep = qcT
                        P.op("pe", MM(sp_[:].rearrange("p (g q) -> p g q", g=2), kT[:, h, kt * 128:(kt + 1) * 128], rhs),
                             reads=[kT, qdep], writes=[sp_])
                        P.op("act", ACT(pt[:], sp_[:], AF.Exp, scale=scale), reads=[sp_], writes=[pt])
                        if msk is not None:
                            P.op("dve", TTo(pt[:].rearrange("p (g q) -> p g q", g=2), pt[:].rearrange("p (g q) -> p g q", g=2),
                                            msk[:].unsqueeze(1).broadcast_to([128, 2, 128]), ALU.mult), reads=[pt, msk], writes=[pt])
                        for g in range(2):
                            P.op("pe", MM(op_[:, 2 * h + g, :], pt[:, g * 128:(g + 1) * 128], va[:, kt, h, :], start=first, stop=True),
                                 reads=[pt, va], writes=[op_], sig=(g == 1))
                            first = False
                k.dbg("dbgA_pt", pt[:], [128, 256], BF16, [pt])
                k.dbg("dbgA_qT", qT[:, 0, 0:512], [64, 512], BF16, [qT])
                P.op("dve", TTo(den[:, 0:4], op_[:, :, 64], esink[:], ALU.add), reads=[op_, esink], writes=[den])
                P.op("dve", RCP(den[:, 4:8], den[:, 0:4]), reads=[den], writes=[den])
                P.op("dve", TTo(ya[:], op_[:, :, 0:64], den[:, 4:8].unsqueeze(2).broadcast_to([128, 4, 64]), ALU.mult),
                     reads=[op_, den], writes=[ya])
                yd = d["ycL%d" % b] if lat else d["ycC%d" % b]
                P.dma("sp", DMA(yd[n * 128:(n + 1) * 128, 0:256], ya[:].rearrange("p h e -> p (h e)")), reads=[ya], writes=[Buf()])
        P.barrier()


def phase_mixD(k, l, ctx_out):
    P = k.P
    d = k.dram
    scale = 32 ** -0.5
    lam_init = 0.8 - 0.6 * math.exp(-0.3 * l)
    with contextlib.ExitStack() as st:
        ident = k.sb(st, [128, 128], BF16)
        P.dma("sp", DMA(ident[:], d["ident_bf"]), writes=[ident])
        gains = k.sb(st, [128, 16, 32], F32)
        gq = k.sb(st, [128, 32], F32)
        gk = k.sb(st, [128, 32], F32)
        load_bcast(k, P, gq, d["d_qnorm"][l:l + 1, :])
        load_bcast(k, P, gk, d["d_knorm"][l:l + 1, :])
        P.op("dve", CP(gains[:, 0:8, :], gq[:].unsqueeze(1).broadcast_to([128, 8, 32])), reads=[gq], writes=[gains])
        P.op("dve", CP(gains[:, 8:16, :], gk[:].unsqueeze(1).broadcast_to([128, 8, 32])), reads=[gk], writes=[gains])
        lq = k.sb(st, [128, 4, 32], F32)
        for i, nm in enumerate(("d_lq1", "d_lk1", "d_lq2", "d_lk2")):
            P.dma("sp", DMA(lq[:, i, :], d[nm][l:l + 1, :].broadcast_to([128, 32])), writes=[lq])
        lam = k.sb(st, [128, 8], F32)
        lp = k.sb(st, [128, 2, 32], F32)
        P.op("dve", TTo(lp[:, 0, :], lq[:, 0, :], lq[:, 1, :], ALU.mult), reads=[lq], writes=[lp])
        P.op("dve", TTo(lp[:, 1, :], lq[:, 2, :], lq[:, 3, :], ALU.mult), reads=[lq], writes=[lp])
        P.op("dve", RED(lam[:, 0:2], lp[:]), reads=[lp], writes=[lam])
        P.op("act", ACT(lam[:, 2:4], lam[:, 0:2], AF.Exp), reads=[lam], writes=[lam])
        P.op("dve", TTo(lam[:, 4:5], lam[:, 3:4], lam[:, 2:3], ALU.subtract), reads=[lam], writes=[lam])
        P.op("dve", TS(lam[:, 5:6], lam[:, 4:5], -lam_init, None, ALU.add), reads=[lam], writes=[lam])
        neglam = lam[:, 5:6]
        subg = k.sb(st, [128, 64], F32)
        load_bcast(k, P, subg, d["d_subnorm"][l:l + 1, :])
        P.op("dve", TS(subg[:], subg[:], 1.0 - lam_init, None, ALU.mult), reads=[subg], writes=[subg])
        qT = k.sb(st, [128, 3, T], BF16, "qTd")
        kT = k.sb(st, [128, 3, T + S], BF16, "kTd")
        qcT = k.sb(st, [128, 3, S], BF16, "qcTd")
        va = k.sb(st, [128, 34, 4, 65], BF16, "vad")
        P.op("pool", MSET(va[:], 1.0), writes=[va])
        srcs = [k.sb(st, [128, 512], BF16, "src") for _ in range(3)]
        outs = [k.sb(st, [128, 512], BF16, "out") for _ in range(2)]
        tabs = [k.sb(st, [128, 2, 2, 8], F32, "tab") for _ in range(3)]
        works = [prep_work(k, st, 512, 16) for _ in range(2)]
        tps = [k.ps(st, [128, 6, 128], BF16, "tpD") for _ in range(2)]
        sps = [k.ps(st, [128, 512], F32, "spD") for _ in range(2)]
        accs = [k.ps(st, [128, 4, 65], F32, "accD") for _ in range(4)]
        pts = [k.sb(st, [128, 512], BF16, "ptD") for _ in range(3)]
        rr = [k.sb(st, [128, 16], F32, "rD") for _ in range(2)]
        o1s = [k.sb(st, [128, 4, 64], F32, "o1D") for _ in range(2)]
        o2s = [k.sb(st, [128, 4, 64], F32, "o2D") for _ in range(2)]
        yds = [k.sb(st, [128, 4, 256], BF16, "ydD") for _ in range(2)]
        for b in range(NB):
            pL, pC = d["pL%d" % b], d["pC%d" % b]
            for i in range(34):
                src, out, tab, w, tp = srcs[i % 3], outs[i % 2], tabs[i % 3], works[i % 2], tps[i % 2]
                lat = i < 32
                pd = pL if lat else pC
                r0 = (i if lat else i - 32) * 128
                P.dma("sp", DMA(src[:], pd[r0:r0 + 128, D_Q:D_Q + 512]), writes=[src])
                P.dma("sp", DMA(va[:, i, :, 0:64], pd[r0:r0 + 128, D_V:D_V + 256].rearrange("p (h e) -> p h e", e=64)),
                      writes=[va])
                if lat:
                    P.dma("sp", DMA(tab[:].rearrange("p c a f -> p (c a f)"), d["ropeD"][r0:r0 + 128, :]), writes=[tab])
                qk_prep(P, "dve", src, 16, 32, gains, tab if lat else None, out, w)
                for u in range(6):
                    qk0 = (u // 3) * 256
                    cw = 96 if u % 3 < 2 else 64
                    cb = qk0 + (u % 3) * 96
                    P.op("pe", TR(tp[0:cw, u, :], out[:, cb:cb + cw], ident[:]), reads=[out, ident], writes=[tp], sig=(u == 5))
                c0 = i * 128
                if lat:
                    P.op("act", ACT(qT[0:96, :, c0:c0 + 128], tp[0:96, 0:3, :], AF.Copy), reads=[tp], writes=[qT])
                else:
                    P.op("act", ACT(qcT[0:96, :, r0:r0 + 128], tp[0:96, 0:3, :], AF.Copy), reads=[tp], writes=[qcT])
                P.op("act", ACT(kT[0:96, :, c0:c0 + 128], tp[0:96, 3:6, :], AF.Copy), reads=[tp], writes=[kT])
            chunks = [(q, True) for q in range(8)] + ([(0, False)] if ctx_out else [])
            isp = 0
            iacc = 0
            for ci, (Q, lat) in enumerate(chunks):
                nq = 512 if lat else 256
                nqt = nq // 128
                keys = list(range(34)) if lat else [32, 33]
                yd = yds[ci % 2]
                for hd in range(4):
                    acc2 = (accs[(iacc % 2) * 2], accs[(iacc % 2) * 2 + 1])
                    r_, o1, o2 = rr[iacc % 2], o1s[iacc % 2], o2s[iacc % 2]
                    iacc += 1
                    for c in range(2):
                        u = hd * 2 + c
                        grp, j = u // 3, u % 3
                        acc = acc2[c]
                        first = True
                        for kt in keys:
                            sp_ = sps[isp % 2]
                            pt = pts[isp % 3]
                            isp += 1
                            if lat:
                                rhs = qT[32 * j:32 * j + 32, grp, Q * 512:Q * 512 + nq]
                                qdep = qT
                            else:
                                rhs = qcT[32 * j:32 * j + 32, grp, 0:nq]
                                qdep = qcT
                            P.op("pe", MM(sp_[:, 0:nq], kT[32 * j:32 * j + 32, grp, kt * 128:(kt + 1) * 128], rhs),
                                 reads=[kT, qdep], writes=[sp_])
                            P.op("act", ACT(pt[:, 0:nq], sp_[:, 0:nq], AF.Exp, scale=scale), reads=[sp_], writes=[pt])
                            for qt in range(nqt):
                                P.op("pe", MM(acc[:, qt, :], pt[:, qt * 128:(qt + 1) * 128], va[:, kt, hd, :], start=first, stop=True),
                                     reads=[pt, va], writes=[acc], sig=(qt == nqt - 1))
                                first = False
                    a0, a1 = acc2
                    P.op("dve", RCP(r_[:, 0:nqt], a0[:, 0:nqt, 64]), reads=[a0], writes=[r_])
                    P.op("dve", RCP(r_[:, 4:4 + nqt], a1[:, 0:nqt, 64]), reads=[a1], writes=[r_])
                    P.op("dve", TS(r_[:, 4:4 + nqt], r_[:, 4:4 + nqt], neglam, None, ALU.mult), reads=[r_, lam], writes=[r_])
                    P.op("dve", TTo(o1[:, 0:nqt, :], a0[:, 0:nqt, 0:64], r_[:, 0:nqt].unsqueeze(2).broadcast_to([128, nqt, 64]), ALU.mult),
                         reads=[a0, r_], writes=[o1])
                    P.op("dve", TTo(o2[:, 0:nqt, :], a1[:, 0:nqt, 0:64], r_[:, 4:4 + nqt].unsqueeze(2).broadcast_to([128, nqt, 64]), ALU.mult),
                         reads=[a1, r_], writes=[o2])
                    P.op("dve", TTo(o1[:, 0:nqt, :], o1[:, 0:nqt, :], o2[:, 0:nqt, :], ALU.add), reads=[o1, o2], writes=[o1])
                    P.op("dve", TTo(o2[:, 0:nqt, :], o1[:, 0:nqt, :], o1[:, 0:nqt, :], ALU.mult), reads=[o1], writes=[o2])
                    P.op("dve", RED(r_[:, 8:8 + nqt], o2[:, 0:nqt, :]), reads=[o2], writes=[r_])
                    P.op("dve", TS(r_[:, 8:8 + nqt], r_[:, 8:8 + nqt], 1.0 / 64, EPS, ALU.mult, ALU.add), reads=[r_], writes=[r_])
                    P.op("act", ACT(r_[:, 12:12 + nqt], r_[:, 8:8 + nqt], AF.Sqrt), reads=[r_], writes=[r_])
                    P.op("dve", RCP(r_[:, 8:8 + nqt], r_[:, 12:12 + nqt]), reads=[r_], writes=[r_])
                    P.op("dve", TTo(o1[:, 0:nqt, :], o1[:, 0:nqt, :], r_[:, 8:8 + nqt].unsqueeze(2).broadcast_to([128, nqt, 64]), ALU.mult),
                         reads=[o1, r_], writes=[o1])
                    P.op("dve", TTo(yd[:, 0:nqt, hd * 64:(hd + 1) * 64], o1[:, 0:nqt, :], subg[:].unsqueeze(1).broadcast_to([128, nqt, 64]), ALU.mult),
                         reads=[o1, subg], writes=[yd])
                ydst = d["ycL%d" % b] if lat else d["ycC%d" % b]
                rows = ydst[Q * 512:Q * 512 + nq, 768:1024].rearrange("(q p) c -> p q c", p=128)
                P.dma("sp", DMA(rows, yd[:, 0:nqt, :]), reads=[yd], writes=[Buf()])
        P.barrier()


def phase_mixC(k, l, ctx_out):
    P = k.P
    d = k.dram
    with contextlib.ExitStack() as st:
        ident = k.sb(st, [128, 128], BF16)
        P.dma("sp", DMA(ident[:], d["ident_bf"]), writes=[ident])
        masks = [k.sb(st, [128, 128], BF16, "mk") for _ in range(2)]
        P.dma("sp", DMA(masks[0][:], d["mask_up"]), writes=[masks[0]])
        P.dma("sp", DMA(masks[1][:], d["mask_low"]), writes=[masks[1]])
        tris = [k.sb(st, [128, 128], F32, "tri") for _ in range(2)]
        P.dma("sp", DMA(tris[0][:], d["tri_f"]), writes=[tris[0]])
        P.dma("sp", DMA(tris[1][:], d["tri_b"]), writes=[tris[1]])
        halfo = k.sb(st, [128, 128], F32)
        P.op("pool", MSET(halfo[:], 0.5), writes=[halfo])
        bg = k.sb(st, [128, 16], F32)
        load_bcast(k, P, bg, d["b_gate"][l:l + 1, :])
        mlg = k.sb(st, [128, 64], F32)
        load_bcast(k, P, mlg, d["ml_norm"][l:l + 1, :])
        hf = k.sb(st, [128, 34, 256], F32, "hf")
        caug = k.sb(st, [128, 2, 65], F32, "caug")
        srcs = [k.sb(st, [128, 768], BF16, "srcC") for _ in range(3)]
        opres = [k.sb(st, [128, 256], BF16, "opre") for _ in range(2)]
        gts = [k.sb(st, [128, 16], F32, "gt") for _ in range(3)]
        vaugs = [k.sb(st, [128, 4, 65], BF16, "vaug") for _ in range(2)]
        for v in vaugs:
            P.op("pool", MSET(v[:], 1.0), writes=[v])
        gws = [k.sb(st, [128, 40], F32, "gw") for _ in range(2)]
        qks = [k.sb(st, [128, 512], BF16, "qk") for _ in range(2)]
        qkTs = [k.sb(st, [128, 4, 128], BF16, "qkT") for _ in range(2)]
        sts = [k.sb(st, [128, 128], BF16, "ST") for _ in range(3)]
        css = [k.sb(st, [128, 2, 65], BF16, "Cs") for _ in range(2)]
        ctmp = k.sb(st, [128, 65], F32, "ctmp")
        hws = [k.sb(st, [128, 16], F32, "hw") for _ in range(2)]
        hts = [k.sb(st, [128, 4, 64], F32, "ht") for _ in range(2)]
        h2s = [k.sb(st, [128, 4, 64], F32, "h2") for _ in range(2)]
        sgs = [k.sb(st, [128, 256], F32, "sg") for _ in range(2)]
        ycs = [k.sb(st, [128, 256], BF16, "yc") for _ in range(2)]
        psg = [k.ps(st, [128, 8], F32, "psg") for _ in range(1)]
        tpc = [k.ps(st, [128, 4, 128], BF16, "tpc") for _ in range(2)]
        sps = [k.ps(st, [128, 128], F32, "spC") for _ in range(2)]
        ups = [k.ps(st, [128, 2, 65], F32, "upC") for _ in range(1)]
        ops = [k.ps(st, [128, 4, 65], F32, "opC") for _ in range(2)]
        it = 0
        ist = 0
        for b in range(NB):
            pL, pC, gL, gC = d["pL%d" % b], d["pC%d" % b], d["gL%d" % b], d["gC%d" % b]
            for dr in range(2):
                order = [32, 33] + list(range(32)) if dr == 0 else [33, 32] + list(range(31, -1, -1))
                icol, fcol = (0, 8) if dr == 0 else (4, 12)
                P.op("dve", MSET(caug[:], 0.0), writes=[caug])
                for i in order:
                    lat = i < 32
                    pd, gd = (pL, gL) if lat else (pC, gC)
                    r0 = (i if lat else i - 32) * 128
                    src, gt, vaug, gw, qk, qkT = srcs[it % 3], gts[it % 3], vaugs[it % 2], gws[it % 2], qks[it % 2], qkTs[it % 2]
                    tp, op_, cs, hw, ht, h2 = tpc[it % 2], ops[it % 2], css[it % 2], hws[it % 2], hts[it % 2], h2s[it % 2]
                    it += 1
                    P.dma("sp", DMA(src[:], pd[r0:r0 + 128, C_Q:C_Q + 768]), writes=[src])
                    P.dma("sp", DMA(gt[:], gd[r0:r0 + 128, :]), writes=[gt])
                    P.op("pool", CP(vaug[:, :, 0:64], src[:, 512:768].rearrange("p (h e) -> p h e", e=64)), reads=[src], writes=[vaug])
                    P.op("dve", TTo(gw[:, 0:4], gt[:, icol:icol + 4], bg[:, icol:icol + 4], ALU.add), reads=[gt, bg], writes=[gw])
                    P.op("dve", TTo(gw[:, 4:8], gt[:, fcol:fcol + 4], bg[:, fcol:fcol + 4], ALU.add), reads=[gt, bg], writes=[gw])
                    P.op("act", ACT(gw[:, 8:12], gw[:, 4:8], AF.Exp, scale=-1.0), reads=[gw], writes=[gw])
                    P.op("act", ACT(gw[:, 12:16], gw[:, 8:12], AF.Ln, bias=1.0), reads=[gw], writes=[gw])
                    pg = psg[0]
                    P.op("pe", MM(pg[:, 0:4], tris[dr][:], gw[:, 12:16], start=True, stop=True), reads=[tris[dr], gw], writes=[pg], sig=False)
                    P.op("pe", MM(pg[:, 4:8], halfo[:], gw[:, 12:16], start=False, stop=True), reads=[halfo, gw], writes=[pg])
                    P.op("act", ACT(gw[:, 16:20], pg[:, 0:4], AF.Exp, scale=-1.0), reads=[pg], writes=[gw])
                    P.op("dve", TTo(gw[:, 20:24], pg[:, 0:4], gw[:, 0:4], ALU.add), reads=[pg, gw], writes=[gw])
                    P.op("act", ACT(gw[:, 24:28], gw[:, 20:24], AF.Exp, bias=math.log(0.125)), reads=[gw], writes=[gw])
                    P.op("act", ACT(gw[:, 28:32], pg[:, 4:8], AF.Exp, scale=-1.0), reads=[pg], writes=[gw])
                    P.op("act", ACT(gw[:, 32:36], pg[:, 4:8], AF.Exp, scale=-2.0), reads=[pg], writes=[gw])
                    P.op("dve", TTo(qk[:, 0:256].rearrange("p (h e) -> p h e", e=64), src[:, 0:256].rearrange("p (h e) -> p h e", e=64),
                                    gw[:, 16:20].unsqueeze(2).broadcast_to([128, 4, 64]), ALU.mult), reads=[src, gw], writes=[qk])
                    P.op("dve", TTo(qk[:, 256:512].rearrange("p (h e) -> p h e", e=64), src[:, 256:512].rearrange("p (h e) -> p h e", e=64),
                                    gw[:, 24:28].unsqueeze(2).broadcast_to([128, 4, 64]), ALU.mult), reads=[src, gw], writes=[qk])
                    for u in range(4):
                        P.op("pe", TR(tp[:, u, :], qk[:, u * 128:(u + 1) * 128], ident[:]), reads=[qk, ident], writes=[tp], sig=(u == 3))
                    P.op("act", ACT(qkT[:], tp[:], AF.Copy), reads=[tp], writes=[qkT])
                    for hc in range(4):
                        pp, off = hc // 2, (hc % 2) * 64
                        P.op("dve", TS(cs[off:off + 64, pp, :], caug[off:off + 64, pp, :], gw[off:off + 64, 28 + hc:29 + hc], None, ALU.mult),
                             reads=[caug, gw], writes=[cs])
                    first = True
                    for hc in range(4):
                        pp, off = hc // 2, (hc % 2) * 64
                        sp_ = sps[ist % 2]
                        ST = sts[ist % 3]
                        ist += 1
                        P.op("pe", MM(sp_[:], qkT[off:off + 64, 2 + pp, :], qkT[off:off + 64, pp, :]), reads=[qkT], writes=[sp_])
                        P.op("dve", TTo(ST[:], sp_[:], masks[dr][:], ALU.mult), reads=[sp_, masks[dr]], writes=[ST])
                        P.op("pe", MM(op_[:, hc, :], ST[:], vaug[:, hc, :], start=first, stop=True), reads=[ST, vaug], writes=[op_], sig=False)
                        first = False
                        P.op("pe", MM(op_[:, hc, :], qkT[off:off + 64, pp, :], cs[off:off + 64, pp, :], start=False, stop=True),
                             reads=[qkT, cs], writes=[op_])
                    up = ups[0]
                    for pp in range(2):
                        P.op("pe", MM(up[:].rearrange("p a e -> p (a e)"), qk[:, 256 + pp * 128:256 + (pp + 1) * 128],
                                      vaug[:, 2 * pp:2 * pp + 2, :].rearrange("p a e -> p (a e)"), start=True, stop=True),
                             reads=[qk, vaug], writes=[up])
                        for hh in range(2):
                            hc, off = 2 * pp + hh, hh * 64
                            P.op("dve", TS(ctmp[off:off + 64, :], caug[off:off + 64, pp, :], gw[off:off + 64, 32 + hc:33 + hc], None, ALU.mult),
                                 reads=[caug, gw], writes=[ctmp])
                            P.op("dve", STT(caug[off:off + 64, pp, :], up[off:off + 64, hh, :], gw[off:off + 64, 28 + hc:29 + hc],
                                            ctmp[off:off + 64, :], ALU.mult, ALU.add), reads=[up, gw, ctmp], writes=[caug])
                    P.op("act", ACT(hw[:, 0:4], op_[:, :, 64], AF.Abs), reads=[op_], writes=[hw])
                    P.op("dve", TS(hw[:, 0:4], hw[:, 0:4], 1.0, None, ALU.max), reads=[hw], writes=[hw])
                    P.op("dve", RCP(hw[:, 4:8], hw[:, 0:4]), reads=[hw], writes=[hw])
                    if dr == 0:
                        P.op("dve", TTo(hf[:, i, :].rearrange("p (h e) -> p h e", e=64), op_[:, :, 0:64],
                                        hw[:, 4:8].unsqueeze(2).broadcast_to([128, 4, 64]), ALU.mult), reads=[op_, hw], writes=[hf])
                        continue
                    P.op("dve", TTo(ht[:], op_[:, :, 0:64], hw[:, 4:8].unsqueeze(2).broadcast_to([128, 4, 64]), ALU.mult),
                         reads=[op_, hw], writes=[ht])
                    if (not lat) and (not ctx_out):
                        continue
                    opre, sg, yc = opres[it % 2], sgs[it % 2], ycs[it % 2]
                    P.dma("sp", DMA(opre[:], pd[r0:r0 + 128, C_O:C_O + 256]), writes=[opre])
                    P.op("act", ACT(sg[:], opre[:], AF.Sigmoid), reads=[opre], writes=[sg])
                    P.op("dve", TTo(ht[:], ht[:], hf[:, i, :].rearrange("p (h e) -> p h e", e=64), ALU.add), reads=[ht, hf], writes=[ht])
                    P.op("pool", TTo(h2[:], ht[:], ht[:], ALU.mult), reads=[ht], writes=[h2])
                    P.op("dve", RED(hw[:, 8:12], h2[:]), reads=[h2], writes=[hw])
                    P.op("dve", TS(hw[:, 8:12], hw[:, 8:12], 1.0 / 64, EPS, ALU.mult, ALU.add), reads=[hw], writes=[hw])
                    P.op("act", ACT(hw[:, 12:16], hw[:, 8:12], AF.Sqrt), reads=[hw], writes=[hw])
                    P.op("dve", RCP(hw[:, 8:12], hw[:, 12:16]), reads=[hw], writes=[hw])
                    P.op("dve", TTo(ht[:], ht[:], hw[:, 8:12].unsqueeze(2).broadcast_to([128, 4, 64]), ALU.mult), reads=[ht, hw], writes=[ht])
                    P.op("pool", TTo(ht[:], ht[:], mlg[:].unsqueeze(1).broadcast_to([128, 4, 64]), ALU.mult), reads=[ht, mlg], writes=[ht])
                    P.op("dve", TTo(yc[:], ht[:].rearrange("p h e -> p (h e)"), sg[:], ALU.mult), reads=[ht, sg], writes=[yc])
                    ydst = d["ycL%d" % b] if lat else d["ycC%d" % b]
                    P.dma("sp", DMA(ydst[r0:r0 + 128, 512:768], yc[:]), reads=[yc], writes=[Buf()])
        P.barrier()


HY_LS = (("L", T), ("C", S))
TWO_PI = 2.0 * math.pi


def phase_hy_filter(k, l, tag, L):
    P = k.P
    d = k.dram
    nt = L // 128
    cw = min(512, L)
    ks = k.dt("KS" + tag, [L, 2, 512], F32)
    with contextlib.ExitStack() as st:
        fw1 = k.sb(st, [33, 64], F32)
        fw2 = k.sb(st, [64, 64], F32)
        fw3 = k.sb(st, [64, 1024], F32)
        P.dma("sp", DMA(fw1[:], d["hy_fw1"][l]), writes=[fw1])
        P.dma("sp", DMA(fw2[:], d["hy_fw2"][l]), writes=[fw2])
        P.dma("sp", DMA(fw3[:], d["hy_fw3"][l]), writes=[fw3])
        col = k.sb(st, [64, 8], F32)
        with k.nc.allow_non_contiguous_dma(reason="tiny per-partition vectors"):
            for i, nm in enumerate(("hy_freq", "hy_fb1", "hy_fb2")):
                P.dma("sp", DMA(col[:, i:i + 1], d[nm][l].rearrange("(p o) -> p o", o=1)), writes=[col])
        P.op("dve", TS(col[:, 0:1], col[:, 0:1], 1.0 / TWO_PI, None, ALU.mult), reads=[col], writes=[col])
        P.op("dve", TTo(col[:, 3:4], col[:, 0:1], col[:, 1:2], ALU.mult), reads=[col], writes=[col])
        P.op("dve", TTo(col[:, 4:5], col[:, 0:1], col[:, 2:3], ALU.mult), reads=[col], writes=[col])
        ones = k.sb(st, [128, 128], BF16)
        P.op("pool", MSET(ones[:], 1.0), writes=[ones])
        alt = k.sb(st, [128, nt], BF16)
        P.dma("sp", DMA(alt[:], d["alt" + tag]), writes=[alt])
        h2T = k.sb(st, [64, L], F32, "h2T")
        sumT = k.sb(st, [128, nt, 512], BF16, "sumT")
        difT = k.sb(st, [128, nt, 512], BF16, "difT")
        zts = [k.sb(st, [33, cw], F32, "zt") for _ in range(2)]
        a1s = [k.sb(st, [64, cw], F32, "a1") for _ in range(2)]
        h1s = [k.sb(st, [64, cw], F32, "h1") for _ in range(2)]
        psm = [k.ps(st, [128, 512], F32, "psm") for _ in range(4)]
        nrm = k.ps(st, [128, 512], F32, "nrm")
        nyq = k.ps(st, [128, 512], F32, "nyq")
        ip = 0
        for c in range(L // cw):
            zt, a1, h1 = zts[c % 2], a1s[c % 2], h1s[c % 2]
            P.dma("sp", DMA(zt[:], d["zT" + tag][:, c * cw:(c + 1) * cw]), writes=[zt])
            for (w_, src_, dst_, bcol) in ((fw1, zt, h1, 3), (fw2, h1, None, 4)):
                ps = psm[ip % 4]
                ip += 1
                kk = 33 if w_ is fw1 else 64
                P.op("pe", MM(ps[0:64, 0:cw], w_[0:kk, :], src_[0:kk, :]), reads=[w_, src_], writes=[ps])
                P.op("dve", TS(a1[:], ps[0:64, 0:cw], col[:, 0:1], col[:, bcol:bcol + 1], ALU.mult, ALU.add), reads=[ps, col], writes=[a1])
                for _ in range(2):
                    P.op("dve", STT(a1[:], a1[:], -0.5, a1[:], ALU.is_lt, ALU.add), reads=[a1], writes=[a1])
                    P.op("dve", STT(a1[:], a1[:], 0.5, a1[:], ALU.is_gt, ALU.subtract), reads=[a1], writes=[a1])
                sc_ = TWO_PI * (1.0 - 1e-6)
                if dst_ is not None:
                    P.op("act", ACT(dst_[:], a1[:], AF.Sin, scale=sc_), reads=[a1], writes=[dst_])
                else:
                    P.op("act", ACT(h2T[:, c * cw:(c + 1) * cw], a1[:], AF.Sin, scale=sc_), reads=[a1], writes=[h2T])
        decs = [k.sb(st, [128, 256], F32, "dec") for _ in range(2)]
        hds = [k.sb(st, [128, 2, 512], F32, "hd") for _ in range(2)]
        sqs = [k.sb(st, [128, 2, 512], BF16, "sqh") for _ in range(2)]
        for lt in range(nt):
            dec, hd, sq = decs[lt % 2], hds[lt % 2], sqs[lt % 2]
            P.dma("sp", DMA(dec[:], d["dec" + tag][lt * 128:(lt + 1) * 128, :]), writes=[dec])
            for dr in range(2):
                ps = psm[ip % 4]
                ip += 1
                P.op("pe", MM(ps[:], h2T[:, lt * 128:(lt + 1) * 128], fw3[:, dr * 512:(dr + 1) * 512]), reads=[h2T, fw3], writes=[ps])
                P.op("dve", TTo(hd[:, dr, :].rearrange("p (o c) -> p o c", o=2), ps[:].rearrange("p (o c) -> p o c", o=2),
                                dec[:].unsqueeze(1).broadcast_to([128, 2, 256]), ALU.mult), reads=[ps, dec], writes=[hd])
            if lt == 0:
                P.op("dve", MSET(hd[0:1, 1, :], 0.0), writes=[hd])
            P.op("dve", TTo(sumT[:, lt, :], hd[:, 0, :], hd[:, 1, :], ALU.add), reads=[hd], writes=[sumT])
            P.op("pool", TTo(difT[:, lt, :], hd[:, 0, :], hd[:, 1, :], ALU.subtract), reads=[hd], writes=[difT])
            P.op("pool", TTo(sq[:], hd[:], hd[:], ALU.mult), reads=[hd], writes=[sq])
            for dr in range(2):
                P.op("pe", MM(nrm[:], ones[:], sq[:, dr, :], start=(lt == 0 and dr == 0), stop=True), reads=[ones, sq], writes=[nrm], sig=(dr == 1))
            P.op("pe", MM(nyq[0:1, :], alt[:, lt:lt + 1], sumT[:, lt, :], start=(lt == 0), stop=True), reads=[alt, sumT], writes=[nyq])
        rn = k.sb(st, [128, 512], F32, "rn")
        rn2 = k.sb(st, [128, 512], F32, "rn2")
        nq = k.sb(st, [1, 512], F32, "nq")
        P.op("dve", TS(rn[:], nrm[:], EPS, None, ALU.add), reads=[nrm], writes=[rn])
        P.op("act", ACT(rn2[:], rn[:], AF.Sqrt), reads=[rn], writes=[rn2])
        P.op("dve", RCP(rn[:], rn2[:]), reads=[rn2], writes=[rn])
        P.op("dve", TS(rn[:], rn[:], 1.0 / L, None, ALU.mult), reads=[rn], writes=[rn])
        P.op("dve", CP(nq[:], nyq[0:1, :]), reads=[nyq], writes=[nq])
        cbs = [k.sb(st, [128, nt, 128], BF16, "cb") for _ in range(2)]
        sbs = [k.sb(st, [128, nt, 128], BF16, "sb") for _ in range(2)]
        kos = [k.sb(st, [128, 2, 512], F32, "ko") for _ in range(2)]
        for ft in range(nt):
            cb, sb_, ko = cbs[ft % 2], sbs[ft % 2], kos[ft % 2]
            P.dma("sp", DMA(cb[:].rearrange("p s f -> p (s f)"), d["Cm" + tag][ft].rearrange("p s f -> p (s f)")), writes=[cb])
            P.dma("sp", DMA(sb_[:].rearrange("p s f -> p (s f)"), d["Sm" + tag][ft].rearrange("p s f -> p (s f)")), writes=[sb_])
            pr, pi = psm[(2 * ft) % 4], psm[(2 * ft + 1) % 4]
            for s_ in range(nt):
                P.op("pe", MM(pr[:], cb[:, s_, :], sumT[:, s_, :], start=(s_ == 0), stop=(s_ == nt - 1)), reads=[cb, sumT], writes=[pr], sig=(s_ == nt - 1))
            for s_ in range(nt):
                P.op("pe", MM(pi[:], sb_[:, s_, :], difT[:, s_, :], start=(s_ == 0), stop=(s_ == nt - 1)), reads=[sb_, difT], writes=[pi], sig=(s_ == nt - 1))
            P.op("dve", TTo(ko[:, 0, :], pr[:], rn[:], ALU.mult), reads=[pr, rn], writes=[ko])
            P.op("dve", TTo(ko[:, 1, :], pi[:], rn[:], ALU.mult), reads=[pi, rn], writes=[ko])
            if ft == 0:
                P.op("dve", TS(ko[0:1, 0, :], ko[0:1, 0, :], 0.5, None, ALU.mult), reads=[ko], writes=[ko])
                P.op("dve", STT(ko[0:1, 1, :], nq[0:1, :], 0.5, rn[0:1, :], ALU.mult, ALU.mult), reads=[nq, rn], writes=[ko])
            P.dma("sp", DMA(ks[ft * 128:(ft + 1) * 128].rearrange("p s f -> p (s f)"), ko[:].rearrange("p s f -> p (s f)")), reads=[ko], writes=[Buf()])
        P.barrier()


def phase_hy_signal(k, l, tag, L):
    P = k.P
    d = k.dram
    nt = L // 128
    ks = k.dt("KS" + tag, [L, 2, 512], F32)
    hg = k.dt("HG" + tag, [L, 2, 512], F32)
    with contextlib.ExitStack() as st:
        wc = k.sb(st, [128, 3, 768], F32, "wc")
        for j in range(3):
            P.dma("sp", DMA(wc[:, j, :], d["hy_conv"][l, j:j + 1, :].broadcast_to([128, 768])), writes=[wc])
        hb = k.sb(st, [128, 2, 256], F32, "hbias")
        for n in range(2):
            P.dma("sp", DMA(hb[:, n, :], d["hy_bias"][l, n:n + 1, :].broadcast_to([128, 256])), writes=[hb])
        z = k.sb(st, [128, nt, 512], BF16, "z")
        zbufs = [Buf() for _ in range(nt)]
        yr = k.sb(st, [128, nt, 512], BF16, "yr")
        yi = k.sb(st, [128, nt, 512], BF16, "yi")
        uas = [k.sb(st, [128, 3, 768], BF16, "ua") for _ in range(2)]
        ucs = [k.sb(st, [128, 768], F32, "uc") for _ in range(2)]
        uts = [k.sb(st, [128, 768], F32, "ut") for _ in range(2)]
        ggs = [k.sb(st, [128, 2, 512], F32, "gg") for _ in range(2)]
        it = 0
        for i in range(nt):
            gg = ggs[i % 2]
            for b in range(NB):
                pd = d["p%s%d" % (tag, b)]
                ua, uc, ut = uas[it % 2], ucs[it % 2], uts[it % 2]
                it += 1
                r0 = i * 128
                if i == 0 or i == nt - 1:
                    P.op("pool", MSET(ua[:], 0.0), writes=[ua])
                if i == 0:
                    P.dma("sp", DMA(ua[1:128, 0, :], pd[0:127, B_U:B_U + 768]), writes=[ua])
                else:
                    P.dma("sp", DMA(ua[:, 0, :], pd[r0 - 1:r0 + 127, B_U:B_U + 768]), writes=[ua])
                P.dma("sp", DMA(ua[:, 1, :], pd[r0:r0 + 128, B_U:B_U + 768]), writes=[ua])
                if i == nt - 1:
                    P.dma("sp", DMA(ua[0:127, 2, :], pd[r0 + 1:r0 + 128, B_U:B_U + 768]), writes=[ua])
                else:
                    P.dma("sp", DMA(ua[:, 2, :], pd[r0 + 1:r0 + 129, B_U:B_U + 768]), writes=[ua])
                P.op("dve", TTo(uc[:], ua[:, 0, :], wc[:, 0, :], ALU.mult), reads=[ua, wc], writes=[uc])
                P.op("pool", TTo(ut[:], ua[:, 1, :], wc[:, 1, :], ALU.mult), reads=[ua, wc], writes=[ut])
                P.op("dve", TTo(uc[:], uc[:], ut[:], ALU.add), reads=[uc, ut], writes=[uc])
                P.op("pool", TTo(ut[:], ua[:, 2, :], wc[:, 2, :], ALU.mult), reads=[ua, wc], writes=[ut])
                P.op("dve", TTo(z[:, i, b * 256:(b + 1) * 256], uc[:, 0:256], ut[:, 0:256], ALU.add), reads=[uc, ut], writes=[zbufs[i]])
                P.op("dve", TTo(gg[:, :, b * 256:(b + 1) * 256], uc[:, 256:768].rearrange("p (n c) -> p n c", n=2),
                                ut[:, 256:768].rearrange("p (n c) -> p n c", n=2), ALU.add), reads=[uc, ut], writes=[gg])
            P.dma("sp", DMA(hg[i * 128:(i + 1) * 128].rearrange("p s f -> p (s f)"), gg[:].rearrange("p s f -> p (s f)")), reads=[gg], writes=[Buf()])
        P.barrier()
        cbs = [k.sb(st, [128, nt, 128], BF16, "cb") for _ in range(2)]
        sbs = [k.sb(st, [128, nt, 128], BF16, "sb") for _ in range(2)]
        kts = [k.sb(st, [128, 2, 256], F32, "kt") for _ in range(2)]
        t1s = [k.sb(st, [128, 512], F32, "t1") for _ in range(2)]
        t2s = [k.sb(st, [128, 512], F32, "t2") for _ in range(2)]
        gts = [k.sb(st, [128, 512], F32, "gt") for _ in range(2)]
        ybs = [k.sb(st, [128, 512], BF16, "yb") for _ in range(2)]
        psm = [k.ps(st, [128, 512], F32, "psh") for _ in range(4)]
        yrb, yib = Buf(), Buf()
        for n in range(2):
            for ft in range(nt):
                cb, sb_, kt, t1, t2 = cbs[ft % 2], sbs[ft % 2], kts[ft % 2], t1s[ft % 2], t2s[ft % 2]
                P.dma("sp", DMA(cb[:].rearrange("p s f -> p (s f)"), d["Cm" + tag][ft].rearrange("p s f -> p (s f)")), writes=[cb])
                P.dma("sp", DMA(sb_[:].rearrange("p s f -> p (s f)"), d["Sm" + tag][ft].rearrange("p s f -> p (s f)")), writes=[sb_])
                P.dma("sp", DMA(kt[:], ks[ft * 128:(ft + 1) * 128, :, n * 256:(n + 1) * 256]), writes=[kt])
                pr, pi = psm[(2 * ft) % 4], psm[(2 * ft + 1) % 4]
                for s_ in range(nt):
                    P.op("pe", MM(pr[:], cb[:, s_, :], z[:, s_, :], start=(s_ == 0), stop=(s_ == nt - 1)), reads=[cb, zbufs[s_]], writes=[pr], sig=(s_ == nt - 1))
                for s_ in range(nt):
                    P.op("pe", MM(pi[:], sb_[:, s_, :], z[:, s_, :], start=(s_ == 0), stop=(s_ == nt - 1)), reads=[sb_, zbufs[s_]], writes=[pi], sig=(s_ == nt - 1))
                kr = kt[:, 0, :].unsqueeze(1).broadcast_to([128, 2, 256])
                ki = kt[:, 1, :].unsqueeze(1).broadcast_to([128, 2, 256])
                v3 = lambda a: a.rearrange("p (b c) -> p b c", b=2)
                P.op("dve", TTo(v3(t1[:]), v3(pr[:]), kr, ALU.mult), reads=[pr, kt], writes=[t1])
                P.op("dve", TTo(v3(t2[:]), v3(pi[:]), ki, ALU.mult), reads=[pi, kt], writes=[t2])
                P.op("pool", TTo(yr[:, ft, :], t1[:], t2[:], ALU.subtract), reads=[t1, t2], writes=[yrb])
                t3, t4 = t1s[(ft + 1) % 2], t2s[(ft + 1) % 2]
                P.op("dve", TTo(v3(t3[:]), v3(pr[:]), ki, ALU.mult), reads=[pr, kt], writes=[t3])
                P.op("dve", TTo(v3(t4[:]), v3(pi[:]), kr, ALU.mult), reads=[pi, kt], writes=[t4])
                P.op("pool", TTo(yi[:, ft, :], t3[:], t4[:], ALU.add), reads=[t3, t4], writes=[yib])
                if ft == 0:
                    P.op("pool", CP(yr[0:1, 0, :], t1[0:1, :]), reads=[t1], writes=[yrb])
                    P.op("pool", CP(yi[0:1, 0, :], t2[0:1, :]), reads=[t2], writes=[yib])
            for tt in range(nt):
                cb, sb_, gt, t1, yb = cbs[tt % 2], sbs[tt % 2], gts[tt % 2], t1s[tt % 2], ybs[tt % 2]
                P.dma("sp", DMA(cb[:].rearrange("p s f -> p (s f)"), d["Cm" + tag][tt].rearrange("p s f -> p (s f)")), writes=[cb])
                P.dma("sp", DMA(sb_[:].rearrange("p s f -> p (s f)"), d["Si" + tag][tt].rearrange("p s f -> p (s f)")), writes=[sb_])
                P.dma("sp", DMA(gt[:], hg[tt * 128:(tt + 1) * 128, n, :]), writes=[gt])
                py = psm[tt % 4]
                for f_ in range(nt):
                    P.op("pe", MM(py[:], cb[:, f_, :], yr[:, f_, :], start=(f_ == 0), stop=False), reads=[cb, yrb], writes=[py], sig=False)
                for f_ in range(nt):
                    P.op("pe", MM(py[:], sb_[:, f_, :], yi[:, f_, :], start=False, stop=(f_ == nt - 1)), reads=[sb_, yib], writes=[py], sig=(f_ == nt - 1))
                v3 = lambda a: a.rearrange("p (b c) -> p b c", b=2)
                P.op("dve", TTo(v3(t1[:]), v3(z[:, tt, :]), hb[:, n, :].unsqueeze(1).broadcast_to([128, 2, 256]), ALU.mult),
                     reads=[zbufs[tt], hb], writes=[t1])
                P.op("dve", TTo(t1[:], t1[:], py[:], ALU.add), reads=[t1, py], writes=[t1])
                if n == 0:
                    P.op("dve", TTo(z[:, tt, :], t1[:], gt[:], ALU.mult), reads=[t1, gt], writes=[zbufs[tt]])
                else:
                    P.op("dve", TTo(yb[:], t1[:], gt[:], ALU.mult), reads=[t1, gt], writes=[yb])
                    for b in range(NB):
                        P.dma("sp", DMA(d["yc%s%d" % (tag, b)][tt * 128:(tt + 1) * 128, 256:512], yb[:, b * 256:(b + 1) * 256]),
                              reads=[yb], writes=[Buf()])
        P.barrier()


SP = S + 128


def phase_out_router(k, l, ctx_out):
    P = k.P
    d = k.dram
    with contextlib.ExitStack() as st:
        ident = k.sb(st, [128, 128], BF16)
        P.dma("sp", DMA(ident[:], d["ident_bf"]), writes=[ident])
        identf = k.sb(st, [128, 128], F32)
        P.dma("sp", DMA(identf[:], d["ident_f"]), writes=[identf])
        wout = k.sb(st, [128, 8, D], BF16, "wout")
        wob = [Buf(), Buf()]
        for c in range(2):
            src = d["w_out"][l, :, c * 512:(c + 1) * 512].rearrange("(k p) n -> p k n", p=128)
            P.dma("pool", DMA(wout[:, :, c * 512:(c + 1) * 512], src), writes=[wob[c]])
        wr = k.sb(st, [128, 8, NE], BF16, "wr")
        P.dma("pool", DMA(wr[:], d["w_router"][l].rearrange("(k p) n -> p k n", p=128)), writes=[wr])
        g2 = k.sb(st, [128, D], F32)
        load_bcast(k, P, g2, d["norm2_g"][l:l + 1, :])
        gt1 = k.sb(st, [128, D], F32)
        A2 = k.sb(st, [128, D], F32)
        sh2 = k.sb(st, [128, D], F32)
        sc2 = k.sb(st, [128, D], F32)
        ycs = [k.sb(st, [128, D], BF16, "ycin") for _ in range(3)]
        yTs = [k.sb(st, [128, 8, 128], BF16, "yT") for _ in range(2)]
        xts = [k.sb(st, [128, D], F32, "xo") for _ in range(3)]
        tms = [k.sb(st, [128, D], F32, "tm") for _ in range(2)]
        hbs = [k.sb(st, [128, D], BF16, "h2b") for _ in range(2)]
        hTs = [k.sb(st, [128, 8, 128], BF16, "h2T") for _ in range(2)]
        sss = [k.sb(st, [128, 4], F32, "ss2") for _ in range(3)]
        junk = k.sb(st, [128, D], BF16, "junk2")
        lgs = [k.sb(st, [128, 40], F32, "lg") for _ in range(2)]
        affs = [k.sb(st, [16, 512], F32, "affs") for _ in range(2)]
        tps = [k.ps(st, [128, 8, 128], BF16, "tpo") for _ in range(2)]
        accs = [k.ps(st, [128, 512], F32, "acco") for _ in range(3)]
        lps = [k.ps(st, [128, 16], F32, "lpo") for _ in range(1)]
        aps_ = [k.ps(st, [16, 512], F32, "apo") for _ in range(2)]
        it = 0
        ia = 0
        for (tag, ntok, xs, mrow) in streams(k):
            if tag[0] == "C" and not ctx_out:
                continue
            load_bcast(k, P, gt1, d["mod"][l, mrow:mrow + 1, 2 * D:3 * D])
            load_bcast(k, P, sc2, d["mod"][l, mrow:mrow + 1, 4 * D:5 * D])
            load_bcast(k, P, sh2, d["mod"][l, mrow:mrow + 1, 3 * D:4 * D])
            P.op("dve", STT(A2[:], sc2[:], 1.0, g2[:], ALU.add, ALU.mult), reads=[sc2, g2], writes=[A2])
            yc_d = d["yc" + tag]
            h2_d = d["h2" + tag]
            aff_d = d["aff" + tag]
            gsz = min(4, ntok // 128)
            for i in range(ntok // 128):
                yc, yT, xt, tm, hb, hT, ss, tp, lg = (ycs[it % 3], yTs[it % 2], xts[it % 3], tms[it % 2], hbs[it % 2],
                                                      hTs[it % 2], sss[it % 3], tps[it % 2], lgs[it % 2])
                it += 1
                r0 = i * 128
                P.dma("sp", DMA(yc[:], yc_d[r0:r0 + 128, :]), writes=[yc])
                P.dma("sp", DMA(xt[:], xs[r0:r0 + 128, :]), writes=[xt])
                for kk in range(8):
                    P.op("pe", TR(tp[:, kk, :], yc[:, kk * 128:(kk + 1) * 128], ident[:]), reads=[yc, ident], writes=[tp], sig=(kk == 7))
                P.op("act", ACT(yT[:], tp[:], AF.Copy), reads=[tp], writes=[yT])
                for c in range(2):
                    acc = accs[ia % 3]
                    ia += 1
                    for kk in range(8):
                        P.op("pe", MM(acc[:], yT[:, kk, :], wout[:, kk, c * 512:(c + 1) * 512], start=(kk == 0), stop=(kk == 7)),
                             reads=[yT, wob[c]], writes=[acc], sig=(kk == 7))
                    P.op("dve", TTo(tm[:, c * 512:(c + 1) * 512], acc[:], gt1[:, c * 512:(c + 1) * 512], ALU.mult), reads=[acc, gt1], writes=[tm])
                P.op("pool", TTo(xt[:], xt[:], tm[:], ALU.add), reads=[xt, tm], writes=[xt])
                P.dma("sp", DMA(xs[r0:r0 + 128, :], xt[:]), reads=[xt], writes=[Buf()])
                P.op("act", ACT(junk[:], xt[:], AF.Square, accum_out=ss[:, 0:1]), reads=[xt], writes=[junk, ss])
                P.op("dve", TS(ss[:, 1:2], ss[:, 0:1], 1.0 / D, EPS, ALU.mult, ALU.add), reads=[ss], writes=[ss])
                P.op("act", ACT(ss[:, 2:3], ss[:, 1:2], AF.Sqrt), reads=[ss], writes=[ss])
                P.op("dve", RCP(ss[:, 3:4], ss[:, 2:3]), reads=[ss], writes=[ss])
                P.op("dve", STT(tm[:], xt[:], ss[:, 3:4], A2[:], ALU.mult, ALU.mult), reads=[xt, ss, A2], writes=[tm])
                P.op("pool", TTo(hb[:], tm[:], sh2[:], ALU.add), reads=[tm, sh2], writes=[hb])
                P.dma("sp", DMA(h2_d[r0:r0 + 128, :], hb[:]), reads=[hb], writes=[Buf()])
                for kk in range(8):
                    P.op("pe", TR(tp[:, kk, :], hb[:, kk * 128:(kk + 1) * 128], ident[:]), reads=[hb, ident], writes=[tp], sig=(kk == 7))
                P.op("act", ACT(hT[:], tp[:], AF.Copy), reads=[tp], writes=[hT])
                lp = lps[0]
                for kk in range(8):
                    P.op("pe", MM(lp[:], hT[:, kk, :], wr[:, kk, :], start=(kk == 0), stop=(kk == 7)), reads=[hT, wr], writes=[lp], sig=(kk == 7))
                P.op("dve", RED(lg[:, 32:33], lp[:], op=ALU.max), reads=[lp], writes=[lg])
                P.op("dve", TS(lg[:, 33:34], lg[:, 32:33], -1.0, None, ALU.mult), reads=[lg], writes=[lg])
                P.op("act", ACT(lg[:, 0:16], lp[:], AF.Exp, bias=lg[:, 33:34], accum_out=lg[:, 34:35]), reads=[lp, lg], writes=[lg])
                P.op("dve", RCP(lg[:, 35:36], lg[:, 34:35]), reads=[lg], writes=[lg])
                P.op("dve", TS(lg[:, 16:32], lg[:, 0:16], lg[:, 35:36], None, ALU.mult), reads=[lg], writes=[lg])
                gi = i % gsz
                ap_ = aps_[(i // gsz) % 2]
                af = affs[(i // gsz) % 2]
                P.op("pe", TR(ap_[:, gi * 128:(gi + 1) * 128], lg[:, 16:32], identf[:]), reads=[lg, identf], writes=[ap_])
                if gi == gsz - 1:
                    n_ = gsz * 128
                    c0 = (i // gsz) * n_
                    P.op("act", ACT(af[:, 0:n_], ap_[:, 0:n_], AF.Copy), reads=[ap_], writes=[af])
                    P.dma("sp", DMA(aff_d[:, c0:c0 + n_], af[:, 0:n_]), reads=[af], writes=[Buf()])
        P.barrier()


def phase_topk(k, l, tag, ntok, cap):
    P = k.P
    d = k.dram
    nti = max(1, cap // 128)
    rows = min(cap, 128)
    with contextlib.ExitStack() as st:
        identf = k.sb(st, [128, 128], F32)
        P.dma("sp", DMA(identf[:], d["ident_f"]), writes=[identf])
        work = k.sb(st, [64, ntok], F32, "tkw")
        P.op("pool", MSET(work[:], 0.0), writes=[work])
        for b in range(NB):
            P.dma("sp", DMA(work[32 * b:32 * b + 16, :], d["aff%s%d" % (tag, b)]), writes=[work])
        vals = k.sb(st, [64, cap], F32, "tkv")
        idx = k.sb(st, [64, cap], U32, "tki")
        idf = k.sb(st, [64, cap], F32, "tkif")
        for r in range(cap // 8):
            sl = slice(r * 8, (r + 1) * 8)
            P.op("dve", (lambda o, i_: (lambda e: e.max(out=o, in_=i_)))(vals[:, sl], work[:]), reads=[work], writes=[vals])
            P.op("dve", (lambda o, m, i_: (lambda e: e.max_index(out=o, in_max=m, in_values=i_)))(idx[:, sl], vals[:, sl], work[:]),
                 reads=[work, vals], writes=[idx])
            P.op("dve", (lambda o, m, i_: (lambda e: e.match_replace(out=o, in_to_replace=m, in_values=i_, imm_value=-1.0)))(work[:], vals[:, sl], work[:]),
                 reads=[vals, work], writes=[work])
        P.op("dve", CP(idf[:], idx[:]), reads=[idx], writes=[idf])
        idT = k.sb(st, [128, nti, 64], I32, "idT")
        gT = k.sb(st, [128, nti, 64], F32, "gT")
        if rows < 128:
            P.op("pool", (lambda o: (lambda e: e.iota(o, pattern=[[0, 64]], base=S, channel_multiplier=1)))(idT[:, 0, :]), writes=[idT])
            P.op("pool", MSET(gT[:], 0.0), writes=[gT])
        tp = k.ps(st, [128, 2, 64], F32, "tkp")
        for j in range(nti):
            P.op("pe", TR(tp[0:rows, 0, :], idf[:, j * rows:(j + 1) * rows], identf[0:64, 0:64]), reads=[idf, identf], writes=[tp], sig=False)
            P.op("pe", TR(tp[0:rows, 1, :], vals[:, j * rows:(j + 1) * rows], identf[0:64, 0:64]), reads=[vals, identf], writes=[tp])
            P.op("dve", CP(idT[0:rows, j, :], tp[0:rows, 0, :]), reads=[tp], writes=[idT])
            P.op("dve", CP(gT[0:rows, j, :], tp[0:rows, 1, :]), reads=[tp], writes=[gT])
        P.dma("sp", DMA(d["idx" + tag].rearrange("p s f -> p (s f)"), idT[:].rearrange("p s f -> p (s f)")), reads=[idT], writes=[Buf()])
        P.dma("sp", DMA(d["gate" + tag].rearrange("p s f -> p (s f)"), gT[:].rearrange("p s f -> p (s f)")), reads=[gT], writes=[Buf()])
        P.barrier()


def phase_moe(k, l, ctx_out):
    P = k.P
    d = k.dram
    with contextlib.ExitStack() as st:
        ident = k.sb(st, [128, 128], BF16)
        P.dma("sp", DMA(ident[:], d["ident_bf"]), writes=[ident])
        idxL = k.sb(st, [128, 4, 64], I32, "idxL")
        gateL = k.sb(st, [128, 4, 64], F32, "gateL")
        idxC = k.sb(st, [128, 1, 64], I32, "idxC")
        gateC = k.sb(st, [128, 1, 64], F32, "gateC")
        P.dma("sp", DMA(idxL[:].rearrange("p s f -> p (s f)"), d["idxL"].rearrange("p s f -> p (s f)")), writes=[idxL])
        P.dma("sp", DMA(gateL[:].rearrange("p s f -> p (s f)"), d["gateL"].rearrange("p s f -> p (s f)")), writes=[gateL])
        if ctx_out:
            P.dma("sp", DMA(idxC[:].rearrange("p s f -> p (s f)"), d["idxC"].rearrange("p s f -> p (s f)")), writes=[idxC])
            P.dma("sp", DMA(gateC[:].rearrange("p s f -> p (s f)"), d["gateC"].rearrange("p s f -> p (s f)")), writes=[gateC])
        gt2 = [k.sb(st, [128, D], F32, "gt2") for _ in range(3)]
        for r in range(3):
            load_bcast(k, P, gt2[r], d["mod"][l, r:r + 1, 5 * D:6 * D])
        wgs = [k.sb(st, [128, 8, D], BF16, "wg") for _ in range(2)]
        wus = [k.sb(st, [128, 8, D], BF16, "wu") for _ in range(2)]
        wds = [k.sb(st, [128, 8, D], BF16, "wd") for _ in range(2)]
        xes = [k.sb(st, [128, D], BF16, "xe") for _ in range(4)]
        xeTs = [k.sb(st, [128, 8, 512], BF16, "xeT") for _ in range(2)]
        actT = k.sb(st, [128, 8, 512], BF16, "actT")
        sils = [k.sb(st, [128, 512], F32, "sil") for _ in range(2)]
        yvs = [k.sb(st, [128, D], F32, "yv") for _ in range(2)]
        tps = [k.ps(st, [128, 8, 128], BF16, "tpm") for _ in range(2)]
        pas = [k.ps(st, [128, 512], F32, "pa") for _ in range(2)]
        pus = [k.ps(st, [128, 512], F32, "pu") for _ in range(2)]
        pys = [k.ps(st, [128, 512], F32, "py") for _ in range(2)]
        G = {}
        iu = 0
        ixe = 0
        iy = 0
        ifc = 0
        for e in range(NE):
            wg, wu, wd = wgs[e % 2], wus[e % 2], wds[e % 2]
            for (w_, nm) in ((wg, "w_e_gate"), (wu, "w_e_up"), (wd, "w_e_down")):
                P.dma("pool", DMA(w_[:], d[nm][l, e].rearrange("(k p) n -> p k n", p=128)), writes=[w_])
            units = [("L", b) for b in range(NB)] + ([("C", b) for b in range(NB)] if ctx_out else [])
            for (tg, b) in units:
                lat = tg == "L"
                nti = 4 if lat else 1
                ntk = nti * 128
                col = 32 * b + e
                idx, gate = (idxL, gateL) if lat else (idxC, gateC)
                h2d = d["h2%s%d" % (tg, b)]
                xd = d["y"].rearrange("b t c -> (b t) c") if lat else d["ctxs"].rearrange("b t c -> (b t) c")
                xoff = b * (T if lat else SP) * D
                mrow = b if lat else 2
                xeT = xeTs[iu % 2]
                iu += 1
                for j in range(nti):
                    xe = xes[ixe % 4]
                    tp = tps[ixe % 2]
                    ixe += 1
                    P.dma("pool", (lambda o, i_, off: (lambda en: en.indirect_dma_start(
                        out=o, out_offset=None, in_=i_, in_offset=bass.IndirectOffsetOnAxis(ap=off, axis=0))))(
                        xe[:], h2d[:, :], idx[:, j, col:col + 1]), reads=[idx], writes=[xe])
                    for kk in range(8):
                        P.op("pe", TR(tp[:, kk, :], xe[:, kk * 128:(kk + 1) * 128], ident[:]), reads=[xe, ident], writes=[tp], sig=(kk == 7))
                    P.op("act", ACT(xeT[:, :, j * 128:(j + 1) * 128], tp[:], AF.Copy), reads=[tp], writes=[xeT])
                for fc in range(8):
                    pa, pu, sil = pas[ifc % 2], pus[ifc % 2], sils[ifc % 2]
                    ifc += 1
                    for kk in range(8):
                        P.op("pe", MM(pa[:, 0:ntk], wg[:, kk, fc * 128:(fc + 1) * 128], xeT[:, kk, 0:ntk], start=(kk == 0), stop=(kk == 7)),
                             reads=[wg, xeT], writes=[pa], sig=(kk == 7))
                    for kk in range(8):
                        P.op("pe", MM(pu[:, 0:ntk], wu[:, kk, fc * 128:(fc + 1) * 128], xeT[:, kk, 0:ntk], start=(kk == 0), stop=(kk == 7)),
                             reads=[wu, xeT], writes=[pu], sig=(kk == 7))
                    P.op("act", ACT(sil[:, 0:ntk], pa[:, 0:ntk], AF.Silu), reads=[pa], writes=[sil])
                    P.op("dve", TTo(actT[:, fc, 0:ntk], sil[:, 0:ntk], pu[:, 0:ntk], ALU.mult), reads=[sil, pu], writes=[actT])
                gkey = (tg, b)
                if gkey not in G:
                    G[gkey] = Buf()
                P.op("pool", lambda en: en.nop(), writes=[G[gkey]])
                for j in range(nti):
                    yv = yvs[iy % 2]
                    for dc in range(2):
                        py = pys[(2 * iy + dc) % 2]
                        for fc in range(8):
                            P.op("pe", MM(py[:], actT[:, fc, j * 128:(j + 1) * 128], wd[:, fc, dc * 512:(dc + 1) * 512], start=(fc == 0), stop=(fc == 7)),
                                 reads=[actT, wd], writes=[py], sig=(fc == 7))
                        P.op("act", ACT(yv[:, dc * 512:(dc + 1) * 512], py[:], AF.Copy, scale=gate[:, j, col:col + 1]), reads=[py, gate], writes=[yv])
                    iy += 1
                    P.op("dve", TTo(yv[:], yv[:], gt2[mrow][:], ALU.mult), reads=[yv, gt2[mrow]], writes=[yv])
                    P.dma("pool", (lambda o, i_, off, eo=xoff: (lambda en: en.indirect_dma_start(
                        out=o, out_offset=bass.IndirectOffsetOnAxis(ap=off, axis=0), in_=i_, in_offset=None, compute_op=ALU.add,
                        element_offset=eo)))(
                        xd, yv[:], idx[:, j, col:col + 1]), reads=[yv, idx, G[gkey]], writes=[])
        P.barrier()


def emit_layers(k, layers):
    phase_init(k)
    phase_ada(k, layers)
    for l in layers:
        ctx_out = l != DEPTH - 1
        if MULTI_BLOCK:
            k.P.mark()
        phase_proj(k, l)
        phase_mixA(k, l, ctx_out)
        for tag, L in HY_LS:
            if tag == "C" and not ctx_out:
                continue
            phase_hy_filter(k, l, tag, L)
            phase_hy_signal(k, l, tag, L)
        phase_mixC(k, l, ctx_out)
        phase_mixD(k, l, ctx_out)
        phase_out_router(k, l, ctx_out)
        phase_topk(k, l, "L", T, 512)
        if ctx_out:
            phase_topk(k, l, "C", S, 32)
        phase_moe(k, l, ctx_out)


def build_program(layers, ext_out=()):
    k = K(ext_out=ext_out)
    declare_io(k)
    emit_layers(k, layers)
    k.P.replay_all()
    k.stack.close()
    return k


FUSED = False
GROUPS = [[0, 1], [2, 3]]
MULTI_BLOCK = True
N_CORES = 8


def kernel(**inputs):
    inp = {n: np.asarray(v) for n, v in inputs.items()}
    maps = [core_inputs(inp, c) for c in range(N_CORES)]
    if FUSED:
        k = build_program(list(range(DEPTH)))
        res = run_bass_kernel_spmd(k.nc, maps, core_ids=list(range(N_CORES)))
        ys = [r["y"] for r in res.results]
    else:
        for grp in GROUPS:
            k = build_program(grp, ext_out=("ctxs",))
            res = run_bass_kernel_spmd(k.nc, maps, core_ids=list(range(N_CORES)))
            for c in range(N_CORES):
                maps[c]["x"] = np.ascontiguousarray(res.results[c]["y"])
                maps[c]["ctx"] = np.ascontiguousarray(res.results[c]["ctxs"][:, 0:S])
        ys = [m["x"] for m in maps]
    return np.concatenate(ys, axis=0).astype(np.float32)
```
